# Optimizing a Trainium2 kernel written in Bass

```python
import math
import jax, jax.numpy as jnp
from jax import lax
import numpy as np

D_MODEL = 1024
BATCH = 8
SEQ = 2048
DEPTH = 1

A_WIDTH = D_MODEL // 2
A_GROUPS = 8
A_GROUP_DIM = A_WIDTH // A_GROUPS
CHUNK = 128
B_WIDTH = D_MODEL // 2
HYENA_ORDER = 2
SHORT_CONV = 3
FILTER_EMB = 33
FILTER_HIDDEN = 64
N_DIRS = 2
DECAY_TARGET = 1e-2
FAST_DECAY_PCT = 0.3
SLOW_DECAY_PCT = 1.5
DECAY_SHIFT = 0.05
N_BRANCHES = 2
D_FF = 4 * D_MODEL
EPS = 1e-6
IN_COLS = 2 * A_WIDTH + (HYENA_ORDER + 1) * B_WIDTH + N_BRANCHES * D_MODEL

kernel_name = "hybrid_gmlp_hyena_encoder_block"


def rmsnorm(x, g):
    xf = x.astype(jnp.float32)
    r = xf * lax.rsqrt(jnp.mean(xf * xf, axis=-1, keepdims=True) + EPS)
    return (r * g.astype(jnp.float32)).astype(x.dtype)


def layernorm(x, g):
    xf = x.astype(jnp.float32)
    mu = jnp.mean(xf, axis=-1, keepdims=True)
    var = jnp.mean(jnp.square(xf - mu), axis=-1, keepdims=True)
    return ((xf - mu) * lax.rsqrt(var + EPS) * g.astype(jnp.float32)).astype(x.dtype)


def spatial_gating(z, v_gain, w_s, b_s):
    u, v = jnp.split(z, 2, axis=-1)
    v = layernorm(v, v_gain)
    bsz, L, _ = v.shape
    vc = v.reshape(bsz, L // CHUNK, CHUNK, A_GROUPS, A_GROUP_DIM)
    s = jnp.einsum('gts,bcsgd->bctgd', w_s, vc) + b_s.T[None, None, :, :, None]
    return u * s.reshape(bsz, L, A_WIDTH)


def short_conv(z, w, b):
    L = z.shape[1]
    half = SHORT_CONV // 2
    zp = jnp.pad(z, ((0, 0), (half, half), (0, 0)))
    return sum(zp[:, k:k + L] * w[k] for k in range(SHORT_CONV)) + b


def hyena_filters(L, w1, b1, f1, w2, b2, f2, w3):
    f32 = jnp.float32
    t = jnp.linspace(0.0, 1.0, L, dtype=f32)[:, None]
    bands = (FILTER_EMB - 1) // 2
    w = 2.0 * math.pi * jnp.arange(L, dtype=f32)[:, None] / L
    fr = jnp.linspace(1e-4, bands - 1, bands, dtype=f32)[None, :]
    feats = jnp.concatenate([t, jnp.cos(fr * w), -jnp.sin(fr * w)], axis=-1)
    h = jnp.sin(f1.astype(f32) * (feats @ w1.astype(f32) + b1.astype(f32)))
    h = jnp.sin(f2.astype(f32) * (h @ w2.astype(f32) + b2.astype(f32)))
    h = (h @ w3.astype(f32)).reshape(L, HYENA_ORDER, N_DIRS, B_WIDTH)
    max_decay = math.log(DECAY_TARGET) / FAST_DECAY_PCT
    min_decay = math.log(DECAY_TARGET) / SLOW_DECAY_PCT
    deltas = jnp.abs(jnp.linspace(min_decay, max_decay, B_WIDTH, dtype=f32))
    decay = jnp.exp(-t[:, :, None, None] * deltas)
    h = h * (decay + DECAY_SHIFT)
    return h * lax.rsqrt(jnp.sum(h * h, axis=(0, 2), keepdims=True) + EPS)


def bidir_fft_conv(z, h_fwd, h_bwd, skip):
    L, C = h_fwd.shape
    k = jnp.concatenate([h_fwd.at[0].add(h_bwd[0]),
                         jnp.zeros((1, C), jnp.float32),
                         h_bwd[:0:-1]], axis=0)
    kf = jnp.fft.rfft(k, axis=0)
    zf32 = z.astype(jnp.float32)
    zf = jnp.fft.rfft(zf32, n=2 * L, axis=1)
    y = jnp.fft.irfft(zf * kf[None], n=2 * L, axis=1)[:, :L]
    return (y + zf32 * skip.astype(jnp.float32)).astype(z.dtype)


def hyena_mixer(p, conv_w, conv_b, w1, b1, f1, w2, b2, f2, w3, skip):
    L = p.shape[1]
    pc = short_conv(p, conv_w, conv_b)
    x1, x2, v = jnp.split(pc, HYENA_ORDER + 1, axis=-1)
    filt = hyena_filters(L, w1, b1, f1, w2, b2, f2, w3)
    z = v
    z = x1 * bidir_fft_conv(z, filt[:, 0, 0], filt[:, 0, 1], skip[0])
    z = x2 * bidir_fft_conv(z, filt[:, 1, 0], filt[:, 1, 1], skip[1])
    return z


def setup_inputs(seed: int = 0) -> dict:
    key = jax.random.key(seed)
    ks = jax.random.split(key, 32)
    nrm = lambda k, s, sc: jax.random.normal(k, s, jnp.float32) * sc
    gain = lambda k, s: 1.0 + 0.02 * jax.random.normal(k, s, jnp.float32)
    Dp = DEPTH
    return {
        "x": nrm(ks[0], (BATCH, SEQ, D_MODEL), 1.0),
        "g_pre_mix": gain(ks[1], (Dp, D_MODEL)),
        "w_in": nrm(ks[2], (Dp, D_MODEL, IN_COLS), D_MODEL ** -0.5),
        "a_v_gain": gain(ks[3], (Dp, A_WIDTH)),
        "a_w_s": nrm(ks[4], (Dp, A_GROUPS, CHUNK, CHUNK), CHUNK ** -0.5),
        "a_b_s": gain(ks[5], (Dp, A_GROUPS, CHUNK)),
        "w_out_a": nrm(ks[6], (Dp, A_WIDTH, D_MODEL), A_WIDTH ** -0.5),
        "b_conv_w": nrm(ks[7], (Dp, SHORT_CONV, (HYENA_ORDER + 1) * B_WIDTH), SHORT_CONV ** -0.5),
        "b_conv_b": nrm(ks[8], (Dp, (HYENA_ORDER + 1) * B_WIDTH), 0.02),
        "b_filt_w1": nrm(ks[9], (Dp, FILTER_EMB, FILTER_HIDDEN), FILTER_EMB ** -0.5),
        "b_filt_b1": nrm(ks[10], (Dp, FILTER_HIDDEN), 0.02),
        "b_filt_f1": gain(ks[11], (Dp, FILTER_HIDDEN)),
        "b_filt_w2": nrm(ks[12], (Dp, FILTER_HIDDEN, FILTER_HIDDEN), FILTER_HIDDEN ** -0.5),
        "b_filt_b2": nrm(ks[13], (Dp, FILTER_HIDDEN), 0.02),
        "b_filt_f2": gain(ks[14], (Dp, FILTER_HIDDEN)),
        "b_filt_w3": nrm(ks[15], (Dp, FILTER_HIDDEN, HYENA_ORDER * N_DIRS * B_WIDTH), FILTER_HIDDEN ** -0.5),
        "b_skip": nrm(ks[16], (Dp, HYENA_ORDER, B_WIDTH), 1.0),
        "w_out_b": nrm(ks[17], (Dp, B_WIDTH, D_MODEL), B_WIDTH ** -0.5),
        "w_o": nrm(ks[18], (Dp, D_MODEL, D_MODEL), D_MODEL ** -0.5),
        "g_post_mix": gain(ks[19], (Dp, D_MODEL)),
        "g_pre_ffn": gain(ks[20], (Dp, D_MODEL)),
        "w_ff1": nrm(ks[21], (Dp, D_MODEL, D_FF), D_MODEL ** -0.5),
        "w_ff2": nrm(ks[22], (Dp, D_FF, D_MODEL), D_FF ** -0.5),
        "g_post_ffn": gain(ks[23], (Dp, D_MODEL)),
    }


def reference(x, g_pre_mix, w_in, a_v_gain, a_w_s, a_b_s, w_out_a, b_conv_w, b_conv_b,
              b_filt_w1, b_filt_b1, b_filt_f1, b_filt_w2, b_filt_b2, b_filt_f2, b_filt_w3,
              b_skip, w_out_b, w_o, g_post_mix, g_pre_ffn, w_ff1, w_ff2, g_post_ffn):
    h = x
    split_a = 2 * A_WIDTH
    split_b = split_a + (HYENA_ORDER + 1) * B_WIDTH
    for i in range(DEPTH):
        xn = rmsnorm(h, g_pre_mix[i])
        p = jnp.einsum('bld,dc->blc', xn, w_in[i])
        p_a, p_b, p_g = p[..., :split_a], p[..., split_a:split_b], p[..., split_b:]
        y_a = spatial_gating(jax.nn.gelu(p_a), a_v_gain[i], a_w_s[i], a_b_s[i])
        y_a = jnp.einsum('blc,cd->bld', y_a, w_out_a[i])
        y_b = hyena_mixer(p_b, b_conv_w[i], b_conv_b[i], b_filt_w1[i], b_filt_b1[i],
                          b_filt_f1[i], b_filt_w2[i], b_filt_b2[i], b_filt_f2[i],
                          b_filt_w3[i], b_skip[i])
        y_b = jnp.einsum('blc,cd->bld', y_b, w_out_b[i])
        g_a, g_b = jnp.split(jax.nn.sigmoid(p_g), N_BRANCHES, axis=-1)
        m = jnp.einsum('bld,de->ble', g_a * y_a + g_b * y_b, w_o[i])
        h = h + rmsnorm(m, g_post_mix[i])
        hn = rmsnorm(h, g_pre_ffn[i])
        f = jnp.square(jax.nn.relu(jnp.einsum('bld,df->blf', hn, w_ff1[i])))
        f = jnp.einsum('blf,fd->bld', f, w_ff2[i])
        h = h + rmsnorm(f, g_post_ffn[i])
    return h
```

```python
import math
from contextlib import ExitStack

import numpy as np
import ml_dtypes

import concourse.bass as bass
import concourse.mybir as mybir
from concourse.bass_utils import run_bass_kernel_spmd

F32 = mybir.dt.float32
BF16 = mybir.dt.bfloat16
AF = mybir.ActivationFunctionType
ALU = mybir.AluOpType

L = 2048
D = 1024
NT = 16
EPS = 1e-6
KB = 1024
STOP_AFTER = None


class Buf:
    def __init__(self, name, off, size):
        self.name, self.off, self.size = name, off, size
        self.toks = {}
        self.inherit = {}
        self.active = False


class Prog:
    ENGS = ("sync", "scalar", "gpsimd", "vector", "tensor")

    def __init__(self, nc, es, n_dma_sems=32):
        self.nc = nc
        self.ops = {e: [] for e in self.ENGS}
        self.sem = {e: es.enter_context(nc.semaphore("s_" + e)) for e in self.ENGS}
        self.cnt = {e: 0 for e in self.ENGS}
        self.dsem = [es.enter_context(nc.semaphore("d%d" % i)) for i in range(n_dma_sems)]
        self.dcnt = [0] * n_dma_sems
        self.dnext = 0
        self.waited = {e: {} for e in self.ENGS}
        self.last_w = {}
        self.readers = {}
        self.bufs = {}
        self.active = []
        self.nwaits = 0
        self.nops = 0

    def buf(self, name, off, size):
        assert name not in self.bufs, name
        self.bufs[name] = Buf(name, off, size)

    def _wait(self, eng, tok):
        if tok is None:
            return
        semkey, val = tok
        if semkey == eng and val > self.cnt[eng]:
            return
        if self.waited[eng].get(semkey, 0) >= val:
            return
        self.waited[eng][semkey] = val
        sem = self.sem[semkey] if isinstance(semkey, str) else self.dsem[semkey]
        self.nwaits += 1
        self.ops[eng].append(lambda e, sem=sem, val=val: e.wait_ge(sem, val))

    def _parent(self, k):
        return self.bufs.get(k[0] if isinstance(k, tuple) else k)

    def _touch(self, eng, k):
        b = self._parent(k)
        if b is None:
            return
        if not b.active:
            for o in list(self.active):
                if o is not b and o.off < b.off + b.size and b.off < o.off + o.size:
                    for src in (o.toks, o.inherit):
                        for sk, v in src.items():
                            if b.inherit.get(sk, 0) < v:
                                b.inherit[sk] = v
                    o.active = False
                    self.active.remove(o)
            b.active = True
            b.toks = {}
            self.active.append(b)
        for sk, v in b.inherit.items():
            self._wait(eng, (sk, v))

    def _deps(self, eng, reads, writes, extra):
        for k in reads:
            self._touch(eng, k)
            self._wait(eng, self.last_w.get(k))
        for k in writes:
            self._touch(eng, k)
            self._wait(eng, self.last_w.get(k))
            for t in self.readers.get(k, ()):
                self._wait(eng, t)
        for t in extra:
            self._wait(eng, t)

    def _record(self, tok, reads, writes):
        for k in reads:
            self.readers.setdefault(k, []).append(tok)
            b = self._parent(k)
            if b is not None and b.toks.get(tok[0], 0) < tok[1]:
                b.toks[tok[0]] = tok[1]
        for k in writes:
            self.last_w[k] = tok
            self.readers[k] = []
            b = self._parent(k)
            if b is not None and b.toks.get(tok[0], 0) < tok[1]:
                b.toks[tok[0]] = tok[1]

    def op(self, eng, fn, reads=(), writes=(), extra=(), signal=True):
        self._deps(eng, reads, writes, extra)
        self.nops += 1
        if signal:
            self.cnt[eng] += 1
            tok = (eng, self.cnt[eng])
            sem = self.sem[eng]
            self.ops[eng].append(lambda e, fn=fn, sem=sem: fn(e).then_inc(sem, 1))
        else:
            tok = (eng, self.cnt[eng] + 1)
            self.ops[eng].append(lambda e, fn=fn: fn(e))
        self._record(tok, reads, writes)
        return tok

    def dma(self, eng, fn, reads=(), writes=(), extra=()):
        self._deps(eng, reads, writes, extra)
        half = len(self.dsem) // 2
        base = 0 if eng == "gpsimd" else half
        self.dnext_q = getattr(self, "dnext_q", {})
        j = self.dnext_q.get(eng, 0)
        self.dnext_q[eng] = (j + 1) % half
        i = base + j
        if self.dcnt[i]:
            self._wait(eng, (i, self.dcnt[i]))
        self.dcnt[i] += 16
        tok = (i, self.dcnt[i])
        sem = self.dsem[i]
        self.ops[eng].append(lambda e, fn=fn, sem=sem: fn(e).then_inc(sem, 16))
        self._record(tok, reads, writes)
        return tok

    def wait(self, eng, tok):
        self._wait(eng, tok)

    def emit(self):
        with self.nc.Block() as block:
            for name in self.ENGS:
                ops = self.ops[name]
                if not ops:
                    continue

                def body(e, ops=ops):
                    for f in ops:
                        f(e)
                getattr(block, name)(body)


def lockstep(gens):
    alive = list(gens)
    while alive:
        for g in list(alive):
            try:
                next(g)
            except StopIteration:
                alive.remove(g)


def _prod(s):
    r = 1
    for v in s:
        r *= v
    return r


NCOLP = 76
C_W0, C_W1, C_W2, C_CB, C_F1, C_B1, C_F2, C_B2, C_BST, C_TNEG = 0, 12, 24, 36, 48, 49, 50, 51, 52, 60
R_GPM, R_GAIN, R_SKIP, R_DELTA, R_GPOSTMIX, R_GPREFFN, R_GPOSTFFN = 0, 1024, 1536, 2560, 3072, 4096, 5120
NROWP = 6144


def build_program(stop_after=None):
    nc = bass.Bass("TRN2", target_bir_lowering=False)
    dt_in = lambda n, s, d=F32: nc.dram_tensor(n, s, d, kind="ExternalInput").ap()
    x = dt_in("x", [L, D])
    w_in = dt_in("w_in", [D, 4608])
    w_out_a = dt_in("w_out_a", [512, D])
    w_out_b = dt_in("w_out_b", [512, D])
    w_o = dt_in("w_o", [D, D])
    w_ff1 = dt_in("w_ff1", [D, 4096])
    w_ff2 = dt_in("w_ff2", [4096, D])
    wsT_d = dt_in("wsT", [128, 8, 128])
    colp_d = dt_in("colp", [128, NCOLP])
    rowp_d = dt_in("rowp", [1, NROWP])
    featsT_d = dt_in("featsT", [33, L])
    fw1_d = dt_in("fw1", [33, 64])
    fw2_d = dt_in("fw2", [64, 64])
    fw3_d = dt_in("fw3", [64, 2048])
    ident_d = dt_in("ident", [128, 128], BF16)
    TF_d = dt_in("TF", [8, 128, 4, 8, 128], BF16)
    TI_d = dt_in("TI", [2, 8, 128, 2, 8, 128], BF16)
    out = nc.dram_tensor("out", [L, D], F32, kind="ExternalOutput").ap()
    wff1s_d = nc.dram_tensor("wff1_bf16_scratch", [16, 128, 2048], BF16, kind="Internal").ap()
    wff2s_d = nc.dram_tensor("wff2_bf16_scratch", [8, 128, 4096], BF16, kind="Internal").ap()
    hsd2_d = nc.dram_tensor("hsd2_scratch", [2, NT, 128, 512], BF16, kind="Internal").ap()
    dbg = None
    if stop_after is not None:
        dbg = nc.dram_tensor("dbg", [128, 20 * 1024], F32, kind="ExternalOutput").ap()

    with ExitStack() as es:
        P = Prog(nc, es)
        ARENA = 204 * KB
        AR = es.enter_context(nc.sbuf_tensor("arena", [128, ARENA // 2], BF16))
        PS = es.enter_context(nc.psum_tensor("ps", [128, 8, 512], F32))
        ident = es.enter_context(nc.sbuf_tensor("ident_sb", [128, 128], BF16))
        ones = es.enter_context(nc.sbuf_tensor("ones", [128, 128], BF16))
        colp = es.enter_context(nc.sbuf_tensor("colp_sb", [128, NCOLP], F32))
        stats = es.enter_context(nc.sbuf_tensor("stats", [128, 136], F32))
        junk3_t = es.enter_context(nc.sbuf_tensor("junk3", [128, D], BF16))
        epsT = es.enter_context(nc.sbuf_tensor("epsT", [128, 1], F32))

        def view(name, off, shape, dtype):
            esz = 4 if dtype == F32 else 2
            n = _prod(shape[1:])
            assert off % 4 == 0 and off + n * esz <= ARENA, (name, off, n * esz)
            ap = AR[0:shape[0], off // 2: off // 2 + n * esz // 2]
            if dtype != BF16:
                ap = ap.bitcast(dtype)
            if len(shape) == 3:
                ap = ap.rearrange("p (a b) -> p a b", a=shape[1])
            elif len(shape) == 4:
                ap = ap.rearrange("p (a b c) -> p a b c", a=shape[1], b=shape[2])
            P.buf(name, off, n * esz)
            return ap

        def psb(b):
            return PS[:, b, :]

        def pspair(s):
            return PS[:, 2 * s:2 * s + 2, :].rearrange("p a b -> p (a b)")

        def psT(b, k):
            return PS[:, b, :].bitcast(BF16)[:, 0:k * 128].rearrange("p (k t) -> p k t", k=k)

        def pk(s):
            return [("ps", 2 * s), ("ps", 2 * s + 1)]

        def mm(o, lhsT, rhs, start, stop, reads, writes, signal=None):
            if signal is None:
                signal = stop
            return P.op("tensor", lambda e: e.matmul(o, lhsT, rhs, start=start, stop=stop),
                        reads=reads, writes=writes, signal=signal)

        def tr(o, in_, reads, writes, signal):
            return P.op("tensor", lambda e: e.transpose(o, in_, ident[:]),
                        reads=list(reads) + ["ident"], writes=writes, signal=signal)

        def act(o, in_, func, reads, writes, scale=1.0, bias=0.0, accum=None):
            if accum is None:
                return P.op("scalar", lambda e: e.activation(out=o, in_=in_, func=func, bias=bias, scale=scale),
                            reads=reads, writes=writes)
            return P.op("scalar", lambda e: e.activation(out=o, in_=in_, func=func, bias=bias, scale=scale,
                                                         accum_out=accum), reads=reads, writes=writes)

        def ts(eng, o, in0, s1, s2, op0, op1, reads, writes):
            if s2 is None:
                return P.op(eng, lambda e: e.tensor_scalar(out=o, in0=in0, scalar1=s1, scalar2=None, op0=op0),
                            reads=reads, writes=writes)
            return P.op(eng, lambda e: e.tensor_scalar(out=o, in0=in0, scalar1=s1, scalar2=s2, op0=op0, op1=op1),
                        reads=reads, writes=writes)

        def stt(eng, o, in0, sc, in1, op0, op1, reads, writes):
            return P.op(eng, lambda e: e.scalar_tensor_tensor(out=o, in0=in0, scalar=sc, in1=in1, op0=op0, op1=op1),
                        reads=reads, writes=writes)

        def tt(eng, o, in0, in1, op, reads, writes):
            return P.op(eng, lambda e: e.tensor_tensor(out=o, in0=in0, in1=in1, op=op), reads=reads, writes=writes)

        def cp(eng, o, in_, reads, writes):
            if eng == "scalar":
                return P.op(eng, lambda e: e.copy(out=o, in_=in_), reads=reads, writes=writes)
            return P.op(eng, lambda e: e.tensor_copy(out=o, in_=in_), reads=reads, writes=writes)

        def rsqrt_col(col, scale, key):
            act(col, col, AF.Sqrt, reads=[key, "epsT"], writes=[key], scale=scale, bias=epsT[:, 0:1])
            P.op("vector", lambda e: e.reciprocal(out=col, in_=col), reads=[key], writes=[key])

        O_XNT = 0
        O_YAT = 32 * KB
        O_ZOT = 48 * KB
        O_A = 64 * KB
        O_ZTOK = 80 * KB
        O_B = 96 * KB
        O_Y = 112 * KB
        O_TAB = 144 * KB
        O_TMP = 160 * KB
        O_ROW = 184 * KB
        O_END = 196 * KB

        xnT = view("xnT", O_XNT, [128, 8, L], BF16)

        P.dma("sync", lambda e: e.dma_start(out=ident[:], in_=ident_d), writes=["ident"])
        P.dma("sync", lambda e: e.dma_start(out=colp[:], in_=colp_d), writes=["colp"])
        P.op("vector", lambda e: e.memset(stats[:], 0.0), writes=["stats"])
        P.op("vector", lambda e: e.memset(epsT[:], EPS), writes=["epsT"])
        P.op("vector", lambda e: e.memset(ones[:], 1.0), writes=["ones"])
        rowP1 = view("rowP1", O_ROW, [128, 3072], F32)
        P.dma("sync", lambda e: e.dma_start(out=rowP1, in_=rowp_d[:, 0:3072].partition_broadcast(128)),
              writes=["rowP1"])
        gpmRow = rowP1[:, 0:1024]
        gainRow = rowP1[:, 1024:1536]
        skipRow = rowP1[:, 1536:2560]
        deltaRow = rowP1[:, 2560:3072]

        w_ff1_v0 = w_ff1.rearrange("(k p) c -> p k c", p=128)

        w_ff2_v0 = w_ff2.rearrange("(k p) c -> p k c", p=128)

        def convert_wff2(q, after=None):
            P.dma("gpsimd", lambda e, q=q: e.dma_start(
                out=wff2s_d[q].rearrange("p (k c) -> p k c", k=4), in_=w_ff2_v0[:, q * 4:(q + 1) * 4, :]),
                writes=[("wff2s", q)], extra=([after] if after is not None else []))

        def convert_wff1(fps, after=None):
            for fp in fps:
                P.dma("gpsimd", lambda e, fp=fp: e.dma_start(
                    out=wff1s_d[fp].rearrange("p (k c) -> p k c", k=8), in_=w_ff1_v0[:, :, fp * 256:(fp + 1) * 256]),
                    writes=[("wff1s", fp)], extra=([after] if after is not None else []))

        S_SS1 = 0
        S_MA_SUM, S_MA_NM, S_MA_VAR = 32, 48, 64
        S_M, S_H, S_F = 80, 96, 112
        S_FB1, S_FB2 = 128, 129
        stc = lambda c: stats[:, c:c + 1]
        stk = lambda c: ("st", c)
        for c in range(136):
            P.last_w[("st", c)] = P.last_w["stats"]

        O_S1 = 170 * KB
        xt_v = [view("xt%d" % b, O_ZTOK + b * 4 * KB, [128, D], F32) for b in range(4)]
        xs_v = [view("xs%d" % b, O_S1 + b * 2 * KB, [128, D], BF16) for b in range(2)]
        junk1 = junk3_t[:]

        def s1_a1(i):
            b = i % 4
            P.dma("sync", lambda e, i=i, b=b: e.dma_start(out=xt_v[b], in_=x[i * 128:(i + 1) * 128, :]),
                  writes=["xt%d" % b])
            col = stc(S_SS1 + i)
            act(junk1, xt_v[b], AF.Square, reads=["xt%d" % b], writes=[stk(S_SS1 + i), "junk3"], accum=col)
            rsqrt_col(col, 1.0 / D, stk(S_SS1 + i))

        def s1_a2(i):
            b = i % 4
            stt("vector", xs_v[i % 2], xt_v[b], stc(S_SS1 + i), gpmRow, ALU.mult, ALU.mult,
                reads=["xt%d" % b, stk(S_SS1 + i), "rowP1"], writes=["xs%d" % (i % 2)])

        def s1_b(i):
            b = i % 2
            for k in range(8):
                tr(psT(6, 8)[:, k, :], xs_v[b][:, k * 128:(k + 1) * 128], reads=["xs%d" % b], writes=[("ps", 6)],
                   signal=(k == 7))
            cp("scalar", xnT[:, :, i * 128:(i + 1) * 128], psT(6, 8), reads=[("ps", 6)], writes=[("xnT", i)])

        def s1_gen():
            s1_a1(0)
            s1_a1(1)
            s1_a2(0)
            for i in range(NT):
                if i + 2 < NT:
                    s1_a1(i + 2)
                if i + 1 < NT:
                    s1_a2(i + 1)
                s1_b(i)
                yield

        xnT_keys = lambda t0, t1: [("xnT", t) for t in range(t0, t1)]
        def maybe_stop(stage, items):
            if stop_after != stage:
                return False
            off = 0
            toks = []
            for (ap, keys) in items:
                n = ap.shape[1]
                for c0 in range(0, n, 2048):
                    c1 = min(n, c0 + 2048)
                    toks.append(P.dma("gpsimd", lambda e, ap=ap, off=off, c0=c0, c1=c1: e.dma_start(
                        out=dbg[0:ap.shape[0], off + c0:off + c1], in_=ap[:, c0:c1]), reads=keys))
                off += n
            for t in toks:
                P.wait("gpsimd", t)
            P.emit()
            return True

        flat = lambda ap: ap.rearrange("p a b -> p (a b)")

        w_in_v = w_in.rearrange("(k p) c -> p k c", p=128)

        ring = {"i": 0}

        def next_pair():
            s = ring["i"] % 3
            ring["i"] += 1
            return s

        wb_pref = {}

        def hy_prefetch(tag, col0, wb_off):
            wb = view("wb_" + tag, wb_off, [128, 8, 512], BF16)
            for hf in range(2):
                P.dma("gpsimd", lambda e, hf=hf: e.dma_start(out=wb[:, :, hf * 256:(hf + 1) * 256],
                                                             in_=w_in_v[:, :, col0 + hf * 256:col0 + (hf + 1) * 256]),
                      writes=[("wb_" + tag, hf)])
            wb_pref[tag] = wb
            return wb

        def hy_group(tag, col0, dest, destname, tmp_off, wb_off=None, after_wb=None, ob_off=None):
            if wb_off is None:
                wb_off = tmp_off
                tmp_off = tmp_off + 8 * KB
            if ob_off is None:
                ob_off = tmp_off + 12 * KB
            if tag in wb_pref:
                wb = wb_pref[tag]
            else:
                wb = hy_prefetch(tag, col0, wb_off)
            pcv = [view("pc_%s%d" % (tag, b), tmp_off + b * 4 * KB, [128, 1024], F32) for b in range(3)]
            obv = [view("ob_%s%d" % (tag, b), ob_off + b * 2 * KB, [128, 1024], BF16) for b in range(3)]
            if after_wb is not None:
                after_wb()
            ekey = ("ps", 7)

            def edges():
                for cb in range(4):
                    for k in range(8):
                        mm(PS[:, 7, 2 * cb:2 * cb + 2], wb[:, k, cb * 128:(cb + 1) * 128], xnT[:, k, 1023:1025],
                           k == 0, k == 7, reads=[("wb_" + tag, cb // 2), ("xnT", 7), ("xnT", 8)], writes=[ekey],
                           signal=(k == 7 and cb == 3))

            pairs_of = {}

            def main_mm(unit):
                half, cb = unit // 4, unit % 4
                wkey = ("wb_" + tag, cb // 2)
                s = next_pair()
                pairs_of[unit] = s
                for tg2 in range(2):
                    t0 = half * 1024 + tg2 * 512
                    for k in range(8):
                        mm(PS[:, 2 * s + tg2, :], wb[:, k, cb * 128:(cb + 1) * 128], xnT[:, k, t0:t0 + 512],
                           k == 0, k == 7, reads=[wkey] + xnT_keys(t0 // 128, t0 // 128 + 4),
                           writes=[("ps", 2 * s + tg2)])

            def main_ew(unit):
                half, cb = unit // 4, unit % 4
                gcb = (col0 - 1024) // 128 + cb
                w0c = colp[:, C_W0 + gcb:C_W0 + gcb + 1]
                w1c = colp[:, C_W1 + gcb:C_W1 + gcb + 1]
                w2c = colp[:, C_W2 + gcb:C_W2 + gcb + 1]
                bc = colp[:, C_CB + gcb:C_CB + gcb + 1]
                edge = PS[:, 7, 2 * cb:2 * cb + 2]
                s = pairs_of[unit]
                pkey = pk(s)
                p = pspair(s)
                b = unit % 3
                pc, ob = pcv[b], obv[b]
                pck, obk = "pc_%s%d" % (tag, b), "ob_%s%d" % (tag, b)
                act(pc, p, AF.Identity, reads=pkey + ["colp"], writes=[pck], scale=w1c, bias=bc)
                if half == 1:
                    stt("vector", pc[:, 0:1], edge[:, 0:1], w0c, pc[:, 0:1], ALU.mult, ALU.add,
                        reads=[ekey, pck, "colp"], writes=[pck])
                stt("vector", pc[:, 1:1024], p[:, 0:1023], w0c, pc[:, 1:1024], ALU.mult, ALU.add,
                    reads=pkey + [pck, "colp"], writes=[pck])
                stt("vector", ob[:, 0:1023], p[:, 1:1024], w2c, pc[:, 0:1023], ALU.mult, ALU.add,
                    reads=pkey + [pck, "colp"], writes=[obk])
                if half == 0:
                    stt("vector", ob[:, 1023:1024], edge[:, 1:2], w2c, pc[:, 1023:1024], ALU.mult, ALU.add,
                        reads=[ekey, pck, "colp"], writes=[obk])
                else:
                    cp("vector", ob[:, 1023:1024], pc[:, 1023:1024], reads=[pck], writes=[obk])

            def trans(unit):
                half, cb = unit // 4, unit % 4
                b = unit % 3
                ob = obv[b]
                obk = "ob_%s%d" % (tag, b)
                for r in range(2):
                    for jj in range(4):
                        st_ = 256 * jj + r
                        tr(psT(6, 8)[:, r * 4 + jj, :], ob[:, st_:st_ + 255:2], reads=[obk], writes=[("ps", 6)],
                           signal=(r == 1 and jj == 3))
                cp("scalar", dest[:, :, 4 * half:4 * half + 4, cb * 128:(cb + 1) * 128],
                   psT(6, 8).rearrange("p (r j) c -> p r j c", r=2),
                   reads=[("ps", 6)], writes=[(destname, r * 8 + 4 * half + jj) for r in range(2) for jj in range(4)])

            for unit in range(9):
                if unit < 8:
                    main_mm(unit)
                    if unit == 0:
                        edges()
                    main_ew(unit)
                if unit >= 1:
                    trans(unit - 1)

        normRow = view("normRow", O_END, [128, 2, 512], F32)

        def sin_reduced(dst, a, q, e2, keya, keyd, kt):
            act(q, a, AF.Sin, reads=[keya], writes=[kt + "q"], scale=0.25)
            act(e2, a, AF.Sin, reads=[keya], writes=[kt + "e"], scale=0.125)
            tt("vector", e2, e2, e2, ALU.mult, reads=[kt + "e"], writes=[kt + "e"])
            ts("vector", e2, e2, -2.0, 1.0, ALU.mult, ALU.add, reads=[kt + "e"], writes=[kt + "e"])
            tt("vector", e2, e2, q, ALU.mult, reads=[kt + "e", kt + "q"], writes=[kt + "e"])
            tt("vector", q, q, q, ALU.mult, reads=[kt + "q"], writes=[kt + "q"])
            ts("vector", q, q, -2.0, 1.0, ALU.mult, ALU.add, reads=[kt + "q"], writes=[kt + "q"])
            stt("vector", dst, e2, 4.0, q, ALU.mult, ALU.mult, reads=[kt + "e", kt + "q"], writes=[keyd])

        def layer_chain(ch, t, featsT, fw1, fw2, h2T, tmpsets, fb1, fb2):
            a_t, s1_t, q_t, e_t = tmpsets[t]
            ka, ks1, kq = "fa%d" % t, "fs%d" % t, "fsr%d" % t
            bank = 2 * t
            mm(PS[0:64, bank, :], fw1, featsT[:, ch * 512:(ch + 1) * 512], True, True,
               reads=["fw1", "featsT"], writes=[("ps", bank)])
            yield
            ts("vector", a_t, PS[0:64, bank, :], colp[0:64, C_F1:C_F1 + 1], fb1, ALU.mult, ALU.add,
               reads=[("ps", bank), "colp", stk(S_FB1)], writes=[ka])
            yield
            for _ in sin_reduced_g(s1_t, a_t, q_t, e_t, ka, ks1, kq):
                yield
            mm(PS[0:64, bank + 1, :], fw2, s1_t, True, True, reads=["fw2", ks1], writes=[("ps", bank + 1)])
            yield
            ts("vector", a_t, PS[0:64, bank + 1, :], colp[0:64, C_F2:C_F2 + 1], fb2, ALU.mult, ALU.add,
               reads=[("ps", bank + 1), "colp", stk(S_FB2)], writes=[ka])
            yield
            for _ in sin_reduced_g(h2T[:, ch * 512:(ch + 1) * 512], a_t, q_t, e_t, ka, ("h2T", ch), kq):
                yield

        def sin_reduced_g(dst, a, q, e2, keya, keyd, kt):
            act(q, a, AF.Sin, reads=[keya], writes=[kt + "q"], scale=0.25)
            act(e2, a, AF.Sin, reads=[keya], writes=[kt + "e"], scale=0.125)
            yield
            tt("vector", e2, e2, e2, ALU.mult, reads=[kt + "e"], writes=[kt + "e"])
            ts("vector", e2, e2, -2.0, 1.0, ALU.mult, ALU.add, reads=[kt + "e"], writes=[kt + "e"])
            tt("vector", e2, e2, q, ALU.mult, reads=[kt + "e", kt + "q"], writes=[kt + "e"])
            yield
            tt("vector", q, q, q, ALU.mult, reads=[kt + "q"], writes=[kt + "q"])
            ts("vector", q, q, -2.0, 1.0, ALU.mult, ALU.add, reads=[kt + "q"], writes=[kt + "q"])
            stt("vector", dst, e2, 4.0, q, ALU.mult, ALU.mult, reads=[kt + "e", kt + "q"], writes=[keyd])
            yield

        def filter_gen_all(hs1, hd1):
            featsT = view("featsT", 32 * KB, [33, L], F32)
            h2T = view("h2T", 40 * KB, [64, L], F32)
            fw3 = view("fw3", 48 * KB, [64, 2048], F32)
            w3sd = view("w3sd", 56 * KB, [64, 2048], F32)
            fw1 = view("fw1", O_END + 4 * KB, [33, 64], F32)
            fw2 = view("fw2", O_END + 4 * KB + 256, [64, 64], F32)
            tmpsets = []
            for t in range(2):
                base = O_Y + t * 8 * KB
                tmpsets.append((view("fa%d" % t, base, [64, 512], F32), view("fs%d" % t, base + 2 * KB, [64, 512], F32),
                                view("fsr%dq" % t, base + 4 * KB, [64, 512], F32),
                                view("fsr%de" % t, base + 6 * KB, [64, 512], F32)))
            Ev = [view("fE%d" % b, O_Y + 16 * KB + b * 2 * KB, [128, 512], F32) for b in range(2)]
            sqv = [view("fsq%d" % b, O_Y + 20 * KB + b * 4 * KB, [128, 4, 512], BF16) for b in range(2)]
            stg = [view("fstg%d" % b, O_Y + 28 * KB + b * 2 * KB, [128, 2, 512], BF16) for b in range(2)]
            P.dma("sync", lambda e: e.dma_start(out=featsT, in_=featsT_d), writes=["featsT"])
            P.dma("sync", lambda e: e.dma_start(out=fw3, in_=fw3_d), writes=["fw3"])
            P.dma("sync", lambda e: e.dma_start(out=fw1, in_=fw1_d), writes=["fw1"])
            P.dma("sync", lambda e: e.dma_start(out=fw2, in_=fw2_d), writes=["fw2"])
            fb1 = stats[0:64, S_FB1:S_FB1 + 1]
            fb2 = stats[0:64, S_FB2:S_FB2 + 1]
            tt("vector", fb1, colp[0:64, C_F1:C_F1 + 1], colp[0:64, C_B1:C_B1 + 1], ALU.mult,
               reads=["colp"], writes=[stk(S_FB1)])
            tt("vector", fb2, colp[0:64, C_F2:C_F2 + 1], colp[0:64, C_B2:C_B2 + 1], ALU.mult,
               reads=["colp"], writes=[stk(S_FB2)])
            for o in range(2):
                f_ = fw3[:, o * 1024:o * 1024 + 512]
                b_ = fw3[:, o * 1024 + 512:o * 1024 + 1024]
                tt("vector", w3sd[:, (2 * o) * 512:(2 * o + 1) * 512], f_, b_, ALU.add, reads=["fw3"],
                   writes=[("w3sd", 2 * o)])
                tt("vector", w3sd[:, (2 * o + 1) * 512:(2 * o + 2) * 512], f_, b_, ALU.subtract, reads=["fw3"],
                   writes=[("w3sd", 2 * o + 1)])
            yield
            for grp in ((0, 1), (2, 3)):
                gens = [layer_chain(ch, ch % 2, featsT, fw1, fw2, h2T, tmpsets, fb1, fb2) for ch in grp]
                alive = list(gens)
                while alive:
                    for g in list(alive):
                        try:
                            next(g)
                        except StopIteration:
                            alive.remove(g)
                    yield
            dsts = [(hs1, "hs1"), (hd1, "hd1")]
            for nt in range(NT):
                b = nt % 2
                r_, j_ = nt // 8, nt % 8
                st_ = 256 * j_ + r_
                for j in range(4):
                    mm(PS[:, j, :], h2T[:, st_:st_ + 255:2], w3sd[:, j * 512:(j + 1) * 512], True, True,
                       reads=[("h2T", j_ // 2), ("w3sd", j)], writes=[("ps", j)])
                Ek = "fE%d" % b
                act(Ev[b], deltaRow, AF.Exp, reads=["rowP1", "colp"], writes=[Ek],
                    scale=colp[:, C_TNEG + nt:C_TNEG + nt + 1])
                outs = [hs1[:, r_, j_, :], hd1[:, r_, j_, :], stg[b][:, 0, :], stg[b][:, 1, :]]
                okeys = [("hs1", nt), ("hd1", nt), ("fstg%d" % b, 0), ("fstg%d" % b, 1)]
                for j in range(4):
                    stt("vector", outs[j], Ev[b], 0.05, PS[:, j, :], ALU.add, ALU.mult,
                        reads=[Ek, ("ps", j)], writes=[okeys[j]])
                for j in range(4):
                    act(sqv[b][:, j, :], outs[j], AF.Square, reads=[okeys[j]], writes=[("fsq%d" % b, j)])
                for j in range(4):
                    acc = 4 + j // 2
                    mm(psb(acc), ones[:], sqv[b][:, j, :], nt == 0 and j % 2 == 0, nt == NT - 1 and j % 2 == 1,
                       reads=["ones", ("fsq%d" % b, j)], writes=[("ps", acc)], signal=(j % 2 == 1))
                for r in range(2):
                    P.dma("sync", lambda e, r=r, nt=nt, b=b: e.dma_start(out=hsd2_d[r, nt], in_=stg[b][:, r, :]),
                          reads=[("fstg%d" % b, r)], writes=[("hsd2", r, nt)])
                yield
            for o in range(2):
                act(normRow[:, o, :], psb(4 + o), AF.Sqrt, reads=[("ps", 4 + o), "epsT"], writes=[("normRow", o)],
                    scale=0.5, bias=epsT[:, 0:1])
                P.op("vector", lambda e, o=o: e.reciprocal(out=normRow[:, o, :], in_=normRow[:, o, :]),
                     reads=[("normRow", o)], writes=[("normRow", o)])
            yield

        def dft_kpass(o, hs, hsname, hd, hdname, Kb, kname, tabs, tabnames, T, Tn, hook=None):
            for fb in range(8):
                tb = tabs[fb % 2]
                tk = tabnames[fb % 2]
                P.dma("sync", lambda e, fb=fb, tb=tb: e.dma_start(out=tb, in_=TF_d[fb]), writes=[tk])
                if hook is not None:
                    hook(fb, P.last_w.get((tabnames[(fb + 1) % 2])))
                b0 = 4 * (fb % 2)
                srcs = [(hs, hsname, 0), (hd, hdname, 0), (hs, hsname, 1), (hd, hdname, 1)]
                for mh in range(8):
                    for q in range(4):
                        src, sname, r = srcs[q]
                        mm(PS[:, b0 + q, :], tb[:, q, mh, :], src[:, r, mh, :], mh == 0, mh == 7,
                           reads=[tk, (sname, r * 8 + mh)], writes=[("ps", b0 + q)], signal=(mh == 7 and q == 3))
                nr = normRow[:, o, :]
                sr = skipRow[:, o * 512:(o + 1) * 512]
                kE = [("ps", b0 + q) for q in range(4)]
                cp("scalar", T[0], PS[:, b0 + 2, :], reads=[kE[2]], writes=[Tn[0]])
                cp("scalar", T[1], PS[:, b0 + 3, :], reads=[kE[3]], writes=[Tn[1]])
                tt("vector", T[2], PS[:, b0, :], T[0], ALU.add, reads=[kE[0], Tn[0]], writes=[Tn[2]])
                tt("vector", T[3], PS[:, b0, :], T[0], ALU.subtract, reads=[kE[0], Tn[0]], writes=[Tn[3]])
                tt("vector", T[4], PS[:, b0 + 1, :], T[1], ALU.add, reads=[kE[1], Tn[1]], writes=[Tn[4]])
                tt("vector", T[5], PS[:, b0 + 1, :], T[1], ALU.subtract, reads=[kE[1], Tn[1]], writes=[Tn[5]])
                tt("vector", T[2], T[2], nr, ALU.mult, reads=[Tn[2], ("normRow", o)], writes=[Tn[2]])
                tt("gpsimd", Kb[0][:, fb, :], T[2], sr, ALU.add, reads=[Tn[2], "rowP1"], writes=[(kname + "0", fb)])
                tt("vector", T[3], T[3], nr, ALU.mult, reads=[Tn[3], ("normRow", o)], writes=[Tn[3]])
                tt("gpsimd", Kb[2][:, fb, :], T[3], sr, ALU.add, reads=[Tn[3], "rowP1"], writes=[(kname + "2", fb)])
                tt("vector", Kb[1][:, fb, :], T[4], nr, ALU.mult, reads=[Tn[4], ("normRow", o)], writes=[(kname + "1", fb)])
                tt("vector", Kb[3][:, fb, :], T[5], nr, ALU.mult, reads=[Tn[5], ("normRow", o)], writes=[(kname + "3", fb)])

        def dft_zpass(o, z, zname, Kb, kname, PMs, pmnames, tabs, tabnames, T, Tn, hook=None):
            for fb in range(8):
                tb = tabs[fb % 2]
                tk = tabnames[fb % 2]
                P.dma("sync", lambda e, fb=fb, tb=tb: e.dma_start(out=tb, in_=TF_d[fb]), writes=[tk])
                if hook is not None:
                    hook(fb, P.last_w.get((tabnames[(fb + 1) % 2])))
                b0 = 4 * (fb % 2)
                for mh in range(8):
                    for q in range(4):
                        r = q // 2
                        mm(PS[:, b0 + q, :], tb[:, q, mh, :], z[:, r, mh, :], mh == 0, mh == 7,
                           reads=[tk, (zname, r * 8 + mh)], writes=[("ps", b0 + q)], signal=(mh == 7 and q == 3))
                kE = [("ps", b0 + q) for q in range(4)]
                Skr, Ski, Dkr, Dki = [Kb[q][:, fb, :] for q in range(4)]
                nSkr, nSki, nDkr, nDki = [(kname + str(q), fb) for q in range(4)]
                cp("scalar", T[0], PS[:, b0 + 2, :], reads=[kE[2]], writes=[Tn[0]])
                cp("scalar", T[1], PS[:, b0 + 3, :], reads=[kE[3]], writes=[Tn[1]])
                tt("vector", T[2], PS[:, b0, :], T[0], ALU.add, reads=[kE[0], Tn[0]], writes=[Tn[2]])
                tt("vector", T[3], PS[:, b0, :], T[0], ALU.subtract, reads=[kE[0], Tn[0]], writes=[Tn[3]])
                tt("vector", T[4], PS[:, b0 + 1, :], T[1], ALU.add, reads=[kE[1], Tn[1]], writes=[Tn[4]])
                tt("vector", T[5], PS[:, b0 + 1, :], T[1], ALU.subtract, reads=[kE[1], Tn[1]], writes=[Tn[5]])
                tt("vector", T[6], T[2], Skr, ALU.mult, reads=[Tn[2], nSkr], writes=[Tn[6]])
                tt("vector", T[7], T[4], Ski, ALU.mult, reads=[Tn[4], nSki], writes=[Tn[7]])
                tt("gpsimd", T[6], T[6], T[7], ALU.subtract, reads=[Tn[6], Tn[7]], writes=[Tn[6]])
                tt("vector", T[10], T[2], Ski, ALU.mult, reads=[Tn[2], nSki], writes=[Tn[10]])
                tt("vector", T[11], T[4], Skr, ALU.mult, reads=[Tn[4], nSkr], writes=[Tn[11]])
                tt("gpsimd", T[7], T[10], T[11], ALU.add, reads=[Tn[10], Tn[11]], writes=[Tn[7]])
                tt("vector", T[8], T[3], Dkr, ALU.mult, reads=[Tn[3], nDkr], writes=[Tn[8]])
                tt("vector", T[9], T[5], Dki, ALU.mult, reads=[Tn[5], nDki], writes=[Tn[9]])
                tt("gpsimd", T[8], T[8], T[9], ALU.subtract, reads=[Tn[8], Tn[9]], writes=[Tn[8]])
                tt("vector", T[10], T[3], Dki, ALU.mult, reads=[Tn[3], nDki], writes=[Tn[10]])
                tt("vector", T[11], T[5], Dkr, ALU.mult, reads=[Tn[5], nDkr], writes=[Tn[11]])
                tt("gpsimd", T[9], T[10], T[11], ALU.add, reads=[Tn[10], Tn[11]], writes=[Tn[9]])
                tt("vector", PMs[0][:, fb, :], T[6], T[8], ALU.add, reads=[Tn[6], Tn[8]], writes=[(pmnames[0], fb)])
                tt("gpsimd", PMs[2][:, fb, :], T[6], T[8], ALU.subtract, reads=[Tn[6], Tn[8]], writes=[(pmnames[2], fb)])
                tt("vector", PMs[1][:, fb, :], T[7], T[9], ALU.add, reads=[Tn[7], Tn[9]], writes=[(pmnames[1], fb)])
                tt("gpsimd", PMs[3][:, fb, :], T[7], T[9], ALU.subtract, reads=[Tn[7], Tn[9]], writes=[(pmnames[3], fb)])

        def dft_inverse(PMs, pmnames, mul, mulname, dst, dstname, tabs, tabnames, post=None, banks=(0, 1, 2, 3), hook=None):
            nbk = 0
            prev = None
            for r in range(2):
                for j in range(8):
                    tb = tabs[nbk % 2]
                    tk = tabnames[nbk % 2]
                    P.dma("sync", lambda e, r=r, j=j, tb=tb: e.dma_start(out=tb, in_=TI_d[r, j]), writes=[tk])
                    if hook is not None:
                        hook(nbk)
                    bank = banks[nbk % len(banks)]
                    yk = ("ps", bank)
                    for fb in range(8):
                        mm(psb(bank), tb[:, 0, fb, :], PMs[2 * r][:, fb, :], fb == 0, False,
                           reads=[tk, (pmnames[2 * r], fb)], writes=[yk], signal=False)
                        mm(psb(bank), tb[:, 1, fb, :], PMs[2 * r + 1][:, fb, :], False, fb == 7,
                           reads=[tk, (pmnames[2 * r + 1], fb)], writes=[yk], signal=(fb == 7))
                    stt("vector", dst[:, r, j, :], psb(bank), 1.0 / 2048.0, mul[:, r, j, :], ALU.mult, ALU.mult,
                        reads=[yk, (mulname, r * 8 + j)], writes=[(dstname, r * 8 + j)])
                    if post is not None and prev is not None:
                        post(*prev)
                    prev = (r, j)
                    nbk += 1
                    yield
            if post is not None:
                post(*prev)
                yield

        T4 = [128, 2, 8, 512]
        flat4 = lambda ap: ap.rearrange("p r j c -> p (r j c)")
        keys16 = lambda nm: [(nm, t) for t in range(16)]
        hs1 = view("hs1", O_A, T4, BF16)
        hd1 = view("hd1", O_B, T4, BF16)
        lockstep([s1_gen(), filter_gen_all(hs1, hd1)])
        if maybe_stop("xn", [(flat(xnT), xnT_keys(0, 16))]):
            return nc
        if maybe_stop("filt1", [(flat4(hs1), keys16("hs1")), (flat4(hd1), keys16("hd1")),
                                (normRow[:, 0, :], [("normRow", 0)])]):
            return nc
        z_tok = view("z_tok", O_ZTOK, T4, BF16)
        hy_group("v", 2048, z_tok, "z_tok", O_TAB)
        if maybe_stop("hyv", [(flat4(z_tok), keys16("z_tok"))]):
            return nc

        def quad_views(prefix, offs):
            vs = [view("%s%d" % (prefix, w), offs[w], [128, 8, 512], BF16) for w in range(4)]
            return vs

        class NameQ:
            pass

        K1 = quad_views("K1q", [O_Y + w * 8 * KB for w in range(4)])
        tabsK1 = [view("tabK1_%d" % b, O_YAT + b * 8 * KB, [128, 4, 8, 128], BF16) for b in range(2)]
        TA = [view("pwA%d" % j, O_TMP + j * 2 * KB, [128, 512], F32) for j in range(12)]
        TAn = ["pwA%d" % j for j in range(12)]
        dft_kpass(0, hs1, "hs1", hd1, "hd1", K1, "K1q", tabsK1, ["tabK1_0", "tabK1_1"], TA, TAn,
                  hook=lambda fb, tok: convert_wff1([fb], after=tok))
        PM1 = quad_views("PMa", [O_A, O_A + 8 * KB, O_B, O_B + 8 * KB])
        PM1n = ["PMa%d" % w for w in range(4)]
        tabsZ1 = [view("tabZ1_%d" % b, O_YAT + b * 8 * KB, [128, 4, 8, 128], BF16) for b in range(2)]
        dft_zpass(0, z_tok, "z_tok", K1, "K1q", PM1, PM1n, tabsZ1, ["tabZ1_0", "tabZ1_1"], TA, TAn,
                  hook=lambda fb, tok: (hy_prefetch("x1", 1024, O_ZOT) if fb == 0 else None, convert_wff1([8 + fb], after=tok)))
        if maybe_stop("fwd1", [(PM1[w].rearrange("p a b -> p (a b)"), [(PM1n[w], f) for f in range(8)]) for w in range(4)]):
            return nc
        x1_tok = view("x1_tok", O_TAB, T4, BF16)
        hy_group("x1", 1024, x1_tok, "x1_tok", O_ZTOK, wb_off=O_ZOT, ob_off=O_ZOT + 8 * KB)
        hs2 = view("hs2", O_ZOT, T4, BF16)
        hd2 = view("hd2", O_Y, T4, BF16)

        def load_filt2(r, dst_, nm_):
            for q4 in range(4):
                P.dma("sync", lambda e, r=r, q4=q4, dst_=dst_: e.dma_start(
                    out=dst_.rearrange("p r j c -> p (r j) c")[:, q4 * 4:(q4 + 1) * 4, :],
                    in_=hsd2_d[r, q4 * 4:(q4 + 1) * 4].rearrange("t p c -> p t c")),
                    reads=[("hsd2", r, t) for t in range(q4 * 4, q4 * 4 + 4)],
                    writes=[(nm_, t) for t in range(q4 * 4, q4 * 4 + 4)])

        u_tok = view("u_tok", O_ZTOK, T4, BF16)
        tabsB = [view("tabB%d" % b, O_YAT + b * 4 * KB, [128, 2, 8, 128], BF16) for b in range(2)]

        def inv1_hook(nb):
            if nb == 3:
                load_filt2(0, hs2, "hs2")
            if nb == 5:
                load_filt2(1, hd2, "hd2")

        lockstep([dft_inverse(PM1, PM1n, x1_tok, "x1_tok", u_tok, "u_tok", tabsB, ["tabB0", "tabB1"], hook=inv1_hook)])
        if maybe_stop("inv1", [(flat4(u_tok), keys16("u_tok"))]):
            return nc
        K2 = quad_views("K2q", [O_A, O_A + 8 * KB, O_B, O_B + 8 * KB])
        tabsK2 = [view("tabK2_%d" % b, O_YAT + b * 8 * KB, [128, 4, 8, 128], BF16) for b in range(2)]
        TC = [view("pwC%d" % j, O_TMP + j * 2 * KB, [128, 512], F32) for j in range(12)]
        TCn = ["pwC%d" % j for j in range(12)]
        dft_kpass(1, hs2, "hs2", hd2, "hd2", K2, "K2q", tabsK2, ["tabK2_0", "tabK2_1"], TC, TCn)
        PM2 = quad_views("PMb", [O_Y + w * 8 * KB for w in range(4)])
        PM2n = ["PMb%d" % w for w in range(4)]
        tabsZ2 = [view("tabZ2_%d" % b, O_YAT + b * 8 * KB, [128, 4, 8, 128], BF16) for b in range(2)]
        dft_zpass(1, u_tok, "u_tok", K2, "K2q", PM2, PM2n, tabsZ2, ["tabZ2_0", "tabZ2_1"], TC, TCn,
                  hook=lambda fb, tok: (hy_prefetch("x2", 1536, O_ZOT) if fb == 0 else None,
                                        convert_wff2(fb, after=tok)))
        x2_tok = view("x2_tok", O_TAB, T4, BF16)
        hy_group("x2", 1536, x2_tok, "x2_tok", O_ZTOK, wb_off=O_ZOT, ob_off=O_ZOT + 8 * KB)
        zo_tok = view("zo_tok", O_ZTOK, T4, BF16)
        zoT = view("zoT", O_ZOT, [128, 4, L], BF16)
        tabsD = [view("tabD%d" % b, O_TMP + b * 4 * KB, [128, 2, 8, 128], BF16) for b in range(2)]

        def zo_post(r, j):
            t = r * 8 + j
            for cb in range(4):
                tr(psT(6, 4)[:, cb, :], zo_tok[:, r, j, cb * 128:(cb + 1) * 128], reads=[("zo_tok", t)],
                   writes=[("ps", 6)], signal=(cb == 3))
            st_ = 256 * j + r
            cp("scalar", zoT[:, :, st_:st_ + 255:2], psT(6, 4), reads=[("ps", 6)],
               writes=[("zoT", 2 * j), ("zoT", 2 * j + 1)])

        inv2_gen = dft_inverse(PM2, PM2n, x2_tok, "x2_tok", zo_tok, "zo_tok", tabsD, ["tabD0", "tabD1"], post=zo_post,
                               banks=(0,))

        y_aT = view("y_aT", O_YAT, [128, 4, L], BF16)
        wbU = view("wbU", O_B, [128, 8, 512], BF16)
        wbV = view("wbV", O_B + 8 * KB, [128, 8, 512], BF16)
        NB_A = 4
        guv = [view("gu%d" % b, O_A + b * 2 * KB, [128, 512], F32) for b in range(NB_A)]
        gvv = [view("gv%d" % b, O_A + 8 * KB + b * 2 * KB, [128, 512], F32) for b in range(2)]
        vnv = [view("vn%d" % b, O_A + 12 * KB + b * 2 * KB, [128, 512], F32) for b in range(2)]
        wsT = view("wsT", O_END + 4 * KB, [128, 8, 128], BF16)
        vnbv = [view("vnb%d" % b, O_TMP + 16 * KB + b * KB, [128, 512], BF16) for b in range(NB_A)]
        yav = [view("ya%d" % b, O_TMP + 20 * KB + b * KB, [128, 512], BF16) for b in range(3)]
        junkA = view("junkA", O_TMP + 23 * KB, [128, 512], BF16)
        ringA = {"i": 0}

        def next_pair_a():
            s_ = 1 + ringA["i"] % 2
            ringA["i"] += 1
            return s_

        def ma_a(i):
            b = i % NB_A
            b2 = i % 2
            s = next_pair_a()
            for k in range(8):
                mm(PS[:, 2 * s, :], xnT[:, k, i * 128:(i + 1) * 128], wbU[:, k, :], k == 0, k == 7,
                   reads=[("xnT", i), "wbU"], writes=[("ps", 2 * s)], signal=False)
                mm(PS[:, 2 * s + 1, :], xnT[:, k, i * 128:(i + 1) * 128], wbV[:, k, :], k == 0, k == 7,
                   reads=[("xnT", i), "wbV"], writes=[("ps", 2 * s + 1)], signal=(k == 7))
            guk, gvk, vnk, vnbk = "gu%d" % b, "gv%d" % b2, "vn%d" % b2, "vnb%d" % b
            act(guv[b], PS[:, 2 * s, :], AF.Gelu_apprx_tanh, reads=[("ps", 2 * s), ("ps", 2 * s + 1)], writes=[guk])
            act(gvv[b2], PS[:, 2 * s + 1, :], AF.Gelu_apprx_tanh, reads=[("ps", 2 * s), ("ps", 2 * s + 1)],
                writes=[gvk, stk(S_MA_SUM + i)], accum=stc(S_MA_SUM + i))
            ts("vector", stc(S_MA_NM + i), stc(S_MA_SUM + i), -1.0 / 512.0, None, ALU.mult, None,
               reads=[stk(S_MA_SUM + i)], writes=[stk(S_MA_NM + i)])
            act(junkA, gvv[b2], AF.Square, reads=[gvk, stk(S_MA_NM + i)], writes=["junkA", stk(S_MA_VAR + i)],
                bias=stc(S_MA_NM + i), accum=stc(S_MA_VAR + i))
            rsqrt_col(stc(S_MA_VAR + i), 1.0 / 512.0, stk(S_MA_VAR + i))
            ts("vector", vnv[b2], gvv[b2], stc(S_MA_NM + i), stc(S_MA_VAR + i), ALU.add, ALU.mult,
               reads=[gvk, stk(S_MA_NM + i), stk(S_MA_VAR + i)], writes=[vnk])
            tt("vector", vnbv[b], vnv[b2], gainRow, ALU.mult, reads=[vnk, "rowP1"], writes=[vnbk])

        def ma_b(i):
            b = i % NB_A
            by = i % 3
            guk, vnbk, yak = "gu%d" % b, "vnb%d" % b, "ya%d" % by
            for g in range(8):
                mm(PS[:, 7, g * 64:(g + 1) * 64], wsT[:, g, :], vnbv[b][:, g * 64:(g + 1) * 64], True, True,
                   reads=["wsT", vnbk], writes=[("ps", 7)], signal=(g == 7))
            for g in range(8):
                stt("vector", yav[by][:, g * 64:(g + 1) * 64], PS[:, 7, g * 64:(g + 1) * 64],
                    colp[:, C_BST + g:C_BST + g + 1], guv[b][:, g * 64:(g + 1) * 64], ALU.add, ALU.mult,
                    reads=[("ps", 7), guk, "colp"], writes=[yak])

        def ma_c(i):
            b = i % 3
            yak = "ya%d" % b
            for cb in range(4):
                tr(psT(1, 4)[:, cb, :], yav[b][:, cb * 128:(cb + 1) * 128], reads=[yak], writes=[("ps", 1)],
                   signal=(cb == 3))
            cp("scalar", y_aT[:, :, i * 128:(i + 1) * 128], psT(1, 4), reads=[("ps", 1)], writes=[("y_aT", i)])

        def mixa_gen():
            P.dma("gpsimd", lambda e: e.dma_start(out=wbU, in_=w_in_v[:, :, 0:512]), writes=["wbU"])
            P.dma("gpsimd", lambda e: e.dma_start(out=wbV, in_=w_in_v[:, :, 512:1024]), writes=["wbV"])
            P.dma("gpsimd", lambda e: e.dma_start(out=wsT, in_=wsT_d), writes=["wsT"])
            for step in range(NT + 3):
                if step < NT:
                    ma_a(step)
                if 0 <= step - 2 < NT:
                    ma_b(step - 2)
                if 0 <= step - 3 < NT:
                    ma_c(step - 3)
                yield

        lockstep([inv2_gen, mixa_gen()])
        if maybe_stop("zo", [(flat(zoT), [("zoT", t) for t in range(16)])]):
            return nc
        if maybe_stop("mixa", [(flat(y_aT), [("y_aT", t) for t in range(16)])]):
            return nc
        mergedT = view("mergedT", O_A, [128, 8, L], BF16)
        woa = view("woa", O_B, [128, 4, D], BF16)
        wob = view("wob", O_B + 8 * KB, [128, 4, D], BF16)
        wgv = [view("wg%d" % b, O_Y + b * 4 * KB, [128, 8, 2, 128], BF16) for b in range(2)]
        sgav = [view("sga%d" % b, O_Y + 8 * KB + b * 2 * KB, [128, 512], F32) for b in range(2)]
        sgbv = [view("sgb%d" % b, O_Y + 12 * KB + b * 2 * KB, [128, 512], F32) for b in range(2)]
        m1v = [view("m1_%d" % b, O_Y + 16 * KB + b * 2 * KB, [128, 512], F32) for b in range(2)]
        m2v = [view("m2_%d" % b, O_Y + 20 * KB + b * 2 * KB, [128, 512], F32) for b in range(2)]
        wo = view("wo", O_TAB, [128, 8, D], BF16)
        rowP3 = view("rowP3", O_ROW, [128, 3072], F32)
        P.dma("sync", lambda e: e.dma_start(out=rowP3, in_=rowp_d[:, 3072:6144].partition_broadcast(128)),
              writes=["rowP3"])
        GpmRow, GpreRow, GpfRow = rowP3[:, 0:1024], rowP3[:, 1024:2048], rowP3[:, 2048:3072]

        def load_wo():
            P.dma("gpsimd", lambda e: e.dma_start(out=wo, in_=w_o.rearrange("(k p) c -> p k c", p=128)), writes=["wo"])

        it = 0
        for j in range(8):
            wb_ = wgv[j % 2]
            wk = "wg%d" % (j % 2)
            for ab in range(2):
                c0 = 2560 + ab * 1024 + j * 128
                P.dma("gpsimd", lambda e, wb_=wb_, ab=ab, c0=c0: e.dma_start(out=wb_[:, :, ab, :],
                                                                             in_=w_in_v[:, :, c0:c0 + 128]),
                      writes=[(wk, ab)])
            if j == 0:
                P.dma("gpsimd", lambda e: e.dma_start(out=woa, in_=w_out_a.rearrange("(k p) c -> p k c", p=128)),
                      writes=["woa"])
                P.dma("gpsimd", lambda e: e.dma_start(out=wob, in_=w_out_b.rearrange("(k p) c -> p k c", p=128)),
                      writes=["wob"])
                load_wo()
            for tg in range(4):
                b0 = 4 * (it % 2)
                bb = it % 2
                it += 1
                t0 = tg * 512
                kya, kyb, kga, kgb = [("ps", b0 + q) for q in range(4)]
                for k in range(8):
                    mm(psb(b0 + 2), wb_[:, k, 0, :], xnT[:, k, t0:t0 + 512], k == 0, k == 7,
                       reads=[(wk, 0)] + xnT_keys(tg * 4, tg * 4 + 4), writes=[kga])
                for k in range(8):
                    mm(psb(b0 + 3), wb_[:, k, 1, :], xnT[:, k, t0:t0 + 512], k == 0, k == 7,
                       reads=[(wk, 1)] + xnT_keys(tg * 4, tg * 4 + 4), writes=[kgb])
                for cb in range(4):
                    mm(psb(b0), woa[:, cb, j * 128:(j + 1) * 128], y_aT[:, cb, t0:t0 + 512], cb == 0, cb == 3,
                       reads=["woa"] + [("y_aT", t) for t in range(tg * 4, tg * 4 + 4)], writes=[kya])
                for cb in range(4):
                    mm(psb(b0 + 1), wob[:, cb, j * 128:(j + 1) * 128], zoT[:, cb, t0:t0 + 512], cb == 0, cb == 3,
                       reads=["wob"] + [("zoT", t) for t in range(tg * 4, tg * 4 + 4)], writes=[kyb])
                act(sgav[bb], psb(b0 + 2), AF.Sigmoid, reads=[kga], writes=["sga%d" % bb])
                act(sgbv[bb], psb(b0 + 3), AF.Sigmoid, reads=[kgb], writes=["sgb%d" % bb])
                tt("vector", m1v[bb], psb(b0), sgav[bb], ALU.mult, reads=[kya, "sga%d" % bb], writes=["m1_%d" % bb])
                tt("vector", m2v[bb], psb(b0 + 1), sgbv[bb], ALU.mult, reads=[kyb, "sgb%d" % bb], writes=["m2_%d" % bb])
                tt("vector", mergedT[:, j, t0:t0 + 512], m1v[bb], m2v[bb], ALU.add,
                   reads=["m1_%d" % bb, "m2_%d" % bb], writes=[("mergedT", j, tg)])

        if maybe_stop("merge", [(flat(mergedT), [("mergedT", j, tg) for j in range(8) for tg in range(4)])]):
            return nc
        wff2 = view("wff2", 0, [128, 32, D], BF16)
        fT = view("fT", O_Y, [128, 32, 512], BF16)
        w1bv = [view("w1b%d" % b, O_B + b * 4 * KB, [128, 8, 256], BF16) for b in range(2)]
        w1bv.append(view("w1b2", O_END + 4 * KB, [128, 8, 256], BF16))
        hnT = view("hnT", O_B + 8 * KB, [128, 8, 512], BF16)
        hv = [view("h%d" % b, O_TMP + b * 4 * KB, [128, D], F32) for b in range(4)]
        tmpv = [view("tmp%d" % b, O_TMP + 16 * KB + b * 4 * KB, [128, D], F32) for b in range(2)]
        hsbv = [view("hsb0", O_END, [128, D], BF16), view("hsb1", O_END + 2 * KB, [128, D], BF16)]
        junk3 = junk3_t[:]
        w_ff2_v = w_ff2.rearrange("(k p) c -> p k c", p=128)
        w_ff1_v = w_ff1.rearrange("(k p) c -> p k c", p=128)

        def load_wff2():
            for q in range(8):
                P.dma("gpsimd", lambda e, q=q: e.dma_start(out=wff2[:, q * 4:(q + 1) * 4, :],
                                                           in_=wff2s_d[q].rearrange("p (k c) -> p k c", k=4)),
                      reads=[("wff2s", q)], writes=[("wff2", q)])

        out_toks = []
        cnt3 = {"ti": 0}

        def mstage(i, q, s):
            h = hv[q]
            hk = "h%d" % q
            P.dma("sync", lambda e, i=i, h=h: e.dma_start(out=h, in_=x[i * 128:(i + 1) * 128, :]), writes=[hk])
            for half in range(2):
                for k in range(8):
                    mm(PS[:, 2 * s + half, :], mergedT[:, k, i * 128:(i + 1) * 128],
                       wo[:, k, half * 512:(half + 1) * 512], k == 0, k == 7,
                       reads=[("mergedT", k, i // 4), "wo"], writes=[("ps", 2 * s + half)])
            mp = pspair(s)
            act(junk3, mp, AF.Square, reads=pk(s), writes=["junk3", stk(S_M + i)], accum=stc(S_M + i))
            rsqrt_col(stc(S_M + i), 1.0 / D, stk(S_M + i))
            tmp = tmpv[cnt3["ti"] % 2]
            tk_ = "tmp%d" % (cnt3["ti"] % 2)
            cnt3["ti"] += 1
            stt("vector", tmp, mp, stc(S_M + i), GpmRow, ALU.mult, ALU.mult,
                reads=pk(s) + [stk(S_M + i), "rowP3"], writes=[tk_])
            tt("vector", h, h, tmp, ALU.add, reads=[hk, tk_], writes=[hk])
            act(junk3, h, AF.Square, reads=[hk], writes=["junk3", stk(S_H + i)], accum=stc(S_H + i))
            rsqrt_col(stc(S_H + i), 1.0 / D, stk(S_H + i))
            stt("vector", hsbv[q % 2], h, stc(S_H + i), GpreRow, ALU.mult, ALU.mult,
                reads=[hk, stk(S_H + i), "rowP3"], writes=["hsb%d" % (q % 2)])

        def mtrans(q):
            for k in range(8):
                tr(psT(6, 8)[:, k, :], hsbv[q % 2][:, k * 128:(k + 1) * 128], reads=["hsb%d" % (q % 2)],
                   writes=[("ps", 6)], signal=(k == 7))
            cp("scalar", hnT[:, :, q * 128:(q + 1) * 128], psT(6, 8), reads=[("ps", 6)], writes=[("hnT", q)])

        def ff1(tb):
            wb1, w1k = None, None
            for fb in range(32):
                if fb % 2 == 0:
                    fp = fb // 2
                    wb1 = w1bv[fp % 3]
                    w1k = "w1b%d" % (fp % 3)
                    P.dma("sync", lambda e, wb1=wb1, fp=fp: e.dma_start(
                        out=wb1.rearrange("p k c -> p (k c)"), in_=wff1s_d[fp]), reads=[("wff1s", fp)], writes=[w1k])
                bank = 4 + (fb % 2)
                for k in range(8):
                    mm(psb(bank), wb1[:, k, (fb % 2) * 128:(fb % 2) * 128 + 128], hnT[:, k, :], k == 0, k == 7,
                       reads=[w1k] + [("hnT", q) for q in range(4)], writes=[("ps", bank)])
                act(psb(bank), psb(bank), AF.Relu, reads=[("ps", bank)], writes=[("ps", bank)])
                act(fT[:, fb, :], psb(bank), AF.Square, reads=[("ps", bank)], writes=[("fT", fb)])

        def ff2(tb, q):
            i = tb * 4 + q
            h = hv[q]
            hk = "h%d" % q
            s = q % 2
            for half in range(2):
                for fb in range(32):
                    mm(PS[:, 2 * s + half, :], fT[:, fb, q * 128:(q + 1) * 128],
                       wff2[:, fb, half * 512:(half + 1) * 512], fb == 0, fb == 31,
                       reads=[("fT", fb), ("wff2", fb // 4)], writes=[("ps", 2 * s + half)])
            fp_ = pspair(s)
            act(junk3, fp_, AF.Square, reads=pk(s), writes=["junk3", stk(S_F + i)], accum=stc(S_F + i))
            rsqrt_col(stc(S_F + i), 1.0 / D, stk(S_F + i))
            tmp = tmpv[cnt3["ti"] % 2]
            tk_ = "tmp%d" % (cnt3["ti"] % 2)
            cnt3["ti"] += 1
            stt("vector", tmp, fp_, stc(S_F + i), GpfRow, ALU.mult, ALU.mult,
                reads=pk(s) + [stk(S_F + i), "rowP3"], writes=[tk_])
            tt("vector", tmp, tmp, h, ALU.add, reads=[tk_, hk], writes=[tk_])
            out_toks.append(P.dma("sync", lambda e, i=i, tmp=tmp: e.dma_start(out=out[i * 128:(i + 1) * 128, :], in_=tmp),
                                  reads=[tk_]))

        for q in range(4):
            mstage(q, q, q % 3)
            if q >= 1:
                mtrans(q - 1)
        mtrans(3)
        load_wff2()
        for tb in range(4):
            ff1(tb)
            for q in range(4):
                ff2(tb, q)
                if tb < 3:
                    mstage((tb + 1) * 4 + q, q, 2)
                    if q >= 1:
                        mtrans(q - 1)
            if tb < 3:
                mtrans(3)
        for t in out_toks:
            P.wait("sync", t)
        P.emit()
        print("program: ops=%d waits=%d counts=%s" % (P.nops, P.nwaits, P.cnt), flush=True)
    return nc


_CONST = {}


def _constants():
    if _CONST:
        return _CONST
    bf = ml_dtypes.bfloat16
    H = L // 2
    m = np.arange(H, dtype=np.float64)
    f = np.arange(H, dtype=np.float64) + 0.5
    tabs = []
    for r in range(2):
        ang = 2.0 * np.pi * np.outer(2 * m + r, f) / (2 * L)
        tabs.append((np.cos(ang), np.sin(ang)))
    def fwd(T):
        return T.reshape(8, 128, 8, 128).transpose(2, 1, 0, 3)
    TF = np.stack([fwd(tabs[0][0]), fwd(tabs[0][1]), fwd(tabs[1][0]), fwd(tabs[1][1])], axis=2)
    def inv(T):
        return T.reshape(8, 128, 8, 128).transpose(0, 3, 2, 1)
    TI = np.stack([np.stack([inv(tabs[r][0]), inv(tabs[r][1])], axis=2) for r in range(2)], axis=0)
    _CONST["TF"] = np.ascontiguousarray(TF).astype(bf)
    _CONST["TI"] = np.ascontiguousarray(TI).astype(bf)
    _CONST["ident"] = np.eye(128, dtype=np.float32).astype(bf)
    t = np.linspace(0.0, 1.0, L, dtype=np.float32)[:, None]
    bands = 16
    w = (2.0 * math.pi * np.arange(L, dtype=np.float32)[:, None] / L).astype(np.float32)
    fr = np.linspace(1e-4, bands - 1, bands, dtype=np.float32)[None, :]
    feats = np.concatenate([t, np.cos(fr * w), -np.sin(fr * w)], axis=-1).astype(np.float32)
    _CONST["featsT"] = np.ascontiguousarray(feats.T)
    max_decay = math.log(1e-2) / 0.3
    min_decay = math.log(1e-2) / 1.5
    _CONST["delta"] = np.abs(np.linspace(min_decay, max_decay, 512, dtype=np.float32)).astype(np.float32)
    _CONST["tneg"] = np.ascontiguousarray((-t[:, 0]).reshape(8, 128, 2).transpose(1, 2, 0).reshape(128, 16))
    return _CONST


_PROG = {}


def kernel(x, g_pre_mix, w_in, a_v_gain, a_w_s, a_b_s, w_out_a, b_conv_w, b_conv_b,
           b_filt_w1, b_filt_b1, b_filt_f1, b_filt_w2, b_filt_b2, b_filt_f2, b_filt_w3,
           b_skip, w_out_b, w_o, g_post_mix, g_pre_ffn, w_ff1, w_ff2, g_post_ffn):
    f32 = lambda a: np.ascontiguousarray(np.asarray(a, dtype=np.float32))
    c = _constants()
    x = f32(x)
    colp = np.zeros((128, NCOLP), np.float32)
    cw = f32(b_conv_w)[0]
    cbias = f32(b_conv_b)[0]
    colp[:, C_W0:C_W0 + 12] = cw[0].reshape(12, 128).T
    colp[:, C_W1:C_W1 + 12] = cw[1].reshape(12, 128).T
    colp[:, C_W2:C_W2 + 12] = cw[2].reshape(12, 128).T
    colp[:, C_CB:C_CB + 12] = cbias.reshape(12, 128).T
    colp[0:64, C_F1] = f32(b_filt_f1)[0]
    colp[0:64, C_B1] = f32(b_filt_b1)[0]
    colp[0:64, C_F2] = f32(b_filt_f2)[0]
    colp[0:64, C_B2] = f32(b_filt_b2)[0]
    colp[:, C_BST:C_BST + 8] = f32(a_b_s)[0].T
    colp[:, C_TNEG:C_TNEG + 16] = c["tneg"]
    rowp = np.zeros((1, NROWP), np.float32)
    rowp[0, R_GPM:R_GPM + 1024] = f32(g_pre_mix)[0]
    rowp[0, R_GAIN:R_GAIN + 512] = f32(a_v_gain)[0]
    rowp[0, R_SKIP:R_SKIP + 1024] = f32(b_skip)[0].reshape(-1)
    rowp[0, R_DELTA:R_DELTA + 512] = c["delta"]
    rowp[0, R_GPOSTMIX:R_GPOSTMIX + 1024] = f32(g_post_mix)[0]
    rowp[0, R_GPREFFN:R_GPREFFN + 1024] = f32(g_pre_ffn)[0]
    rowp[0, R_GPOSTFFN:R_GPOSTFFN + 1024] = f32(g_post_ffn)[0]
    shared = {
        "w_in": f32(w_in)[0], "w_out_a": f32(w_out_a)[0], "w_out_b": f32(w_out_b)[0], "w_o": f32(w_o)[0],
        "w_ff1": f32(w_ff1)[0], "w_ff2": f32(w_ff2)[0],
        "wsT": np.ascontiguousarray(f32(a_w_s)[0].transpose(2, 0, 1)),
        "colp": colp, "rowp": rowp, "featsT": c["featsT"],
        "fw1": f32(b_filt_w1)[0], "fw2": f32(b_filt_w2)[0], "fw3": f32(b_filt_w3)[0],
        "ident": c["ident"], "TF": c["TF"], "TI": c["TI"],
    }
    if "nc" not in _PROG:
        _PROG["nc"] = build_program(STOP_AFTER)
    nc = _PROG["nc"]
    in_maps = []
    for b in range(8):
        m = dict(shared)
        m["x"] = np.ascontiguousarray(x[b])
        in_maps.append(m)
    res = run_bass_kernel_spmd(nc, in_maps, core_ids=list(range(8)))
    outs = [np.asarray(r["out"], dtype=np.float32) for r in res.results]
    return np.stack(outs, axis=0)
```

```python
import math
from contextlib import ExitStack

import numpy as np
import ml_dtypes

import concourse.bass as bass
import concourse.mybir as mybir
from concourse.bass_utils import run_bass_kernel_spmd

F32 = mybir.dt.float32
BF16 = mybir.dt.bfloat16
AF = mybir.ActivationFunctionType
ALU = mybir.AluOpType

L = 2048
D = 1024
NT = 16
EPS = 1e-6
KB = 1024
STOP_AFTER = None


class Buf:
    def __init__(self, name, off, size):
        self.name, self.off, self.size = name, off, size
        self.toks = {}
        self.inherit = {}
        self.active = False


class Prog:
    ENGS = ("sync", "scalar", "gpsimd", "vector", "tensor")

    def __init__(self, nc, es, n_dma_sems=32):
        self.nc = nc
        self.ops = {e: [] for e in self.ENGS}
        self.sem = {e: es.enter_context(nc.semaphore("s_" + e)) for e in self.ENGS}
        self.cnt = {e: 0 for e in self.ENGS}
        self.dsem = [es.enter_context(nc.semaphore("d%d" % i)) for i in range(n_dma_sems)]
        self.dcnt = [0] * n_dma_sems
        self.dnext = 0
        self.waited = {e: {} for e in self.ENGS}
        self.last_w = {}
        self.readers = {}
        self.bufs = {}
        self.active = []
        self.nwaits = 0
        self.nops = 0

    def buf(self, name, off, size):
        assert name not in self.bufs, name
        self.bufs[name] = Buf(name, off, size)

    def _wait(self, eng, tok):
        if tok is None:
            return
        semkey, val = tok
        if semkey == eng and val > self.cnt[eng]:
            return
        if self.waited[eng].get(semkey, 0) >= val:
            return
        self.waited[eng][semkey] = val
        sem = self.sem[semkey] if isinstance(semkey, str) else self.dsem[semkey]
        self.nwaits += 1
        self.ops[eng].append(lambda e, sem=sem, val=val: e.wait_ge(sem, val))

    def _parent(self, k):
        return self.bufs.get(k[0] if isinstance(k, tuple) else k)

    def _touch(self, eng, k):
        b = self._parent(k)
        if b is None:
            return
        if not b.active:
            for o in list(self.active):
                if o is not b and o.off < b.off + b.size and b.off < o.off + o.size:
                    for src in (o.toks, o.inherit):
                        for sk, v in src.items():
                            if b.inherit.get(sk, 0) < v:
                                b.inherit[sk] = v
                    o.active = False
                    self.active.remove(o)
            b.active = True
            b.toks = {}
            self.active.append(b)
        for sk, v in b.inherit.items():
            self._wait(eng, (sk, v))

    def _deps(self, eng, reads, writes, extra):
        for k in reads:
            self._touch(eng, k)
            self._wait(eng, self.last_w.get(k))
        for k in writes:
            self._touch(eng, k)
            self._wait(eng, self.last_w.get(k))
            for t in self.readers.get(k, ()):
                self._wait(eng, t)
        for t in extra:
            self._wait(eng, t)

    def _record(self, tok, reads, writes):
        for k in reads:
            self.readers.setdefault(k, []).append(tok)
            b = self._parent(k)
            if b is not None and b.toks.get(tok[0], 0) < tok[1]:
                b.toks[tok[0]] = tok[1]
        for k in writes:
            self.last_w[k] = tok
            self.readers[k] = []
            b = self._parent(k)
            if b is not None and b.toks.get(tok[0], 0) < tok[1]:
                b.toks[tok[0]] = tok[1]

    def op(self, eng, fn, reads=(), writes=(), extra=(), signal=True):
        self._deps(eng, reads, writes, extra)
        self.nops += 1
        if signal:
            self.cnt[eng] += 1
            tok = (eng, self.cnt[eng])
            sem = self.sem[eng]
            self.ops[eng].append(lambda e, fn=fn, sem=sem: fn(e).then_inc(sem, 1))
        else:
            tok = (eng, self.cnt[eng] + 1)
            self.ops[eng].append(lambda e, fn=fn: fn(e))
        self._record(tok, reads, writes)
        return tok

    def dma(self, eng, fn, reads=(), writes=(), extra=()):
        self._deps(eng, reads, writes, extra)
        half = len(self.dsem) // 2
        base = 0 if eng == "gpsimd" else half
        self.dnext_q = getattr(self, "dnext_q", {})
        j = self.dnext_q.get(eng, 0)
        self.dnext_q[eng] = (j + 1) % half
        i = base + j
        if self.dcnt[i]:
            self._wait(eng, (i, self.dcnt[i]))
        self.dcnt[i] += 16
        tok = (i, self.dcnt[i])
        sem = self.dsem[i]
        self.ops[eng].append(lambda e, fn=fn, sem=sem: fn(e).then_inc(sem, 16))
        self._record(tok, reads, writes)
        return tok

    def wait(self, eng, tok):
        self._wait(eng, tok)

    def emit(self):
        with self.nc.Block() as block:
            for name in self.ENGS:
                ops = self.ops[name]
                if not ops:
                    continue

                def body(e, ops=ops):
                    for f in ops:
                        f(e)
                getattr(block, name)(body)


def lockstep(gens):
    alive = list(gens)
    while alive:
        for g in list(alive):
            try:
                next(g)
            except StopIteration:
                alive.remove(g)


def _prod(s):
    r = 1
    for v in s:
        r *= v
    return r


NCOLP = 76
C_W0, C_W1, C_W2, C_CB, C_F1, C_B1, C_F2, C_B2, C_BST, C_TNEG = 0, 12, 24, 36, 48, 49, 50, 51, 52, 60
R_GPM, R_GAIN, R_SKIP, R_DELTA, R_GPOSTMIX, R_GPREFFN, R_GPOSTFFN = 0, 1024, 1536, 2560, 3072, 4096, 5120
NROWP = 6144


def build_program(stop_after=None):
    nc = bass.Bass("TRN2", target_bir_lowering=False)
    dt_in = lambda n, s, d=F32: nc.dram_tensor(n, s, d, kind="ExternalInput").ap()
    x = dt_in("x", [L, D])
    w_in = dt_in("w_in", [D, 4608])
    w_out_a = dt_in("w_out_a", [512, D])
    w_out_b = dt_in("w_out_b", [512, D])
    w_o = dt_in("w_o", [D, D])
    w_ff1 = dt_in("w_ff1", [D, 4096])
    w_ff2 = dt_in("w_ff2", [4096, D])
    wsT_d = dt_in("wsT", [128, 8, 128])
    colp_d = dt_in("colp", [128, NCOLP])
    rowp_d = dt_in("rowp", [1, NROWP])
    featsT_d = dt_in("featsT", [33, L])
    fw1_d = dt_in("fw1", [33, 64])
    fw2_d = dt_in("fw2", [64, 64])
    fw3_d = dt_in("fw3", [64, 2048])
    ident_d = dt_in("ident", [128, 128], BF16)
    TF_d = dt_in("TF", [8, 128, 4, 8, 128], BF16)
    TI_d = dt_in("TI", [2, 8, 128, 2, 8, 128], BF16)
    out = nc.dram_tensor("out", [L, D], F32, kind="ExternalOutput").ap()
    wff1s_d = nc.dram_tensor("wff1_bf16_scratch", [16, 128, 2048], BF16, kind="Internal").ap()
    hsd2_d = nc.dram_tensor("hsd2_scratch", [2, NT, 128, 512], BF16, kind="Internal").ap()
    dbg = None
    if stop_after is not None:
        dbg = nc.dram_tensor("dbg", [128, 20 * 1024], F32, kind="ExternalOutput").ap()

    with ExitStack() as es:
        P = Prog(nc, es)
        ARENA = 204 * KB
        AR = es.enter_context(nc.sbuf_tensor("arena", [128, ARENA // 2], BF16))
        PS = es.enter_context(nc.psum_tensor("ps", [128, 8, 512], F32))
        ident = es.enter_context(nc.sbuf_tensor("ident_sb", [128, 128], BF16))
        ones = es.enter_context(nc.sbuf_tensor("ones", [128, 128], BF16))
        colp = es.enter_context(nc.sbuf_tensor("colp_sb", [128, NCOLP], F32))
        stats = es.enter_context(nc.sbuf_tensor("stats", [128, 136], F32))
        junk3_t = es.enter_context(nc.sbuf_tensor("junk3", [128, D], BF16))
        epsT = es.enter_context(nc.sbuf_tensor("epsT", [128, 1], F32))

        def view(name, off, shape, dtype):
            esz = 4 if dtype == F32 else 2
            n = _prod(shape[1:])
            assert off % 4 == 0 and off + n * esz <= ARENA, (name, off, n * esz)
            ap = AR[0:shape[0], off // 2: off // 2 + n * esz // 2]
            if dtype != BF16:
                ap = ap.bitcast(dtype)
            if len(shape) == 3:
                ap = ap.rearrange("p (a b) -> p a b", a=shape[1])
            elif len(shape) == 4:
                ap = ap.rearrange("p (a b c) -> p a b c", a=shape[1], b=shape[2])
            P.buf(name, off, n * esz)
            return ap

        def psb(b):
            return PS[:, b, :]

        def pspair(s):
            return PS[:, 2 * s:2 * s + 2, :].rearrange("p a b -> p (a b)")

        def psT(b, k):
            return PS[:, b, :].bitcast(BF16)[:, 0:k * 128].rearrange("p (k t) -> p k t", k=k)

        def pk(s):
            return [("ps", 2 * s), ("ps", 2 * s + 1)]

        def mm(o, lhsT, rhs, start, stop, reads, writes, signal=None):
            if signal is None:
                signal = stop
            return P.op("tensor", lambda e: e.matmul(o, lhsT, rhs, start=start, stop=stop),
                        reads=reads, writes=writes, signal=signal)

        def tr(o, in_, reads, writes, signal):
            return P.op("tensor", lambda e: e.transpose(o, in_, ident[:]),
                        reads=list(reads) + ["ident"], writes=writes, signal=signal)

        def act(o, in_, func, reads, writes, scale=1.0, bias=0.0, accum=None):
            if accum is None:
                return P.op("scalar", lambda e: e.activation(out=o, in_=in_, func=func, bias=bias, scale=scale),
                            reads=reads, writes=writes)
            return P.op("scalar", lambda e: e.activation(out=o, in_=in_, func=func, bias=bias, scale=scale,
                                                         accum_out=accum), reads=reads, writes=writes)

        def ts(eng, o, in0, s1, s2, op0, op1, reads, writes):
            if s2 is None:
                return P.op(eng, lambda e: e.tensor_scalar(out=o, in0=in0, scalar1=s1, scalar2=None, op0=op0),
                            reads=reads, writes=writes)
            return P.op(eng, lambda e: e.tensor_scalar(out=o, in0=in0, scalar1=s1, scalar2=s2, op0=op0, op1=op1),
                        reads=reads, writes=writes)

        def stt(eng, o, in0, sc, in1, op0, op1, reads, writes):
            return P.op(eng, lambda e: e.scalar_tensor_tensor(out=o, in0=in0, scalar=sc, in1=in1, op0=op0, op1=op1),
                        reads=reads, writes=writes)

        def tt(eng, o, in0, in1, op, reads, writes):
            return P.op(eng, lambda e: e.tensor_tensor(out=o, in0=in0, in1=in1, op=op), reads=reads, writes=writes)

        def cp(eng, o, in_, reads, writes):
            if eng == "scalar":
                return P.op(eng, lambda e: e.copy(out=o, in_=in_), reads=reads, writes=writes)
            return P.op(eng, lambda e: e.tensor_copy(out=o, in_=in_), reads=reads, writes=writes)

        def rsqrt_col(col, scale, key):
            act(col, col, AF.Sqrt, reads=[key, "epsT"], writes=[key], scale=scale, bias=epsT[:, 0:1])
            P.op("vector", lambda e: e.reciprocal(out=col, in_=col), reads=[key], writes=[key])

        O_XNT = 0
        O_YAT = 32 * KB
        O_ZOT = 48 * KB
        O_A = 64 * KB
        O_ZTOK = 80 * KB
        O_B = 96 * KB
        O_Y = 112 * KB
        O_TAB = 144 * KB
        O_TMP = 160 * KB
        O_ROW = 184 * KB
        O_END = 196 * KB

        xnT = view("xnT", O_XNT, [128, 8, L], BF16)

        P.dma("sync", lambda e: e.dma_start(out=ident[:], in_=ident_d), writes=["ident"])
        P.dma("sync", lambda e: e.dma_start(out=colp[:], in_=colp_d), writes=["colp"])
        P.op("vector", lambda e: e.memset(stats[:], 0.0), writes=["stats"])
        P.op("vector", lambda e: e.memset(epsT[:], EPS), writes=["epsT"])
        P.op("vector", lambda e: e.memset(ones[:], 1.0), writes=["ones"])
        rowP1 = view("rowP1", O_ROW, [128, 3072], F32)
        P.dma("sync", lambda e: e.dma_start(out=rowP1, in_=rowp_d[:, 0:3072].partition_broadcast(128)),
              writes=["rowP1"])
        gpmRow = rowP1[:, 0:1024]
        gainRow = rowP1[:, 1024:1536]
        skipRow = rowP1[:, 1536:2560]
        deltaRow = rowP1[:, 2560:3072]

        w_ff1_v0 = w_ff1.rearrange("(k p) c -> p k c", p=128)

        def convert_wff1(fps, after=None):
            for fp in fps:
                P.dma("gpsimd", lambda e, fp=fp: e.dma_start(
                    out=wff1s_d[fp].rearrange("p (k c) -> p k c", k=8), in_=w_ff1_v0[:, :, fp * 256:(fp + 1) * 256]),
                    writes=[("wff1s", fp)], extra=([after] if after is not None else []))

        S_SS1 = 0
        S_MA_SUM, S_MA_NM, S_MA_VAR = 32, 48, 64
        S_M, S_H, S_F = 80, 96, 112
        S_FB1, S_FB2 = 128, 129
        stc = lambda c: stats[:, c:c + 1]
        stk = lambda c: ("st", c)
        for c in range(136):
            P.last_w[("st", c)] = P.last_w["stats"]

        O_S1 = 170 * KB
        xt_v = [view("xt%d" % b, O_ZTOK + b * 4 * KB, [128, D], F32) for b in range(4)]
        xs_v = [view("xs%d" % b, O_S1 + b * 2 * KB, [128, D], BF16) for b in range(2)]
        junk1 = junk3_t[:]

        def s1_a1(i):
            b = i % 4
            P.dma("sync", lambda e, i=i, b=b: e.dma_start(out=xt_v[b], in_=x[i * 128:(i + 1) * 128, :]),
                  writes=["xt%d" % b])
            col = stc(S_SS1 + i)
            act(junk1, xt_v[b], AF.Square, reads=["xt%d" % b], writes=[stk(S_SS1 + i), "junk3"], accum=col)
            rsqrt_col(col, 1.0 / D, stk(S_SS1 + i))

        def s1_a2(i):
            b = i % 4
            stt("vector", xs_v[i % 2], xt_v[b], stc(S_SS1 + i), gpmRow, ALU.mult, ALU.mult,
                reads=["xt%d" % b, stk(S_SS1 + i), "rowP1"], writes=["xs%d" % (i % 2)])

        def s1_b(i):
            b = i % 2
            for k in range(8):
                tr(psT(6, 8)[:, k, :], xs_v[b][:, k * 128:(k + 1) * 128], reads=["xs%d" % b], writes=[("ps", 6)],
                   signal=(k == 7))
            cp("scalar", xnT[:, :, i * 128:(i + 1) * 128], psT(6, 8), reads=[("ps", 6)], writes=[("xnT", i)])

        def s1_gen():
            s1_a1(0)
            s1_a1(1)
            s1_a2(0)
            for i in range(NT):
                if i + 2 < NT:
                    s1_a1(i + 2)
                if i + 1 < NT:
                    s1_a2(i + 1)
                s1_b(i)
                yield

        xnT_keys = lambda t0, t1: [("xnT", t) for t in range(t0, t1)]
        def maybe_stop(stage, items):
            if stop_after != stage:
                return False
            off = 0
            toks = []
            for (ap, keys) in items:
                n = ap.shape[1]
                for c0 in range(0, n, 2048):
                    c1 = min(n, c0 + 2048)
                    toks.append(P.dma("gpsimd", lambda e, ap=ap, off=off, c0=c0, c1=c1: e.dma_start(
                        out=dbg[0:ap.shape[0], off + c0:off + c1], in_=ap[:, c0:c1]), reads=keys))
                off += n
            for t in toks:
                P.wait("gpsimd", t)
            P.emit()
            return True

        flat = lambda ap: ap.rearrange("p a b -> p (a b)")

        w_in_v = w_in.rearrange("(k p) c -> p k c", p=128)

        ring = {"i": 0}

        def next_pair():
            s = ring["i"] % 3
            ring["i"] += 1
            return s

        wb_pref = {}

        def hy_prefetch(tag, col0, wb_off):
            wb = view("wb_" + tag, wb_off, [128, 8, 512], BF16)
            for hf in range(2):
                P.dma("gpsimd", lambda e, hf=hf: e.dma_start(out=wb[:, :, hf * 256:(hf + 1) * 256],
                                                             in_=w_in_v[:, :, col0 + hf * 256:col0 + (hf + 1) * 256]),
                      writes=[("wb_" + tag, hf)])
            wb_pref[tag] = wb
            return wb

        def hy_group(tag, col0, dest, destname, tmp_off, wb_off=None, after_wb=None, ob_off=None):
            if wb_off is None:
                wb_off = tmp_off
                tmp_off = tmp_off + 8 * KB
            if ob_off is None:
                ob_off = tmp_off + 12 * KB
            if tag in wb_pref:
                wb = wb_pref[tag]
            else:
                wb = hy_prefetch(tag, col0, wb_off)
            pcv = [view("pc_%s%d" % (tag, b), tmp_off + b * 4 * KB, [128, 1024], F32) for b in range(3)]
            obv = [view("ob_%s%d" % (tag, b), ob_off + b * 2 * KB, [128, 1024], BF16) for b in range(3)]
            if after_wb is not None:
                after_wb()
            ekey = ("ps", 7)

            def edges():
                for cb in range(4):
                    for k in range(8):
                        mm(PS[:, 7, 2 * cb:2 * cb + 2], wb[:, k, cb * 128:(cb + 1) * 128], xnT[:, k, 1023:1025],
                           k == 0, k == 7, reads=[("wb_" + tag, cb // 2), ("xnT", 7), ("xnT", 8)], writes=[ekey],
                           signal=(k == 7 and cb == 3))

            pairs_of = {}

            def main_mm(unit):
                half, cb = unit // 4, unit % 4
                wkey = ("wb_" + tag, cb // 2)
                s = next_pair()
                pairs_of[unit] = s
                for tg2 in range(2):
                    t0 = half * 1024 + tg2 * 512
                    for k in range(8):
                        mm(PS[:, 2 * s + tg2, :], wb[:, k, cb * 128:(cb + 1) * 128], xnT[:, k, t0:t0 + 512],
                           k == 0, k == 7, reads=[wkey] + xnT_keys(t0 // 128, t0 // 128 + 4),
                           writes=[("ps", 2 * s + tg2)])

            def main_ew(unit):
                half, cb = unit // 4, unit % 4
                gcb = (col0 - 1024) // 128 + cb
                w0c = colp[:, C_W0 + gcb:C_W0 + gcb + 1]
                w1c = colp[:, C_W1 + gcb:C_W1 + gcb + 1]
                w2c = colp[:, C_W2 + gcb:C_W2 + gcb + 1]
                bc = colp[:, C_CB + gcb:C_CB + gcb + 1]
                edge = PS[:, 7, 2 * cb:2 * cb + 2]
                s = pairs_of[unit]
                pkey = pk(s)
                p = pspair(s)
                b = unit % 3
                pc, ob = pcv[b], obv[b]
                pck, obk = "pc_%s%d" % (tag, b), "ob_%s%d" % (tag, b)
                act(pc, p, AF.Identity, reads=pkey + ["colp"], writes=[pck], scale=w1c, bias=bc)
                if half == 1:
                    stt("vector", pc[:, 0:1], edge[:, 0:1], w0c, pc[:, 0:1], ALU.mult, ALU.add,
                        reads=[ekey, pck, "colp"], writes=[pck])
                stt("vector", pc[:, 1:1024], p[:, 0:1023], w0c, pc[:, 1:1024], ALU.mult, ALU.add,
                    reads=pkey + [pck, "colp"], writes=[pck])
                stt("vector", ob[:, 0:1023], p[:, 1:1024], w2c, pc[:, 0:1023], ALU.mult, ALU.add,
                    reads=pkey + [pck, "colp"], writes=[obk])
                if half == 0:
                    stt("vector", ob[:, 1023:1024], edge[:, 1:2], w2c, pc[:, 1023:1024], ALU.mult, ALU.add,
                        reads=[ekey, pck, "colp"], writes=[obk])
                else:
                    cp("vector", ob[:, 1023:1024], pc[:, 1023:1024], reads=[pck], writes=[obk])

            def trans(unit):
                half, cb = unit // 4, unit % 4
                b = unit % 3
                ob = obv[b]
                obk = "ob_%s%d" % (tag, b)
                for r in range(2):
                    for jj in range(4):
                        st_ = 256 * jj + r
                        tr(psT(6, 8)[:, r * 4 + jj, :], ob[:, st_:st_ + 255:2], reads=[obk], writes=[("ps", 6)],
                           signal=(r == 1 and jj == 3))
                cp("scalar", dest[:, :, 4 * half:4 * half + 4, cb * 128:(cb + 1) * 128],
                   psT(6, 8).rearrange("p (r j) c -> p r j c", r=2),
                   reads=[("ps", 6)], writes=[(destname, r * 8 + 4 * half + jj) for r in range(2) for jj in range(4)])

            for unit in range(9):
                if unit < 8:
                    main_mm(unit)
                    if unit == 0:
                        edges()
                    main_ew(unit)
                if unit >= 1:
                    trans(unit - 1)

        normRow = view("normRow", O_END, [128, 2, 512], F32)

        def sin_reduced(dst, a, q, e2, keya, keyd, kt):
            act(q, a, AF.Sin, reads=[keya], writes=[kt + "q"], scale=0.25)
            act(e2, a, AF.Sin, reads=[keya], writes=[kt + "e"], scale=0.125)
            tt("vector", e2, e2, e2, ALU.mult, reads=[kt + "e"], writes=[kt + "e"])
            ts("vector", e2, e2, -2.0, 1.0, ALU.mult, ALU.add, reads=[kt + "e"], writes=[kt + "e"])
            tt("vector", e2, e2, q, ALU.mult, reads=[kt + "e", kt + "q"], writes=[kt + "e"])
            tt("vector", q, q, q, ALU.mult, reads=[kt + "q"], writes=[kt + "q"])
            ts("vector", q, q, -2.0, 1.0, ALU.mult, ALU.add, reads=[kt + "q"], writes=[kt + "q"])
            stt("vector", dst, e2, 4.0, q, ALU.mult, ALU.mult, reads=[kt + "e", kt + "q"], writes=[keyd])

        def layer_chain(ch, t, featsT, fw1, fw2, h2T, tmpsets, fb1, fb2):
            a_t, s1_t, q_t, e_t = tmpsets[t]
            ka, ks1, kq = "fa%d" % t, "fs%d" % t, "fsr%d" % t
            bank = 2 * t
            mm(PS[0:64, bank, :], fw1, featsT[:, ch * 512:(ch + 1) * 512], True, True,
               reads=["fw1", "featsT"], writes=[("ps", bank)])
            yield
            ts("vector", a_t, PS[0:64, bank, :], colp[0:64, C_F1:C_F1 + 1], fb1, ALU.mult, ALU.add,
               reads=[("ps", bank), "colp", stk(S_FB1)], writes=[ka])
            yield
            for _ in sin_reduced_g(s1_t, a_t, q_t, e_t, ka, ks1, kq):
                yield
            mm(PS[0:64, bank + 1, :], fw2, s1_t, True, True, reads=["fw2", ks1], writes=[("ps", bank + 1)])
            yield
            ts("vector", a_t, PS[0:64, bank + 1, :], colp[0:64, C_F2:C_F2 + 1], fb2, ALU.mult, ALU.add,
               reads=[("ps", bank + 1), "colp", stk(S_FB2)], writes=[ka])
            yield
            for _ in sin_reduced_g(h2T[:, ch * 512:(ch + 1) * 512], a_t, q_t, e_t, ka, ("h2T", ch), kq):
                yield

        def sin_reduced_g(dst, a, q, e2, keya, keyd, kt):
            act(q, a, AF.Sin, reads=[keya], writes=[kt + "q"], scale=0.25)
            act(e2, a, AF.Sin, reads=[keya], writes=[kt + "e"], scale=0.125)
            yield
            tt("vector", e2, e2, e2, ALU.mult, reads=[kt + "e"], writes=[kt + "e"])
            ts("vector", e2, e2, -2.0, 1.0, ALU.mult, ALU.add, reads=[kt + "e"], writes=[kt + "e"])
            tt("vector", e2, e2, q, ALU.mult, reads=[kt + "e", kt + "q"], writes=[kt + "e"])
            yield
            tt("vector", q, q, q, ALU.mult, reads=[kt + "q"], writes=[kt + "q"])
            ts("vector", q, q, -2.0, 1.0, ALU.mult, ALU.add, reads=[kt + "q"], writes=[kt + "q"])
            stt("vector", dst, e2, 4.0, q, ALU.mult, ALU.mult, reads=[kt + "e", kt + "q"], writes=[keyd])
            yield

        def filter_gen_all(hs1, hd1):
            featsT = view("featsT", 32 * KB, [33, L], F32)
            h2T = view("h2T", 40 * KB, [64, L], F32)
            fw3 = view("fw3", 48 * KB, [64, 2048], F32)
            w3sd = view("w3sd", 56 * KB, [64, 2048], F32)
            fw1 = view("fw1", O_END + 4 * KB, [33, 64], F32)
            fw2 = view("fw2", O_END + 4 * KB + 256, [64, 64], F32)
            tmpsets = []
            for t in range(2):
                base = O_Y + t * 8 * KB
                tmpsets.append((view("fa%d" % t, base, [64, 512], F32), view("fs%d" % t, base + 2 * KB, [64, 512], F32),
                                view("fsr%dq" % t, base + 4 * KB, [64, 512], F32),
                                view("fsr%de" % t, base + 6 * KB, [64, 512], F32)))
            Ev = [view("fE%d" % b, O_Y + 16 * KB + b * 2 * KB, [128, 512], F32) for b in range(2)]
            sqv = [view("fsq%d" % b, O_Y + 20 * KB + b * 4 * KB, [128, 4, 512], BF16) for b in range(2)]
            stg = [view("fstg%d" % b, O_Y + 28 * KB + b * 2 * KB, [128, 2, 512], BF16) for b in range(2)]
            P.dma("sync", lambda e: e.dma_start(out=featsT, in_=featsT_d), writes=["featsT"])
            P.dma("sync", lambda e: e.dma_start(out=fw3, in_=fw3_d), writes=["fw3"])
            P.dma("sync", lambda e: e.dma_start(out=fw1, in_=fw1_d), writes=["fw1"])
            P.dma("sync", lambda e: e.dma_start(out=fw2, in_=fw2_d), writes=["fw2"])
            fb1 = stats[0:64, S_FB1:S_FB1 + 1]
            fb2 = stats[0:64, S_FB2:S_FB2 + 1]
            tt("vector", fb1, colp[0:64, C_F1:C_F1 + 1], colp[0:64, C_B1:C_B1 + 1], ALU.mult,
               reads=["colp"], writes=[stk(S_FB1)])
            tt("vector", fb2, colp[0:64, C_F2:C_F2 + 1], colp[0:64, C_B2:C_B2 + 1], ALU.mult,
               reads=["colp"], writes=[stk(S_FB2)])
            for o in range(2):
                f_ = fw3[:, o * 1024:o * 1024 + 512]
                b_ = fw3[:, o * 1024 + 512:o * 1024 + 1024]
                tt("vector", w3sd[:, (2 * o) * 512:(2 * o + 1) * 512], f_, b_, ALU.add, reads=["fw3"],
                   writes=[("w3sd", 2 * o)])
                tt("vector", w3sd[:, (2 * o + 1) * 512:(2 * o + 2) * 512], f_, b_, ALU.subtract, reads=["fw3"],
                   writes=[("w3sd", 2 * o + 1)])
            yield
            for grp in ((0, 1), (2, 3)):
                gens = [layer_chain(ch, ch % 2, featsT, fw1, fw2, h2T, tmpsets, fb1, fb2) for ch in grp]
                alive = list(gens)
                while alive:
                    for g in list(alive):
                        try:
                            next(g)
                        except StopIteration:
                            alive.remove(g)
                    yield
            dsts = [(hs1, "hs1"), (hd1, "hd1")]
            for nt in range(NT):
                b = nt % 2
                r_, j_ = nt // 8, nt % 8
                st_ = 256 * j_ + r_
                for j in range(4):
                    mm(PS[:, j, :], h2T[:, st_:st_ + 255:2], w3sd[:, j * 512:(j + 1) * 512], True, True,
                       reads=[("h2T", j_ // 2), ("w3sd", j)], writes=[("ps", j)])
                Ek = "fE%d" % b
                act(Ev[b], deltaRow, AF.Exp, reads=["rowP1", "colp"], writes=[Ek],
                    scale=colp[:, C_TNEG + nt:C_TNEG + nt + 1])
                outs = [hs1[:, r_, j_, :], hd1[:, r_, j_, :], stg[b][:, 0, :], stg[b][:, 1, :]]
                okeys = [("hs1", nt), ("hd1", nt), ("fstg%d" % b, 0), ("fstg%d" % b, 1)]
                for j in range(4):
                    stt("vector", outs[j], Ev[b], 0.05, PS[:, j, :], ALU.add, ALU.mult,
                        reads=[Ek, ("ps", j)], writes=[okeys[j]])
                for j in range(4):
                    act(sqv[b][:, j, :], outs[j], AF.Square, reads=[okeys[j]], writes=[("fsq%d" % b, j)])
                for j in range(4):
                    acc = 4 + j // 2
                    mm(psb(acc), ones[:], sqv[b][:, j, :], nt == 0 and j % 2 == 0, nt == NT - 1 and j % 2 == 1,
                       reads=["ones", ("fsq%d" % b, j)], writes=[("ps", acc)], signal=(j % 2 == 1))
                for r in range(2):
                    P.dma("sync", lambda e, r=r, nt=nt, b=b: e.dma_start(out=hsd2_d[r, nt], in_=stg[b][:, r, :]),
                          reads=[("fstg%d" % b, r)], writes=[("hsd2", r, nt)])
                yield
            for o in range(2):
                act(normRow[:, o, :], psb(4 + o), AF.Sqrt, reads=[("ps", 4 + o), "epsT"], writes=[("normRow", o)],
                    scale=0.5, bias=epsT[:, 0:1])
                P.op("vector", lambda e, o=o: e.reciprocal(out=normRow[:, o, :], in_=normRow[:, o, :]),
                     reads=[("normRow", o)], writes=[("normRow", o)])
            yield

        def dft_kpass(o, hs, hsname, hd, hdname, Kb, kname, tabs, tabnames, T, Tn, hook=None):
            for fb in range(8):
                tb = tabs[fb % 2]
                tk = tabnames[fb % 2]
                P.dma("sync", lambda e, fb=fb, tb=tb: e.dma_start(out=tb, in_=TF_d[fb]), writes=[tk])
                if hook is not None:
                    hook(fb, P.last_w.get((tabnames[(fb + 1) % 2])))
                b0 = 4 * (fb % 2)
                srcs = [(hs, hsname, 0), (hd, hdname, 0), (hs, hsname, 1), (hd, hdname, 1)]
                for mh in range(8):
                    for q in range(4):
                        src, sname, r = srcs[q]
                        mm(PS[:, b0 + q, :], tb[:, q, mh, :], src[:, r, mh, :], mh == 0, mh == 7,
                           reads=[tk, (sname, r * 8 + mh)], writes=[("ps", b0 + q)], signal=(mh == 7 and q == 3))
                nr = normRow[:, o, :]
                sr = skipRow[:, o * 512:(o + 1) * 512]
                kE = [("ps", b0 + q) for q in range(4)]
                cp("scalar", T[0], PS[:, b0 + 2, :], reads=[kE[2]], writes=[Tn[0]])
                cp("scalar", T[1], PS[:, b0 + 3, :], reads=[kE[3]], writes=[Tn[1]])
                tt("vector", T[2], PS[:, b0, :], T[0], ALU.add, reads=[kE[0], Tn[0]], writes=[Tn[2]])
                tt("vector", T[3], PS[:, b0, :], T[0], ALU.subtract, reads=[kE[0], Tn[0]], writes=[Tn[3]])
                tt("vector", T[4], PS[:, b0 + 1, :], T[1], ALU.add, reads=[kE[1], Tn[1]], writes=[Tn[4]])
                tt("vector", T[5], PS[:, b0 + 1, :], T[1], ALU.subtract, reads=[kE[1], Tn[1]], writes=[Tn[5]])
                tt("vector", T[2], T[2], nr, ALU.mult, reads=[Tn[2], ("normRow", o)], writes=[Tn[2]])
                tt("gpsimd", Kb[0][:, fb, :], T[2], sr, ALU.add, reads=[Tn[2], "rowP1"], writes=[(kname + "0", fb)])
                tt("vector", T[3], T[3], nr, ALU.mult, reads=[Tn[3], ("normRow", o)], writes=[Tn[3]])
                tt("gpsimd", Kb[2][:, fb, :], T[3], sr, ALU.add, reads=[Tn[3], "rowP1"], writes=[(kname + "2", fb)])
                tt("vector", Kb[1][:, fb, :], T[4], nr, ALU.mult, reads=[Tn[4], ("normRow", o)], writes=[(kname + "1", fb)])
                tt("vector", Kb[3][:, fb, :], T[5], nr, ALU.mult, reads=[Tn[5], ("normRow", o)], writes=[(kname + "3", fb)])

        def dft_zpass(o, z, zname, Kb, kname, PMs, pmnames, tabs, tabnames, T, Tn, hook=None):
            for fb in range(8):
                tb = tabs[fb % 2]
                tk = tabnames[fb % 2]
                P.dma("sync", lambda e, fb=fb, tb=tb: e.dma_start(out=tb, in_=TF_d[fb]), writes=[tk])
                if hook is not None:
                    hook(fb, P.last_w.get((tabnames[(fb + 1) % 2])))
                b0 = 4 * (fb % 2)
                for mh in range(8):
                    for q in range(4):
                        r = q // 2
                        mm(PS[:, b0 + q, :], tb[:, q, mh, :], z[:, r, mh, :], mh == 0, mh == 7,
                           reads=[tk, (zname, r * 8 + mh)], writes=[("ps", b0 + q)], signal=(mh == 7 and q == 3))
                kE = [("ps", b0 + q) for q in range(4)]
                Skr, Ski, Dkr, Dki = [Kb[q][:, fb, :] for q in range(4)]
                nSkr, nSki, nDkr, nDki = [(kname + str(q), fb) for q in range(4)]
                cp("scalar", T[0], PS[:, b0 + 2, :], reads=[kE[2]], writes=[Tn[0]])
                cp("scalar", T[1], PS[:, b0 + 3, :], reads=[kE[3]], writes=[Tn[1]])
                tt("vector", T[2], PS[:, b0, :], T[0], ALU.add, reads=[kE[0], Tn[0]], writes=[Tn[2]])
                tt("vector", T[3], PS[:, b0, :], T[0], ALU.subtract, reads=[kE[0], Tn[0]], writes=[Tn[3]])
                tt("vector", T[4], PS[:, b0 + 1, :], T[1], ALU.add, reads=[kE[1], Tn[1]], writes=[Tn[4]])
                tt("vector", T[5], PS[:, b0 + 1, :], T[1], ALU.subtract, reads=[kE[1], Tn[1]], writes=[Tn[5]])
                tt("vector", T[6], T[2], Skr, ALU.mult, reads=[Tn[2], nSkr], writes=[Tn[6]])
                tt("vector", T[7], T[4], Ski, ALU.mult, reads=[Tn[4], nSki], writes=[Tn[7]])
                tt("gpsimd", T[6], T[6], T[7], ALU.subtract, reads=[Tn[6], Tn[7]], writes=[Tn[6]])
                tt("vector", T[10], T[2], Ski, ALU.mult, reads=[Tn[2], nSki], writes=[Tn[10]])
                tt("vector", T[11], T[4], Skr, ALU.mult, reads=[Tn[4], nSkr], writes=[Tn[11]])
                tt("gpsimd", T[7], T[10], T[11], ALU.add, reads=[Tn[10], Tn[11]], writes=[Tn[7]])
                tt("vector", T[8], T[3], Dkr, ALU.mult, reads=[Tn[3], nDkr], writes=[Tn[8]])
                tt("vector", T[9], T[5], Dki, ALU.mult, reads=[Tn[5], nDki], writes=[Tn[9]])
                tt("gpsimd", T[8], T[8], T[9], ALU.subtract, reads=[Tn[8], Tn[9]], writes=[Tn[8]])
                tt("vector", T[10], T[3], Dki, ALU.mult, reads=[Tn[3], nDki], writes=[Tn[10]])
                tt("vector", T[11], T[5], Dkr, ALU.mult, reads=[Tn[5], nDkr], writes=[Tn[11]])
                tt("gpsimd", T[9], T[10], T[11], ALU.add, reads=[Tn[10], Tn[11]], writes=[Tn[9]])
                tt("vector", PMs[0][:, fb, :], T[6], T[8], ALU.add, reads=[Tn[6], Tn[8]], writes=[(pmnames[0], fb)])
                tt("gpsimd", PMs[2][:, fb, :], T[6], T[8], ALU.subtract, reads=[Tn[6], Tn[8]], writes=[(pmnames[2], fb)])
                tt("vector", PMs[1][:, fb, :], T[7], T[9], ALU.add, reads=[Tn[7], Tn[9]], writes=[(pmnames[1], fb)])
                tt("gpsimd", PMs[3][:, fb, :], T[7], T[9], ALU.subtract, reads=[Tn[7], Tn[9]], writes=[(pmnames[3], fb)])

        def dft_inverse(PMs, pmnames, mul, mulname, dst, dstname, tabs, tabnames, post=None, banks=(0, 1, 2, 3), hook=None):
            nbk = 0
            prev = None
            for r in range(2):
                for j in range(8):
                    tb = tabs[nbk % 2]
                    tk = tabnames[nbk % 2]
                    P.dma("sync", lambda e, r=r, j=j, tb=tb: e.dma_start(out=tb, in_=TI_d[r, j]), writes=[tk])
                    if hook is not None:
                        hook(nbk)
                    bank = banks[nbk % len(banks)]
                    yk = ("ps", bank)
                    for fb in range(8):
                        mm(psb(bank), tb[:, 0, fb, :], PMs[2 * r][:, fb, :], fb == 0, False,
                           reads=[tk, (pmnames[2 * r], fb)], writes=[yk], signal=False)
                        mm(psb(bank), tb[:, 1, fb, :], PMs[2 * r + 1][:, fb, :], False, fb == 7,
                           reads=[tk, (pmnames[2 * r + 1], fb)], writes=[yk], signal=(fb == 7))
                    stt("vector", dst[:, r, j, :], psb(bank), 1.0 / 2048.0, mul[:, r, j, :], ALU.mult, ALU.mult,
                        reads=[yk, (mulname, r * 8 + j)], writes=[(dstname, r * 8 + j)])
                    if post is not None and prev is not None:
                        post(*prev)
                    prev = (r, j)
                    nbk += 1
                    yield
            if post is not None:
                post(*prev)
                yield

        T4 = [128, 2, 8, 512]
        flat4 = lambda ap: ap.rearrange("p r j c -> p (r j c)")
        keys16 = lambda nm: [(nm, t) for t in range(16)]
        hs1 = view("hs1", O_A, T4, BF16)
        hd1 = view("hd1", O_B, T4, BF16)
        lockstep([s1_gen(), filter_gen_all(hs1, hd1)])
        if maybe_stop("xn", [(flat(xnT), xnT_keys(0, 16))]):
            return nc
        if maybe_stop("filt1", [(flat4(hs1), keys16("hs1")), (flat4(hd1), keys16("hd1")),
                                (normRow[:, 0, :], [("normRow", 0)])]):
            return nc
        z_tok = view("z_tok", O_ZTOK, T4, BF16)
        hy_group("v", 2048, z_tok, "z_tok", O_TAB)
        if maybe_stop("hyv", [(flat4(z_tok), keys16("z_tok"))]):
            return nc

        def quad_views(prefix, offs):
            vs = [view("%s%d" % (prefix, w), offs[w], [128, 8, 512], BF16) for w in range(4)]
            return vs

        class NameQ:
            pass

        K1 = quad_views("K1q", [O_Y + w * 8 * KB for w in range(4)])
        tabsK1 = [view("tabK1_%d" % b, O_YAT + b * 8 * KB, [128, 4, 8, 128], BF16) for b in range(2)]
        TA = [view("pwA%d" % j, O_TMP + j * 2 * KB, [128, 512], F32) for j in range(12)]
        TAn = ["pwA%d" % j for j in range(12)]
        dft_kpass(0, hs1, "hs1", hd1, "hd1", K1, "K1q", tabsK1, ["tabK1_0", "tabK1_1"], TA, TAn,
                  hook=lambda fb, tok: convert_wff1([fb], after=tok))
        PM1 = quad_views("PMa", [O_A, O_A + 8 * KB, O_B, O_B + 8 * KB])
        PM1n = ["PMa%d" % w for w in range(4)]
        tabsZ1 = [view("tabZ1_%d" % b, O_YAT + b * 8 * KB, [128, 4, 8, 128], BF16) for b in range(2)]
        dft_zpass(0, z_tok, "z_tok", K1, "K1q", PM1, PM1n, tabsZ1, ["tabZ1_0", "tabZ1_1"], TA, TAn,
                  hook=lambda fb, tok: (hy_prefetch("x1", 1024, O_ZOT) if fb == 0 else None, convert_wff1([8 + fb], after=tok)))
        if maybe_stop("fwd1", [(PM1[w].rearrange("p a b -> p (a b)"), [(PM1n[w], f) for f in range(8)]) for w in range(4)]):
            return nc
        x1_tok = view("x1_tok", O_TAB, T4, BF16)
        hy_group("x1", 1024, x1_tok, "x1_tok", O_ZTOK, wb_off=O_ZOT, ob_off=O_ZOT + 8 * KB)
        hs2 = view("hs2", O_ZOT, T4, BF16)
        hd2 = view("hd2", O_Y, T4, BF16)

        def load_filt2(r, dst_, nm_):
            for q4 in range(4):
                P.dma("sync", lambda e, r=r, q4=q4, dst_=dst_: e.dma_start(
                    out=dst_.rearrange("p r j c -> p (r j) c")[:, q4 * 4:(q4 + 1) * 4, :],
                    in_=hsd2_d[r, q4 * 4:(q4 + 1) * 4].rearrange("t p c -> p t c")),
                    reads=[("hsd2", r, t) for t in range(q4 * 4, q4 * 4 + 4)],
                    writes=[(nm_, t) for t in range(q4 * 4, q4 * 4 + 4)])

        u_tok = view("u_tok", O_ZTOK, T4, BF16)
        tabsB = [view("tabB%d" % b, O_YAT + b * 4 * KB, [128, 2, 8, 128], BF16) for b in range(2)]

        def inv1_hook(nb):
            if nb == 3:
                load_filt2(0, hs2, "hs2")
            if nb == 5:
                load_filt2(1, hd2, "hd2")

        lockstep([dft_inverse(PM1, PM1n, x1_tok, "x1_tok", u_tok, "u_tok", tabsB, ["tabB0", "tabB1"], hook=inv1_hook)])
        if maybe_stop("inv1", [(flat4(u_tok), keys16("u_tok"))]):
            return nc
        K2 = quad_views("K2q", [O_A, O_A + 8 * KB, O_B, O_B + 8 * KB])
        tabsK2 = [view("tabK2_%d" % b, O_YAT + b * 8 * KB, [128, 4, 8, 128], BF16) for b in range(2)]
        TC = [view("pwC%d" % j, O_TMP + j * 2 * KB, [128, 512], F32) for j in range(12)]
        TCn = ["pwC%d" % j for j in range(12)]
        dft_kpass(1, hs2, "hs2", hd2, "hd2", K2, "K2q", tabsK2, ["tabK2_0", "tabK2_1"], TC, TCn)
        PM2 = quad_views("PMb", [O_Y + w * 8 * KB for w in range(4)])
        PM2n = ["PMb%d" % w for w in range(4)]
        tabsZ2 = [view("tabZ2_%d" % b, O_YAT + b * 8 * KB, [128, 4, 8, 128], BF16) for b in range(2)]
        dft_zpass(1, u_tok, "u_tok", K2, "K2q", PM2, PM2n, tabsZ2, ["tabZ2_0", "tabZ2_1"], TC, TCn,
                  hook=lambda fb, tok: (hy_prefetch("x2", 1536, O_ZOT) if fb == 0 else None))
        x2_tok = view("x2_tok", O_TAB, T4, BF16)
        hy_group("x2", 1536, x2_tok, "x2_tok", O_ZTOK, wb_off=O_ZOT, ob_off=O_ZOT + 8 * KB)
        zo_tok = view("zo_tok", O_ZTOK, T4, BF16)
        zoT = view("zoT", O_ZOT, [128, 4, L], BF16)
        tabsD = [view("tabD%d" % b, O_TMP + b * 4 * KB, [128, 2, 8, 128], BF16) for b in range(2)]

        def zo_post(r, j):
            t = r * 8 + j
            for cb in range(4):
                tr(psT(6, 4)[:, cb, :], zo_tok[:, r, j, cb * 128:(cb + 1) * 128], reads=[("zo_tok", t)],
                   writes=[("ps", 6)], signal=(cb == 3))
            st_ = 256 * j + r
            cp("scalar", zoT[:, :, st_:st_ + 255:2], psT(6, 4), reads=[("ps", 6)],
               writes=[("zoT", 2 * j), ("zoT", 2 * j + 1)])

        inv2_gen = dft_inverse(PM2, PM2n, x2_tok, "x2_tok", zo_tok, "zo_tok", tabsD, ["tabD0", "tabD1"], post=zo_post,
                               banks=(0,))

        y_aT = view("y_aT", O_YAT, [128, 4, L], BF16)
        wbU = view("wbU", O_B, [128, 8, 512], BF16)
        wbV = view("wbV", O_B + 8 * KB, [128, 8, 512], BF16)
        NB_A = 4
        guv = [view("gu%d" % b, O_A + b * 2 * KB, [128, 512], F32) for b in range(NB_A)]
        gvv = [view("gv%d" % b, O_A + 8 * KB + b * 2 * KB, [128, 512], F32) for b in range(2)]
        vnv = [view("vn%d" % b, O_A + 12 * KB + b * 2 * KB, [128, 512], F32) for b in range(2)]
        wsT = view("wsT", O_END + 4 * KB, [128, 8, 128], BF16)
        vnbv = [view("vnb%d" % b, O_TMP + 16 * KB + b * KB, [128, 512], BF16) for b in range(NB_A)]
        yav = [view("ya%d" % b, O_TMP + 20 * KB + b * KB, [128, 512], BF16) for b in range(3)]
        junkA = view("junkA", O_TMP + 23 * KB, [128, 512], BF16)
        ringA = {"i": 0}

        def next_pair_a():
            s_ = 1 + ringA["i"] % 2
            ringA["i"] += 1
            return s_

        def ma_a(i):
            b = i % NB_A
            b2 = i % 2
            s = next_pair_a()
            for k in range(8):
                mm(PS[:, 2 * s, :], xnT[:, k, i * 128:(i + 1) * 128], wbU[:, k, :], k == 0, k == 7,
                   reads=[("xnT", i), "wbU"], writes=[("ps", 2 * s)], signal=False)
                mm(PS[:, 2 * s + 1, :], xnT[:, k, i * 128:(i + 1) * 128], wbV[:, k, :], k == 0, k == 7,
                   reads=[("xnT", i), "wbV"], writes=[("ps", 2 * s + 1)], signal=(k == 7))
            guk, gvk, vnk, vnbk = "gu%d" % b, "gv%d" % b2, "vn%d" % b2, "vnb%d" % b
            act(guv[b], PS[:, 2 * s, :], AF.Gelu_apprx_tanh, reads=[("ps", 2 * s), ("ps", 2 * s + 1)], writes=[guk])
            act(gvv[b2], PS[:, 2 * s + 1, :], AF.Gelu_apprx_tanh, reads=[("ps", 2 * s), ("ps", 2 * s + 1)],
                writes=[gvk, stk(S_MA_SUM + i)], accum=stc(S_MA_SUM + i))
            ts("vector", stc(S_MA_NM + i), stc(S_MA_SUM + i), -1.0 / 512.0, None, ALU.mult, None,
               reads=[stk(S_MA_SUM + i)], writes=[stk(S_MA_NM + i)])
            act(junkA, gvv[b2], AF.Square, reads=[gvk, stk(S_MA_NM + i)], writes=["junkA", stk(S_MA_VAR + i)],
                bias=stc(S_MA_NM + i), accum=stc(S_MA_VAR + i))
            rsqrt_col(stc(S_MA_VAR + i), 1.0 / 512.0, stk(S_MA_VAR + i))
            ts("vector", vnv[b2], gvv[b2], stc(S_MA_NM + i), stc(S_MA_VAR + i), ALU.add, ALU.mult,
               reads=[gvk, stk(S_MA_NM + i), stk(S_MA_VAR + i)], writes=[vnk])
            tt("vector", vnbv[b], vnv[b2], gainRow, ALU.mult, reads=[vnk, "rowP1"], writes=[vnbk])

        def ma_b(i):
            b = i % NB_A
            by = i % 3
            guk, vnbk, yak = "gu%d" % b, "vnb%d" % b, "ya%d" % by
            for g in range(8):
                mm(PS[:, 7, g * 64:(g + 1) * 64], wsT[:, g, :], vnbv[b][:, g * 64:(g + 1) * 64], True, True,
                   reads=["wsT", vnbk], writes=[("ps", 7)], signal=(g == 7))
            for g in range(8):
                stt("vector", yav[by][:, g * 64:(g + 1) * 64], PS[:, 7, g * 64:(g + 1) * 64],
                    colp[:, C_BST + g:C_BST + g + 1], guv[b][:, g * 64:(g + 1) * 64], ALU.add, ALU.mult,
                    reads=[("ps", 7), guk, "colp"], writes=[yak])

        def ma_c(i):
            b = i % 3
            yak = "ya%d" % b
            for cb in range(4):
                tr(psT(1, 4)[:, cb, :], yav[b][:, cb * 128:(cb + 1) * 128], reads=[yak], writes=[("ps", 1)],
                   signal=(cb == 3))
            cp("scalar", y_aT[:, :, i * 128:(i + 1) * 128], psT(1, 4), reads=[("ps", 1)], writes=[("y_aT", i)])

        def mixa_gen():
            P.dma("gpsimd", lambda e: e.dma_start(out=wbU, in_=w_in_v[:, :, 0:512]), writes=["wbU"])
            P.dma("gpsimd", lambda e: e.dma_start(out=wbV, in_=w_in_v[:, :, 512:1024]), writes=["wbV"])
            P.dma("gpsimd", lambda e: e.dma_start(out=wsT, in_=wsT_d), writes=["wsT"])
            for step in range(NT + 3):
                if step < NT:
                    ma_a(step)
                if 0 <= step - 2 < NT:
                    ma_b(step - 2)
                if 0 <= step - 3 < NT:
                    ma_c(step - 3)
                yield

        lockstep([inv2_gen, mixa_gen()])
        if maybe_stop("zo", [(flat(zoT), [("zoT", t) for t in range(16)])]):
            return nc
        if maybe_stop("mixa", [(flat(y_aT), [("y_aT", t) for t in range(16)])]):
            return nc
        mergedT = view("mergedT", O_A, [128, 8, L], BF16)
        woa = view("woa", O_B, [128, 4, D], BF16)
        wob = view("wob", O_B + 8 * KB, [128, 4, D], BF16)
        wgv = [view("wg%d" % b, O_Y + b * 4 * KB, [128, 8, 2, 128], BF16) for b in range(2)]
        sgav = [view("sga%d" % b, O_Y + 8 * KB + b * 2 * KB, [128, 512], F32) for b in range(2)]
        sgbv = [view("sgb%d" % b, O_Y + 12 * KB + b * 2 * KB, [128, 512], F32) for b in range(2)]
        m1v = [view("m1_%d" % b, O_Y + 16 * KB + b * 2 * KB, [128, 512], F32) for b in range(2)]
        m2v = [view("m2_%d" % b, O_Y + 20 * KB + b * 2 * KB, [128, 512], F32) for b in range(2)]
        wo = view("wo", O_TAB, [128, 8, D], BF16)
        rowP3 = view("rowP3", O_ROW, [128, 3072], F32)
        P.dma("sync", lambda e: e.dma_start(out=rowP3, in_=rowp_d[:, 3072:6144].partition_broadcast(128)),
              writes=["rowP3"])
        GpmRow, GpreRow, GpfRow = rowP3[:, 0:1024], rowP3[:, 1024:2048], rowP3[:, 2048:3072]

        def load_wo():
            P.dma("gpsimd", lambda e: e.dma_start(out=wo, in_=w_o.rearrange("(k p) c -> p k c", p=128)), writes=["wo"])

        it = 0
        for j in range(8):
            wb_ = wgv[j % 2]
            wk = "wg%d" % (j % 2)
            for ab in range(2):
                c0 = 2560 + ab * 1024 + j * 128
                P.dma("gpsimd", lambda e, wb_=wb_, ab=ab, c0=c0: e.dma_start(out=wb_[:, :, ab, :],
                                                                             in_=w_in_v[:, :, c0:c0 + 128]),
                      writes=[(wk, ab)])
            if j == 0:
                P.dma("gpsimd", lambda e: e.dma_start(out=woa, in_=w_out_a.rearrange("(k p) c -> p k c", p=128)),
                      writes=["woa"])
                P.dma("gpsimd", lambda e: e.dma_start(out=wob, in_=w_out_b.rearrange("(k p) c -> p k c", p=128)),
                      writes=["wob"])
                load_wo()
            for tg in range(4):
                b0 = 4 * (it % 2)
                bb = it % 2
                it += 1
                t0 = tg * 512
                kya, kyb, kga, kgb = [("ps", b0 + q) for q in range(4)]
                for k in range(8):
                    mm(psb(b0 + 2), wb_[:, k, 0, :], xnT[:, k, t0:t0 + 512], k == 0, k == 7,
                       reads=[(wk, 0)] + xnT_keys(tg * 4, tg * 4 + 4), writes=[kga])
                for k in range(8):
                    mm(psb(b0 + 3), wb_[:, k, 1, :], xnT[:, k, t0:t0 + 512], k == 0, k == 7,
                       reads=[(wk, 1)] + xnT_keys(tg * 4, tg * 4 + 4), writes=[kgb])
                for cb in range(4):
                    mm(psb(b0), woa[:, cb, j * 128:(j + 1) * 128], y_aT[:, cb, t0:t0 + 512], cb == 0, cb == 3,
                       reads=["woa"] + [("y_aT", t) for t in range(tg * 4, tg * 4 + 4)], writes=[kya])
                for cb in range(4):
                    mm(psb(b0 + 1), wob[:, cb, j * 128:(j + 1) * 128], zoT[:, cb, t0:t0 + 512], cb == 0, cb == 3,
                       reads=["wob"] + [("zoT", t) for t in range(tg * 4, tg * 4 + 4)], writes=[kyb])
                act(sgav[bb], psb(b0 + 2), AF.Sigmoid, reads=[kga], writes=["sga%d" % bb])
                act(sgbv[bb], psb(b0 + 3), AF.Sigmoid, reads=[kgb], writes=["sgb%d" % bb])
                tt("vector", m1v[bb], psb(b0), sgav[bb], ALU.mult, reads=[kya, "sga%d" % bb], writes=["m1_%d" % bb])
                tt("vector", m2v[bb], psb(b0 + 1), sgbv[bb], ALU.mult, reads=[kyb, "sgb%d" % bb], writes=["m2_%d" % bb])
                tt("vector", mergedT[:, j, t0:t0 + 512], m1v[bb], m2v[bb], ALU.add,
                   reads=["m1_%d" % bb, "m2_%d" % bb], writes=[("mergedT", j, tg)])

        if maybe_stop("merge", [(flat(mergedT), [("mergedT", j, tg) for j in range(8) for tg in range(4)])]):
            return nc
        wff2 = view("wff2", 0, [128, 32, D], BF16)
        fT = view("fT", O_Y, [128, 32, 512], BF16)
        w1bv = [view("w1b%d" % b, O_B + b * 4 * KB, [128, 8, 256], BF16) for b in range(2)]
        w1bv.append(view("w1b2", O_END + 4 * KB, [128, 8, 256], BF16))
        hnT = view("hnT", O_B + 8 * KB, [128, 8, 512], BF16)
        hv = [view("h%d" % b, O_TMP + b * 4 * KB, [128, D], F32) for b in range(4)]
        tmpv = [view("tmp%d" % b, O_TMP + 16 * KB + b * 4 * KB, [128, D], F32) for b in range(2)]
        hsbv = [view("hsb0", O_END, [128, D], BF16), view("hsb1", O_END + 2 * KB, [128, D], BF16)]
        junk3 = junk3_t[:]
        w_ff2_v = w_ff2.rearrange("(k p) c -> p k c", p=128)
        w_ff1_v = w_ff1.rearrange("(k p) c -> p k c", p=128)

        def load_wff2():
            for q in range(8):
                P.dma("gpsimd", lambda e, q=q: e.dma_start(out=wff2[:, q * 4:(q + 1) * 4, :],
                                                           in_=w_ff2_v[:, q * 4:(q + 1) * 4, :]),
                      writes=[("wff2", q)])

        out_toks = []
        cnt3 = {"ti": 0}

        def mstage(i, q, s):
            h = hv[q]
            hk = "h%d" % q
            P.dma("sync", lambda e, i=i, h=h: e.dma_start(out=h, in_=x[i * 128:(i + 1) * 128, :]), writes=[hk])
            for half in range(2):
                for k in range(8):
                    mm(PS[:, 2 * s + half, :], mergedT[:, k, i * 128:(i + 1) * 128],
                       wo[:, k, half * 512:(half + 1) * 512], k == 0, k == 7,
                       reads=[("mergedT", k, i // 4), "wo"], writes=[("ps", 2 * s + half)])
            mp = pspair(s)
            act(junk3, mp, AF.Square, reads=pk(s), writes=["junk3", stk(S_M + i)], accum=stc(S_M + i))
            rsqrt_col(stc(S_M + i), 1.0 / D, stk(S_M + i))
            tmp = tmpv[cnt3["ti"] % 2]
            tk_ = "tmp%d" % (cnt3["ti"] % 2)
            cnt3["ti"] += 1
            stt("vector", tmp, mp, stc(S_M + i), GpmRow, ALU.mult, ALU.mult,
                reads=pk(s) + [stk(S_M + i), "rowP3"], writes=[tk_])
            tt("vector", h, h, tmp, ALU.add, reads=[hk, tk_], writes=[hk])
            act(junk3, h, AF.Square, reads=[hk], writes=["junk3", stk(S_H + i)], accum=stc(S_H + i))
            rsqrt_col(stc(S_H + i), 1.0 / D, stk(S_H + i))
            stt("vector", hsbv[q % 2], h, stc(S_H + i), GpreRow, ALU.mult, ALU.mult,
                reads=[hk, stk(S_H + i), "rowP3"], writes=["hsb%d" % (q % 2)])

        def mtrans(q):
            for k in range(8):
                tr(psT(6, 8)[:, k, :], hsbv[q % 2][:, k * 128:(k + 1) * 128], reads=["hsb%d" % (q % 2)],
                   writes=[("ps", 6)], signal=(k == 7))
            cp("scalar", hnT[:, :, q * 128:(q + 1) * 128], psT(6, 8), reads=[("ps", 6)], writes=[("hnT", q)])

        def w1_load(fp):
            wb1 = w1bv[fp % 3]
            P.dma("sync", lambda e, wb1=wb1, fp=fp: e.dma_start(
                out=wb1.rearrange("p k c -> p (k c)"), in_=wff1s_d[fp]), reads=[("wff1s", fp)],
                writes=["w1b%d" % (fp % 3)])

        w1_pre = {"n": 0}

        def ff1(tb):
            for fp in range(w1_pre["n"], 3):
                w1_load(fp)
            for fb in range(32):
                fp = fb // 2
                if fb % 2 == 0 and fp + 3 < 16 and fp >= 0:
                    pass
                wb1 = w1bv[fp % 3]
                w1k = "w1b%d" % (fp % 3)
                bank = 4 + (fb % 2)
                for k in range(8):
                    mm(psb(bank), wb1[:, k, (fb % 2) * 128:(fb % 2) * 128 + 128], hnT[:, k, :], k == 0, k == 7,
                       reads=[w1k] + [("hnT", q) for q in range(4)], writes=[("ps", bank)])
                act(psb(bank), psb(bank), AF.Relu, reads=[("ps", bank)], writes=[("ps", bank)])
                act(fT[:, fb, :], psb(bank), AF.Square, reads=[("ps", bank)], writes=[("fT", fb)])
                if fb % 2 == 1 and fp + 3 < 16:
                    w1_load(fp + 3)
            if tb < 3:
                for fp in range(3):
                    w1_load(fp)
                w1_pre["n"] = 3
            else:
                w1_pre["n"] = 0

        def ff2(tb, q):
            i = tb * 4 + q
            h = hv[q]
            hk = "h%d" % q
            s = q % 2
            for half in range(2):
                for fb in range(32):
                    mm(PS[:, 2 * s + half, :], fT[:, fb, q * 128:(q + 1) * 128],
                       wff2[:, fb, half * 512:(half + 1) * 512], fb == 0, fb == 31,
                       reads=[("fT", fb), ("wff2", fb // 4)], writes=[("ps", 2 * s + half)])
            fp_ = pspair(s)
            act(junk3, fp_, AF.Square, reads=pk(s), writes=["junk3", stk(S_F + i)], accum=stc(S_F + i))
            rsqrt_col(stc(S_F + i), 1.0 / D, stk(S_F + i))
            tmp = tmpv[cnt3["ti"] % 2]
            tk_ = "tmp%d" % (cnt3["ti"] % 2)
            cnt3["ti"] += 1
            stt("vector", tmp, fp_, stc(S_F + i), GpfRow, ALU.mult, ALU.mult,
                reads=pk(s) + [stk(S_F + i), "rowP3"], writes=[tk_])
            tt("vector", tmp, tmp, h, ALU.add, reads=[tk_, hk], writes=[tk_])
            out_toks.append(P.dma("sync", lambda e, i=i, tmp=tmp: e.dma_start(out=out[i * 128:(i + 1) * 128, :], in_=tmp),
                                  reads=[tk_]))

        for q in range(4):
            mstage(q, q, q % 3)
            if q >= 1:
                mtrans(q - 1)
        mtrans(3)
        load_wff2()
        for tb in range(4):
            ff1(tb)
            for q in range(4):
                ff2(tb, q)
                if tb < 3:
                    mstage((tb + 1) * 4 + q, q, 2)
                    if q >= 1:
                        mtrans(q - 1)
            if tb < 3:
                mtrans(3)
        for t in out_toks:
            P.wait("sync", t)
        P.emit()
        print("program: ops=%d waits=%d counts=%s" % (P.nops, P.nwaits, P.cnt), flush=True)
    return nc


_CONST = {}


def _constants():
    if _CONST:
        return _CONST
    bf = ml_dtypes.bfloat16
    H = L // 2
    m = np.arange(H, dtype=np.float64)
    f = np.arange(H, dtype=np.float64) + 0.5
    tabs = []
    for r in range(2):
        ang = 2.0 * np.pi * np.outer(2 * m + r, f) / (2 * L)
        tabs.append((np.cos(ang), np.sin(ang)))
    def fwd(T):
        return T.reshape(8, 128, 8, 128).transpose(2, 1, 0, 3)
    TF = np.stack([fwd(tabs[0][0]), fwd(tabs[0][1]), fwd(tabs[1][0]), fwd(tabs[1][1])], axis=2)
    def inv(T):
        return T.reshape(8, 128, 8, 128).transpose(0, 3, 2, 1)
    TI = np.stack([np.stack([inv(tabs[r][0]), inv(tabs[r][1])], axis=2) for r in range(2)], axis=0)
    _CONST["TF"] = np.ascontiguousarray(TF).astype(bf)
    _CONST["TI"] = np.ascontiguousarray(TI).astype(bf)
    _CONST["ident"] = np.eye(128, dtype=np.float32).astype(bf)
    t = np.linspace(0.0, 1.0, L, dtype=np.float32)[:, None]
    bands = 16
    w = (2.0 * math.pi * np.arange(L, dtype=np.float32)[:, None] / L).astype(np.float32)
    fr = np.linspace(1e-4, bands - 1, bands, dtype=np.float32)[None, :]
    feats = np.concatenate([t, np.cos(fr * w), -np.sin(fr * w)], axis=-1).astype(np.float32)
    _CONST["featsT"] = np.ascontiguousarray(feats.T)
    max_decay = math.log(1e-2) / 0.3
    min_decay = math.log(1e-2) / 1.5
    _CONST["delta"] = np.abs(np.linspace(min_decay, max_decay, 512, dtype=np.float32)).astype(np.float32)
    _CONST["tneg"] = np.ascontiguousarray((-t[:, 0]).reshape(8, 128, 2).transpose(1, 2, 0).reshape(128, 16))
    return _CONST


_PROG = {}


def kernel(x, g_pre_mix, w_in, a_v_gain, a_w_s, a_b_s, w_out_a, b_conv_w, b_conv_b,
           b_filt_w1, b_filt_b1, b_filt_f1, b_filt_w2, b_filt_b2, b_filt_f2, b_filt_w3,
           b_skip, w_out_b, w_o, g_post_mix, g_pre_ffn, w_ff1, w_ff2, g_post_ffn):
    f32 = lambda a: np.ascontiguousarray(np.asarray(a, dtype=np.float32))
    c = _constants()
    x = f32(x)
    colp = np.zeros((128, NCOLP), np.float32)
    cw = f32(b_conv_w)[0]
    cbias = f32(b_conv_b)[0]
    colp[:, C_W0:C_W0 + 12] = cw[0].reshape(12, 128).T
    colp[:, C_W1:C_W1 + 12] = cw[1].reshape(12, 128).T
    colp[:, C_W2:C_W2 + 12] = cw[2].reshape(12, 128).T
    colp[:, C_CB:C_CB + 12] = cbias.reshape(12, 128).T
    colp[0:64, C_F1] = f32(b_filt_f1)[0]
    colp[0:64, C_B1] = f32(b_filt_b1)[0]
    colp[0:64, C_F2] = f32(b_filt_f2)[0]
    colp[0:64, C_B2] = f32(b_filt_b2)[0]
    colp[:, C_BST:C_BST + 8] = f32(a_b_s)[0].T
    colp[:, C_TNEG:C_TNEG + 16] = c["tneg"]
    rowp = np.zeros((1, NROWP), np.float32)
    rowp[0, R_GPM:R_GPM + 1024] = f32(g_pre_mix)[0]
    rowp[0, R_GAIN:R_GAIN + 512] = f32(a_v_gain)[0]
    rowp[0, R_SKIP:R_SKIP + 1024] = f32(b_skip)[0].reshape(-1)
    rowp[0, R_DELTA:R_DELTA + 512] = c["delta"]
    rowp[0, R_GPOSTMIX:R_GPOSTMIX + 1024] = f32(g_post_mix)[0]
    rowp[0, R_GPREFFN:R_GPREFFN + 1024] = f32(g_pre_ffn)[0]
    rowp[0, R_GPOSTFFN:R_GPOSTFFN + 1024] = f32(g_post_ffn)[0]
    shared = {
        "w_in": f32(w_in)[0], "w_out_a": f32(w_out_a)[0], "w_out_b": f32(w_out_b)[0], "w_o": f32(w_o)[0],
        "w_ff1": f32(w_ff1)[0], "w_ff2": f32(w_ff2)[0],
        "wsT": np.ascontiguousarray(f32(a_w_s)[0].transpose(2, 0, 1)),
        "colp": colp, "rowp": rowp, "featsT": c["featsT"],
        "fw1": f32(b_filt_w1)[0], "fw2": f32(b_filt_w2)[0], "fw3": f32(b_filt_w3)[0],
        "ident": c["ident"], "TF": c["TF"], "TI": c["TI"],
    }
    if "nc" not in _PROG:
        _PROG["nc"] = build_program(STOP_AFTER)
    nc = _PROG["nc"]
    in_maps = []
    for b in range(8):
        m = dict(shared)
        m["x"] = np.ascontiguousarray(x[b])
        in_maps.append(m)
    res = run_bass_kernel_spmd(nc, in_maps, core_ids=list(range(8)))
    outs = [np.asarray(r["out"], dtype=np.float32) for r in res.results]
    return np.stack(outs, axis=0)
```

```python
import math
from contextlib import ExitStack

import numpy as np
import ml_dtypes

import concourse.bass as bass
import concourse.mybir as mybir
from concourse.bass_utils import run_bass_kernel_spmd

F32 = mybir.dt.float32
BF16 = mybir.dt.bfloat16
AF = mybir.ActivationFunctionType
ALU = mybir.AluOpType

L = 2048
D = 1024
NT = 16
EPS = 1e-6
KB = 1024
STOP_AFTER = None


class Buf:
    def __init__(self, name, off, size):
        self.name, self.off, self.size = name, off, size
        self.toks = {}
        self.inherit = {}
        self.active = False


class Prog:
    ENGS = ("sync", "scalar", "gpsimd", "vector", "tensor")

    def __init__(self, nc, es, n_dma_sems=32):
        self.nc = nc
        self.ops = {e: [] for e in self.ENGS}
        self.sem = {e: es.enter_context(nc.semaphore("s_" + e)) for e in self.ENGS}
        self.cnt = {e: 0 for e in self.ENGS}
        self.dsem = [es.enter_context(nc.semaphore("d%d" % i)) for i in range(n_dma_sems)]
        self.dcnt = [0] * n_dma_sems
        self.dnext = 0
        self.waited = {e: {} for e in self.ENGS}
        self.last_w = {}
        self.readers = {}
        self.bufs = {}
        self.active = []
        self.nwaits = 0
        self.nops = 0

    def buf(self, name, off, size):
        assert name not in self.bufs, name
        self.bufs[name] = Buf(name, off, size)

    def _wait(self, eng, tok):
        if tok is None:
            return
        semkey, val = tok
        if semkey == eng and val > self.cnt[eng]:
            return
        if self.waited[eng].get(semkey, 0) >= val:
            return
        self.waited[eng][semkey] = val
        sem = self.sem[semkey] if isinstance(semkey, str) else self.dsem[semkey]
        self.nwaits += 1
        self.ops[eng].append(lambda e, sem=sem, val=val: e.wait_ge(sem, val))

    def _parent(self, k):
        return self.bufs.get(k[0] if isinstance(k, tuple) else k)

    def _touch(self, eng, k):
        b = self._parent(k)
        if b is None:
            return
        if not b.active:
            for o in list(self.active):
                if o is not b and o.off < b.off + b.size and b.off < o.off + o.size:
                    for src in (o.toks, o.inherit):
                        for sk, v in src.items():
                            if b.inherit.get(sk, 0) < v:
                                b.inherit[sk] = v
                    o.active = False
                    self.active.remove(o)
            b.active = True
            b.toks = {}
            self.active.append(b)
        for sk, v in b.inherit.items():
            self._wait(eng, (sk, v))

    def _deps(self, eng, reads, writes, extra):
        for k in reads:
            self._touch(eng, k)
            self._wait(eng, self.last_w.get(k))
        for k in writes:
            self._touch(eng, k)
            self._wait(eng, self.last_w.get(k))
            for t in self.readers.get(k, ()):
                self._wait(eng, t)
        for t in extra:
            self._wait(eng, t)

    def _record(self, tok, reads, writes):
        for k in reads:
            self.readers.setdefault(k, []).append(tok)
            b = self._parent(k)
            if b is not None and b.toks.get(tok[0], 0) < tok[1]:
                b.toks[tok[0]] = tok[1]
        for k in writes:
            self.last_w[k] = tok
            self.readers[k] = []
            b = self._parent(k)
            if b is not None and b.toks.get(tok[0], 0) < tok[1]:
                b.toks[tok[0]] = tok[1]

    def op(self, eng, fn, reads=(), writes=(), extra=(), signal=True):
        self._deps(eng, reads, writes, extra)
        self.nops += 1
        if signal:
            self.cnt[eng] += 1
            tok = (eng, self.cnt[eng])
            sem = self.sem[eng]
            self.ops[eng].append(lambda e, fn=fn, sem=sem: fn(e).then_inc(sem, 1))
        else:
            tok = (eng, self.cnt[eng] + 1)
            self.ops[eng].append(lambda e, fn=fn: fn(e))
        self._record(tok, reads, writes)
        return tok

    def dma(self, eng, fn, reads=(), writes=(), extra=()):
        self._deps(eng, reads, writes, extra)
        half = len(self.dsem) // 2
        base = 0 if eng == "gpsimd" else half
        self.dnext_q = getattr(self, "dnext_q", {})
        j = self.dnext_q.get(eng, 0)
        self.dnext_q[eng] = (j + 1) % half
        i = base + j
        if self.dcnt[i]:
            self._wait(eng, (i, self.dcnt[i]))
        self.dcnt[i] += 16
        tok = (i, self.dcnt[i])
        sem = self.dsem[i]
        self.ops[eng].append(lambda e, fn=fn, sem=sem: fn(e).then_inc(sem, 16))
        self._record(tok, reads, writes)
        return tok

    def wait(self, eng, tok):
        self._wait(eng, tok)

    def emit(self):
        with self.nc.Block() as block:
            for name in self.ENGS:
                ops = self.ops[name]
                if not ops:
                    continue

                def body(e, ops=ops):
                    for f in ops:
                        f(e)
                getattr(block, name)(body)


def lockstep(gens):
    alive = list(gens)
    while alive:
        for g in list(alive):
            try:
                next(g)
            except StopIteration:
                alive.remove(g)


def _prod(s):
    r = 1
    for v in s:
        r *= v
    return r


NCOLP = 76
C_W0, C_W1, C_W2, C_CB, C_F1, C_B1, C_F2, C_B2, C_BST, C_TNEG = 0, 12, 24, 36, 48, 49, 50, 51, 52, 60
R_GPM, R_GAIN, R_SKIP, R_DELTA, R_GPOSTMIX, R_GPREFFN, R_GPOSTFFN = 0, 1024, 1536, 2560, 3072, 4096, 5120
NROWP = 6144


def build_program(stop_after=None):
    nc = bass.Bass("TRN2", target_bir_lowering=False)
    dt_in = lambda n, s, d=F32: nc.dram_tensor(n, s, d, kind="ExternalInput").ap()
    x = dt_in("x", [L, D])
    w_in = dt_in("w_in", [D, 4608])
    w_out_a = dt_in("w_out_a", [512, D])
    w_out_b = dt_in("w_out_b", [512, D])
    w_o = dt_in("w_o", [D, D])
    w_ff1 = dt_in("w_ff1", [D, 4096])
    w_ff2 = dt_in("w_ff2", [4096, D])
    wsT_d = dt_in("wsT", [128, 8, 128])
    colp_d = dt_in("colp", [128, NCOLP])
    rowp_d = dt_in("rowp", [1, NROWP])
    featsT_d = dt_in("featsT", [33, L])
    fw1_d = dt_in("fw1", [33, 64])
    fw2_d = dt_in("fw2", [64, 64])
    fw3_d = dt_in("fw3", [64, 2048])
    ident_d = dt_in("ident", [128, 128], BF16)
    TF_d = dt_in("TF", [8, 128, 4, 8, 128], BF16)
    TI_d = dt_in("TI", [2, 8, 128, 2, 8, 128], BF16)
    out = nc.dram_tensor("out", [L, D], F32, kind="ExternalOutput").ap()
    wff1s_d = nc.dram_tensor("wff1_bf16_scratch", [16, 128, 2048], BF16, kind="Internal").ap()
    hsd2_d = nc.dram_tensor("hsd2_scratch", [2, NT, 128, 512], BF16, kind="Internal").ap()
    dbg = None
    if stop_after is not None:
        dbg = nc.dram_tensor("dbg", [128, 20 * 1024], F32, kind="ExternalOutput").ap()

    with ExitStack() as es:
        P = Prog(nc, es)
        ARENA = 204 * KB
        AR = es.enter_context(nc.sbuf_tensor("arena", [128, ARENA // 2], BF16))
        PS = es.enter_context(nc.psum_tensor("ps", [128, 8, 512], F32))
        ident = es.enter_context(nc.sbuf_tensor("ident_sb", [128, 128], BF16))
        ones = es.enter_context(nc.sbuf_tensor("ones", [128, 128], BF16))
        colp = es.enter_context(nc.sbuf_tensor("colp_sb", [128, NCOLP], F32))
        stats = es.enter_context(nc.sbuf_tensor("stats", [128, 136], F32))
        junk3_t = es.enter_context(nc.sbuf_tensor("junk3", [128, D], BF16))
        epsT = es.enter_context(nc.sbuf_tensor("epsT", [128, 1], F32))

        def view(name, off, shape, dtype):
            esz = 4 if dtype == F32 else 2
            n = _prod(shape[1:])
            assert off % 4 == 0 and off + n * esz <= ARENA, (name, off, n * esz)
            ap = AR[0:shape[0], off // 2: off // 2 + n * esz // 2]
            if dtype != BF16:
                ap = ap.bitcast(dtype)
            if len(shape) == 3:
                ap = ap.rearrange("p (a b) -> p a b", a=shape[1])
            elif len(shape) == 4:
                ap = ap.rearrange("p (a b c) -> p a b c", a=shape[1], b=shape[2])
            P.buf(name, off, n * esz)
            return ap

        def psb(b):
            return PS[:, b, :]

        def pspair(s):
            return PS[:, 2 * s:2 * s + 2, :].rearrange("p a b -> p (a b)")

        def psT(b, k):
            return PS[:, b, :].bitcast(BF16)[:, 0:k * 128].rearrange("p (k t) -> p k t", k=k)

        def pk(s):
            return [("ps", 2 * s), ("ps", 2 * s + 1)]

        def mm(o, lhsT, rhs, start, stop, reads, writes, signal=None):
            if signal is None:
                signal = stop
            return P.op("tensor", lambda e: e.matmul(o, lhsT, rhs, start=start, stop=stop),
                        reads=reads, writes=writes, signal=signal)

        def tr(o, in_, reads, writes, signal):
            return P.op("tensor", lambda e: e.transpose(o, in_, ident[:]),
                        reads=list(reads) + ["ident"], writes=writes, signal=signal)

        def act(o, in_, func, reads, writes, scale=1.0, bias=0.0, accum=None):
            if accum is None:
                return P.op("scalar", lambda e: e.activation(out=o, in_=in_, func=func, bias=bias, scale=scale),
                            reads=reads, writes=writes)
            return P.op("scalar", lambda e: e.activation(out=o, in_=in_, func=func, bias=bias, scale=scale,
                                                         accum_out=accum), reads=reads, writes=writes)

        def ts(eng, o, in0, s1, s2, op0, op1, reads, writes):
            if s2 is None:
                return P.op(eng, lambda e: e.tensor_scalar(out=o, in0=in0, scalar1=s1, scalar2=None, op0=op0),
                            reads=reads, writes=writes)
            return P.op(eng, lambda e: e.tensor_scalar(out=o, in0=in0, scalar1=s1, scalar2=s2, op0=op0, op1=op1),
                        reads=reads, writes=writes)

        def stt(eng, o, in0, sc, in1, op0, op1, reads, writes):
            return P.op(eng, lambda e: e.scalar_tensor_tensor(out=o, in0=in0, scalar=sc, in1=in1, op0=op0, op1=op1),
                        reads=reads, writes=writes)

        def tt(eng, o, in0, in1, op, reads, writes):
            return P.op(eng, lambda e: e.tensor_tensor(out=o, in0=in0, in1=in1, op=op), reads=reads, writes=writes)

        def cp(eng, o, in_, reads, writes):
            if eng == "scalar":
                return P.op(eng, lambda e: e.copy(out=o, in_=in_), reads=reads, writes=writes)
            return P.op(eng, lambda e: e.tensor_copy(out=o, in_=in_), reads=reads, writes=writes)

        def rsqrt_col(col, scale, key):
            act(col, col, AF.Sqrt, reads=[key, "epsT"], writes=[key], scale=scale, bias=epsT[:, 0:1])
            P.op("vector", lambda e: e.reciprocal(out=col, in_=col), reads=[key], writes=[key])

        O_XNT = 0
        O_YAT = 32 * KB
        O_ZOT = 48 * KB
        O_A = 64 * KB
        O_ZTOK = 80 * KB
        O_B = 96 * KB
        O_Y = 112 * KB
        O_TAB = 144 * KB
        O_TMP = 160 * KB
        O_ROW = 184 * KB
        O_END = 196 * KB

        xnT = view("xnT", O_XNT, [128, 8, L], BF16)

        P.dma("sync", lambda e: e.dma_start(out=ident[:], in_=ident_d), writes=["ident"])
        P.dma("sync", lambda e: e.dma_start(out=colp[:], in_=colp_d), writes=["colp"])
        P.op("vector", lambda e: e.memset(stats[:], 0.0), writes=["stats"])
        P.op("vector", lambda e: e.memset(epsT[:], EPS), writes=["epsT"])
        P.op("vector", lambda e: e.memset(ones[:], 1.0), writes=["ones"])
        rowP1 = view("rowP1", O_ROW, [128, 3072], F32)
        P.dma("sync", lambda e: e.dma_start(out=rowP1, in_=rowp_d[:, 0:3072].partition_broadcast(128)),
              writes=["rowP1"])
        gpmRow = rowP1[:, 0:1024]
        gainRow = rowP1[:, 1024:1536]
        skipRow = rowP1[:, 1536:2560]
        deltaRow = rowP1[:, 2560:3072]

        w_ff1_v0 = w_ff1.rearrange("(k p) c -> p k c", p=128)

        def convert_wff1(fps, after=None):
            for fp in fps:
                P.dma("gpsimd", lambda e, fp=fp: e.dma_start(
                    out=wff1s_d[fp].rearrange("p (k c) -> p k c", k=8), in_=w_ff1_v0[:, :, fp * 256:(fp + 1) * 256]),
                    writes=[("wff1s", fp)], extra=([after] if after is not None else []))

        S_SS1 = 0
        S_MA_SUM, S_MA_NM, S_MA_VAR = 32, 48, 64
        S_M, S_H, S_F = 80, 96, 112
        S_FB1, S_FB2 = 128, 129
        stc = lambda c: stats[:, c:c + 1]
        stk = lambda c: ("st", c)
        for c in range(136):
            P.last_w[("st", c)] = P.last_w["stats"]

        O_S1 = 170 * KB
        xt_v = [view("xt%d" % b, O_ZTOK + b * 4 * KB, [128, D], F32) for b in range(4)]
        xs_v = [view("xs%d" % b, O_S1 + b * 2 * KB, [128, D], BF16) for b in range(2)]
        junk1 = junk3_t[:]

        def s1_a1(i):
            b = i % 4
            P.dma("sync", lambda e, i=i, b=b: e.dma_start(out=xt_v[b], in_=x[i * 128:(i + 1) * 128, :]),
                  writes=["xt%d" % b])
            col = stc(S_SS1 + i)
            act(junk1, xt_v[b], AF.Square, reads=["xt%d" % b], writes=[stk(S_SS1 + i), "junk3"], accum=col)
            rsqrt_col(col, 1.0 / D, stk(S_SS1 + i))

        def s1_a2(i):
            b = i % 4
            stt("vector", xs_v[i % 2], xt_v[b], stc(S_SS1 + i), gpmRow, ALU.mult, ALU.mult,
                reads=["xt%d" % b, stk(S_SS1 + i), "rowP1"], writes=["xs%d" % (i % 2)])

        def s1_b(i):
            b = i % 2
            for k in range(8):
                tr(psT(6, 8)[:, k, :], xs_v[b][:, k * 128:(k + 1) * 128], reads=["xs%d" % b], writes=[("ps", 6)],
                   signal=(k == 7))
            cp("scalar", xnT[:, :, i * 128:(i + 1) * 128], psT(6, 8), reads=[("ps", 6)], writes=[("xnT", i)])

        def s1_gen():
            s1_a1(0)
            s1_a1(1)
            s1_a2(0)
            for i in range(NT):
                if i + 2 < NT:
                    s1_a1(i + 2)
                if i + 1 < NT:
                    s1_a2(i + 1)
                s1_b(i)
                yield

        xnT_keys = lambda t0, t1: [("xnT", t) for t in range(t0, t1)]
        def maybe_stop(stage, items):
            if stop_after != stage:
                return False
            off = 0
            toks = []
            for (ap, keys) in items:
                n = ap.shape[1]
                for c0 in range(0, n, 2048):
                    c1 = min(n, c0 + 2048)
                    toks.append(P.dma("gpsimd", lambda e, ap=ap, off=off, c0=c0, c1=c1: e.dma_start(
                        out=dbg[0:ap.shape[0], off + c0:off + c1], in_=ap[:, c0:c1]), reads=keys))
                off += n
            for t in toks:
                P.wait("gpsimd", t)
            P.emit()
            return True

        flat = lambda ap: ap.rearrange("p a b -> p (a b)")

        w_in_v = w_in.rearrange("(k p) c -> p k c", p=128)

        ring = {"i": 0}

        def next_pair():
            s = ring["i"] % 3
            ring["i"] += 1
            return s

        wb_pref = {}

        def hy_prefetch(tag, col0, wb_off):
            wb = view("wb_" + tag, wb_off, [128, 8, 512], BF16)
            for hf in range(2):
                P.dma("gpsimd", lambda e, hf=hf: e.dma_start(out=wb[:, :, hf * 256:(hf + 1) * 256],
                                                             in_=w_in_v[:, :, col0 + hf * 256:col0 + (hf + 1) * 256]),
                      writes=[("wb_" + tag, hf)])
            wb_pref[tag] = wb
            return wb

        def hy_group(tag, col0, dest, destname, tmp_off, wb_off=None, after_wb=None, ob_off=None):
            if wb_off is None:
                wb_off = tmp_off
                tmp_off = tmp_off + 8 * KB
            if ob_off is None:
                ob_off = tmp_off + 12 * KB
            if tag in wb_pref:
                wb = wb_pref[tag]
            else:
                wb = hy_prefetch(tag, col0, wb_off)
            pcv = [view("pc_%s%d" % (tag, b), tmp_off + b * 4 * KB, [128, 1024], F32) for b in range(3)]
            obv = [view("ob_%s%d" % (tag, b), ob_off + b * 2 * KB, [128, 1024], BF16) for b in range(3)]
            if after_wb is not None:
                after_wb()
            ekey = ("ps", 7)

            def edges():
                for cb in range(4):
                    for k in range(8):
                        mm(PS[:, 7, 2 * cb:2 * cb + 2], wb[:, k, cb * 128:(cb + 1) * 128], xnT[:, k, 1023:1025],
                           k == 0, k == 7, reads=[("wb_" + tag, cb // 2), ("xnT", 7), ("xnT", 8)], writes=[ekey],
                           signal=(k == 7 and cb == 3))

            pairs_of = {}

            def main_mm(unit):
                half, cb = unit // 4, unit % 4
                wkey = ("wb_" + tag, cb // 2)
                s = next_pair()
                pairs_of[unit] = s
                for tg2 in range(2):
                    t0 = half * 1024 + tg2 * 512
                    for k in range(8):
                        mm(PS[:, 2 * s + tg2, :], wb[:, k, cb * 128:(cb + 1) * 128], xnT[:, k, t0:t0 + 512],
                           k == 0, k == 7, reads=[wkey] + xnT_keys(t0 // 128, t0 // 128 + 4),
                           writes=[("ps", 2 * s + tg2)])

            def main_ew(unit):
                half, cb = unit // 4, unit % 4
                gcb = (col0 - 1024) // 128 + cb
                w0c = colp[:, C_W0 + gcb:C_W0 + gcb + 1]
                w1c = colp[:, C_W1 + gcb:C_W1 + gcb + 1]
                w2c = colp[:, C_W2 + gcb:C_W2 + gcb + 1]
                bc = colp[:, C_CB + gcb:C_CB + gcb + 1]
                edge = PS[:, 7, 2 * cb:2 * cb + 2]
                s = pairs_of[unit]
                pkey = pk(s)
                p = pspair(s)
                b = unit % 3
                pc, ob = pcv[b], obv[b]
                pck, obk = "pc_%s%d" % (tag, b), "ob_%s%d" % (tag, b)
                act(pc, p, AF.Identity, reads=pkey + ["colp"], writes=[pck], scale=w1c, bias=bc)
                if half == 1:
                    stt("vector", pc[:, 0:1], edge[:, 0:1], w0c, pc[:, 0:1], ALU.mult, ALU.add,
                        reads=[ekey, pck, "colp"], writes=[pck])
                stt("vector", pc[:, 1:1024], p[:, 0:1023], w0c, pc[:, 1:1024], ALU.mult, ALU.add,
                    reads=pkey + [pck, "colp"], writes=[pck])
                stt("vector", ob[:, 0:1023], p[:, 1:1024], w2c, pc[:, 0:1023], ALU.mult, ALU.add,
                    reads=pkey + [pck, "colp"], writes=[obk])
                if half == 0:
                    stt("vector", ob[:, 1023:1024], edge[:, 1:2], w2c, pc[:, 1023:1024], ALU.mult, ALU.add,
                        reads=[ekey, pck, "colp"], writes=[obk])
                else:
                    cp("vector", ob[:, 1023:1024], pc[:, 1023:1024], reads=[pck], writes=[obk])

            def trans(unit):
                half, cb = unit // 4, unit % 4
                b = unit % 3
                ob = obv[b]
                obk = "ob_%s%d" % (tag, b)
                for r in range(2):
                    for jj in range(4):
                        st_ = 256 * jj + r
                        tr(psT(6, 8)[:, r * 4 + jj, :], ob[:, st_:st_ + 255:2], reads=[obk], writes=[("ps", 6)],
                           signal=(r == 1 and jj == 3))
                cp("scalar", dest[:, :, 4 * half:4 * half + 4, cb * 128:(cb + 1) * 128],
                   psT(6, 8).rearrange("p (r j) c -> p r j c", r=2),
                   reads=[("ps", 6)], writes=[(destname, r * 8 + 4 * half + jj) for r in range(2) for jj in range(4)])

            for unit in range(9):
                if unit < 8:
                    main_mm(unit)
                    if unit == 0:
                        edges()
                    main_ew(unit)
                if unit >= 1:
                    trans(unit - 1)

        normRow = view("normRow", O_END, [128, 2, 512], F32)

        def sin_reduced(dst, a, q, e2, keya, keyd, kt):
            act(q, a, AF.Sin, reads=[keya], writes=[kt + "q"], scale=0.25)
            act(e2, a, AF.Sin, reads=[keya], writes=[kt + "e"], scale=0.125)
            tt("vector", e2, e2, e2, ALU.mult, reads=[kt + "e"], writes=[kt + "e"])
            ts("vector", e2, e2, -2.0, 1.0, ALU.mult, ALU.add, reads=[kt + "e"], writes=[kt + "e"])
            tt("vector", e2, e2, q, ALU.mult, reads=[kt + "e", kt + "q"], writes=[kt + "e"])
            tt("vector", q, q, q, ALU.mult, reads=[kt + "q"], writes=[kt + "q"])
            ts("vector", q, q, -2.0, 1.0, ALU.mult, ALU.add, reads=[kt + "q"], writes=[kt + "q"])
            stt("vector", dst, e2, 4.0, q, ALU.mult, ALU.mult, reads=[kt + "e", kt + "q"], writes=[keyd])

        def layer_chain(ch, t, featsT, fw1, fw2, h2T, tmpsets, fb1, fb2):
            a_t, s1_t, q_t, e_t = tmpsets[t]
            ka, ks1, kq = "fa%d" % t, "fs%d" % t, "fsr%d" % t
            bank = 2 * t
            mm(PS[0:64, bank, :], fw1, featsT[:, ch * 512:(ch + 1) * 512], True, True,
               reads=["fw1", "featsT"], writes=[("ps", bank)])
            yield
            ts("vector", a_t, PS[0:64, bank, :], colp[0:64, C_F1:C_F1 + 1], fb1, ALU.mult, ALU.add,
               reads=[("ps", bank), "colp", stk(S_FB1)], writes=[ka])
            yield
            for _ in sin_reduced_g(s1_t, a_t, q_t, e_t, ka, ks1, kq):
                yield
            mm(PS[0:64, bank + 1, :], fw2, s1_t, True, True, reads=["fw2", ks1], writes=[("ps", bank + 1)])
            yield
            ts("vector", a_t, PS[0:64, bank + 1, :], colp[0:64, C_F2:C_F2 + 1], fb2, ALU.mult, ALU.add,
               reads=[("ps", bank + 1), "colp", stk(S_FB2)], writes=[ka])
            yield
            for _ in sin_reduced_g(h2T[:, ch * 512:(ch + 1) * 512], a_t, q_t, e_t, ka, ("h2T", ch), kq):
                yield

        def sin_reduced_g(dst, a, q, e2, keya, keyd, kt):
            act(q, a, AF.Sin, reads=[keya], writes=[kt + "q"], scale=0.25)
            act(e2, a, AF.Sin, reads=[keya], writes=[kt + "e"], scale=0.125)
            yield
            tt("vector", e2, e2, e2, ALU.mult, reads=[kt + "e"], writes=[kt + "e"])
            ts("vector", e2, e2, -2.0, 1.0, ALU.mult, ALU.add, reads=[kt + "e"], writes=[kt + "e"])
            tt("vector", e2, e2, q, ALU.mult, reads=[kt + "e", kt + "q"], writes=[kt + "e"])
            yield
            tt("vector", q, q, q, ALU.mult, reads=[kt + "q"], writes=[kt + "q"])
            ts("vector", q, q, -2.0, 1.0, ALU.mult, ALU.add, reads=[kt + "q"], writes=[kt + "q"])
            stt("vector", dst, e2, 4.0, q, ALU.mult, ALU.mult, reads=[kt + "e", kt + "q"], writes=[keyd])
            yield

        def filter_gen_all(hs1, hd1):
            featsT = view("featsT", 32 * KB, [33, L], F32)
            h2T = view("h2T", 40 * KB, [64, L], F32)
            fw3 = view("fw3", 48 * KB, [64, 2048], F32)
            w3sd = view("w3sd", 56 * KB, [64, 2048], F32)
            fw1 = view("fw1", O_END + 4 * KB, [33, 64], F32)
            fw2 = view("fw2", O_END + 4 * KB + 256, [64, 64], F32)
            tmpsets = []
            for t in range(2):
                base = O_Y + t * 8 * KB
                tmpsets.append((view("fa%d" % t, base, [64, 512], F32), view("fs%d" % t, base + 2 * KB, [64, 512], F32),
                                view("fsr%dq" % t, base + 4 * KB, [64, 512], F32),
                                view("fsr%de" % t, base + 6 * KB, [64, 512], F32)))
            Ev = [view("fE%d" % b, O_Y + 16 * KB + b * 2 * KB, [128, 512], F32) for b in range(2)]
            sqv = [view("fsq%d" % b, O_Y + 20 * KB + b * 4 * KB, [128, 4, 512], BF16) for b in range(2)]
            stg = [view("fstg%d" % b, O_Y + 28 * KB + b * 2 * KB, [128, 2, 512], BF16) for b in range(2)]
            P.dma("sync", lambda e: e.dma_start(out=featsT, in_=featsT_d), writes=["featsT"])
            P.dma("sync", lambda e: e.dma_start(out=fw3, in_=fw3_d), writes=["fw3"])
            P.dma("sync", lambda e: e.dma_start(out=fw1, in_=fw1_d), writes=["fw1"])
            P.dma("sync", lambda e: e.dma_start(out=fw2, in_=fw2_d), writes=["fw2"])
            fb1 = stats[0:64, S_FB1:S_FB1 + 1]
            fb2 = stats[0:64, S_FB2:S_FB2 + 1]
            tt("vector", fb1, colp[0:64, C_F1:C_F1 + 1], colp[0:64, C_B1:C_B1 + 1], ALU.mult,
               reads=["colp"], writes=[stk(S_FB1)])
            tt("vector", fb2, colp[0:64, C_F2:C_F2 + 1], colp[0:64, C_B2:C_B2 + 1], ALU.mult,
               reads=["colp"], writes=[stk(S_FB2)])
            for o in range(2):
                f_ = fw3[:, o * 1024:o * 1024 + 512]
                b_ = fw3[:, o * 1024 + 512:o * 1024 + 1024]
                tt("vector", w3sd[:, (2 * o) * 512:(2 * o + 1) * 512], f_, b_, ALU.add, reads=["fw3"],
                   writes=[("w3sd", 2 * o)])
                tt("vector", w3sd[:, (2 * o + 1) * 512:(2 * o + 2) * 512], f_, b_, ALU.subtract, reads=["fw3"],
                   writes=[("w3sd", 2 * o + 1)])
            yield
            for grp in ((0, 1), (2, 3)):
                gens = [layer_chain(ch, ch % 2, featsT, fw1, fw2, h2T, tmpsets, fb1, fb2) for ch in grp]
                alive = list(gens)
                while alive:
                    for g in list(alive):
                        try:
                            next(g)
                        except StopIteration:
                            alive.remove(g)
                    yield
            dsts = [(hs1, "hs1"), (hd1, "hd1")]
            for nt in range(NT):
                b = nt % 2
                r_, j_ = nt // 8, nt % 8
                st_ = 256 * j_ + r_
                for j in range(4):
                    mm(PS[:, j, :], h2T[:, st_:st_ + 255:2], w3sd[:, j * 512:(j + 1) * 512], True, True,
                       reads=[("h2T", j_ // 2), ("w3sd", j)], writes=[("ps", j)])
                Ek = "fE%d" % b
                act(Ev[b], deltaRow, AF.Exp, reads=["rowP1", "colp"], writes=[Ek],
                    scale=colp[:, C_TNEG + nt:C_TNEG + nt + 1])
                outs = [hs1[:, r_, j_, :], hd1[:, r_, j_, :], stg[b][:, 0, :], stg[b][:, 1, :]]
                okeys = [("hs1", nt), ("hd1", nt), ("fstg%d" % b, 0), ("fstg%d" % b, 1)]
                for j in range(4):
                    stt("vector", outs[j], Ev[b], 0.05, PS[:, j, :], ALU.add, ALU.mult,
                        reads=[Ek, ("ps", j)], writes=[okeys[j]])
                for j in range(4):
                    act(sqv[b][:, j, :], outs[j], AF.Square, reads=[okeys[j]], writes=[("fsq%d" % b, j)])
                for j in range(4):
                    acc = 4 + j // 2
                    mm(psb(acc), ones[:], sqv[b][:, j, :], nt == 0 and j % 2 == 0, nt == NT - 1 and j % 2 == 1,
                       reads=["ones", ("fsq%d" % b, j)], writes=[("ps", acc)], signal=(j % 2 == 1))
                for r in range(2):
                    P.dma("sync", lambda e, r=r, nt=nt, b=b: e.dma_start(out=hsd2_d[r, nt], in_=stg[b][:, r, :]),
                          reads=[("fstg%d" % b, r)], writes=[("hsd2", r, nt)])
                yield
            for o in range(2):
                act(normRow[:, o, :], psb(4 + o), AF.Sqrt, reads=[("ps", 4 + o), "epsT"], writes=[("normRow", o)],
                    scale=0.5, bias=epsT[:, 0:1])
                P.op("vector", lambda e, o=o: e.reciprocal(out=normRow[:, o, :], in_=normRow[:, o, :]),
                     reads=[("normRow", o)], writes=[("normRow", o)])
            yield

        def dft_kpass(o, hs, hsname, hd, hdname, Kb, kname, tabs, tabnames, T, Tn, hook=None):
            for fb in range(8):
                tb = tabs[fb % 2]
                tk = tabnames[fb % 2]
                P.dma("sync", lambda e, fb=fb, tb=tb: e.dma_start(out=tb, in_=TF_d[fb]), writes=[tk])
                if hook is not None:
                    hook(fb, P.last_w.get((tabnames[(fb + 1) % 2])))
                b0 = 4 * (fb % 2)
                srcs = [(hs, hsname, 0), (hd, hdname, 0), (hs, hsname, 1), (hd, hdname, 1)]
                for mh in range(8):
                    for q in range(4):
                        src, sname, r = srcs[q]
                        mm(PS[:, b0 + q, :], tb[:, q, mh, :], src[:, r, mh, :], mh == 0, mh == 7,
                           reads=[tk, (sname, r * 8 + mh)], writes=[("ps", b0 + q)], signal=(mh == 7 and q == 3))
                nr = normRow[:, o, :]
                sr = skipRow[:, o * 512:(o + 1) * 512]
                kE = [("ps", b0 + q) for q in range(4)]
                cp("scalar", T[0], PS[:, b0 + 2, :], reads=[kE[2]], writes=[Tn[0]])
                cp("scalar", T[1], PS[:, b0 + 3, :], reads=[kE[3]], writes=[Tn[1]])
                tt("vector", T[2], PS[:, b0, :], T[0], ALU.add, reads=[kE[0], Tn[0]], writes=[Tn[2]])
                tt("vector", T[3], PS[:, b0, :], T[0], ALU.subtract, reads=[kE[0], Tn[0]], writes=[Tn[3]])
                tt("vector", T[4], PS[:, b0 + 1, :], T[1], ALU.add, reads=[kE[1], Tn[1]], writes=[Tn[4]])
                tt("vector", T[5], PS[:, b0 + 1, :], T[1], ALU.subtract, reads=[kE[1], Tn[1]], writes=[Tn[5]])
                tt("vector", T[2], T[2], nr, ALU.mult, reads=[Tn[2], ("normRow", o)], writes=[Tn[2]])
                tt("gpsimd", Kb[0][:, fb, :], T[2], sr, ALU.add, reads=[Tn[2], "rowP1"], writes=[(kname + "0", fb)])
                tt("vector", T[3], T[3], nr, ALU.mult, reads=[Tn[3], ("normRow", o)], writes=[Tn[3]])
                tt("gpsimd", Kb[2][:, fb, :], T[3], sr, ALU.add, reads=[Tn[3], "rowP1"], writes=[(kname + "2", fb)])
                tt("vector", Kb[1][:, fb, :], T[4], nr, ALU.mult, reads=[Tn[4], ("normRow", o)], writes=[(kname + "1", fb)])
                tt("vector", Kb[3][:, fb, :], T[5], nr, ALU.mult, reads=[Tn[5], ("normRow", o)], writes=[(kname + "3", fb)])

        def dft_zpass(o, z, zname, Kb, kname, PMs, pmnames, tabs, tabnames, T, Tn, hook=None):
            for fb in range(8):
                tb = tabs[fb % 2]
                tk = tabnames[fb % 2]
                P.dma("sync", lambda e, fb=fb, tb=tb: e.dma_start(out=tb, in_=TF_d[fb]), writes=[tk])
                if hook is not None:
                    hook(fb, P.last_w.get((tabnames[(fb + 1) % 2])))
                b0 = 4 * (fb % 2)
                for mh in range(8):
                    for q in range(4):
                        r = q // 2
                        mm(PS[:, b0 + q, :], tb[:, q, mh, :], z[:, r, mh, :], mh == 0, mh == 7,
                           reads=[tk, (zname, r * 8 + mh)], writes=[("ps", b0 + q)], signal=(mh == 7 and q == 3))
                kE = [("ps", b0 + q) for q in range(4)]
                Skr, Ski, Dkr, Dki = [Kb[q][:, fb, :] for q in range(4)]
                nSkr, nSki, nDkr, nDki = [(kname + str(q), fb) for q in range(4)]
                cp("scalar", T[0], PS[:, b0 + 2, :], reads=[kE[2]], writes=[Tn[0]])
                cp("scalar", T[1], PS[:, b0 + 3, :], reads=[kE[3]], writes=[Tn[1]])
                tt("vector", T[2], PS[:, b0, :], T[0], ALU.add, reads=[kE[0], Tn[0]], writes=[Tn[2]])
                tt("vector", T[3], PS[:, b0, :], T[0], ALU.subtract, reads=[kE[0], Tn[0]], writes=[Tn[3]])
                tt("vector", T[4], PS[:, b0 + 1, :], T[1], ALU.add, reads=[kE[1], Tn[1]], writes=[Tn[4]])
                tt("vector", T[5], PS[:, b0 + 1, :], T[1], ALU.subtract, reads=[kE[1], Tn[1]], writes=[Tn[5]])
                tt("vector", T[6], T[2], Skr, ALU.mult, reads=[Tn[2], nSkr], writes=[Tn[6]])
                tt("vector", T[7], T[4], Ski, ALU.mult, reads=[Tn[4], nSki], writes=[Tn[7]])
                tt("gpsimd", T[6], T[6], T[7], ALU.subtract, reads=[Tn[6], Tn[7]], writes=[Tn[6]])
                tt("vector", T[10], T[2], Ski, ALU.mult, reads=[Tn[2], nSki], writes=[Tn[10]])
                tt("vector", T[11], T[4], Skr, ALU.mult, reads=[Tn[4], nSkr], writes=[Tn[11]])
                tt("gpsimd", T[7], T[10], T[11], ALU.add, reads=[Tn[10], Tn[11]], writes=[Tn[7]])
                tt("vector", T[8], T[3], Dkr, ALU.mult, reads=[Tn[3], nDkr], writes=[Tn[8]])
                tt("vector", T[9], T[5], Dki, ALU.mult, reads=[Tn[5], nDki], writes=[Tn[9]])
                tt("gpsimd", T[8], T[8], T[9], ALU.subtract, reads=[Tn[8], Tn[9]], writes=[Tn[8]])
                tt("vector", T[10], T[3], Dki, ALU.mult, reads=[Tn[3], nDki], writes=[Tn[10]])
                tt("vector", T[11], T[5], Dkr, ALU.mult, reads=[Tn[5], nDkr], writes=[Tn[11]])
                tt("gpsimd", T[9], T[10], T[11], ALU.add, reads=[Tn[10], Tn[11]], writes=[Tn[9]])
                tt("vector", PMs[0][:, fb, :], T[6], T[8], ALU.add, reads=[Tn[6], Tn[8]], writes=[(pmnames[0], fb)])
                tt("gpsimd", PMs[2][:, fb, :], T[6], T[8], ALU.subtract, reads=[Tn[6], Tn[8]], writes=[(pmnames[2], fb)])
                tt("gpsimd", PMs[1][:, fb, :], T[7], T[9], ALU.add, reads=[Tn[7], Tn[9]], writes=[(pmnames[1], fb)])
                tt("gpsimd", PMs[3][:, fb, :], T[7], T[9], ALU.subtract, reads=[Tn[7], Tn[9]], writes=[(pmnames[3], fb)])

        def dft_inverse(PMs, pmnames, mul, mulname, dst, dstname, tabs, tabnames, post=None, banks=(0, 1, 2, 3), hook=None):
            nbk = 0
            prev = None
            for r in range(2):
                for j in range(8):
                    tb = tabs[nbk % 2]
                    tk = tabnames[nbk % 2]
                    P.dma("sync", lambda e, r=r, j=j, tb=tb: e.dma_start(out=tb, in_=TI_d[r, j]), writes=[tk])
                    if hook is not None:
                        hook(nbk)
                    bank = banks[nbk % len(banks)]
                    yk = ("ps", bank)
                    for fb in range(8):
                        mm(psb(bank), tb[:, 0, fb, :], PMs[2 * r][:, fb, :], fb == 0, False,
                           reads=[tk, (pmnames[2 * r], fb)], writes=[yk], signal=False)
                        mm(psb(bank), tb[:, 1, fb, :], PMs[2 * r + 1][:, fb, :], False, fb == 7,
                           reads=[tk, (pmnames[2 * r + 1], fb)], writes=[yk], signal=(fb == 7))
                    stt("vector", dst[:, r, j, :], psb(bank), 1.0 / 2048.0, mul[:, r, j, :], ALU.mult, ALU.mult,
                        reads=[yk, (mulname, r * 8 + j)], writes=[(dstname, r * 8 + j)])
                    if post is not None and prev is not None:
                        post(*prev)
                    prev = (r, j)
                    nbk += 1
                    yield
            if post is not None:
                post(*prev)
                yield

        T4 = [128, 2, 8, 512]
        flat4 = lambda ap: ap.rearrange("p r j c -> p (r j c)")
        keys16 = lambda nm: [(nm, t) for t in range(16)]
        hs1 = view("hs1", O_A, T4, BF16)
        hd1 = view("hd1", O_B, T4, BF16)
        lockstep([s1_gen(), filter_gen_all(hs1, hd1)])
        if maybe_stop("xn", [(flat(xnT), xnT_keys(0, 16))]):
            return nc
        if maybe_stop("filt1", [(flat4(hs1), keys16("hs1")), (flat4(hd1), keys16("hd1")),
                                (normRow[:, 0, :], [("normRow", 0)])]):
            return nc
        z_tok = view("z_tok", O_ZTOK, T4, BF16)
        hy_group("v", 2048, z_tok, "z_tok", O_TAB)
        if maybe_stop("hyv", [(flat4(z_tok), keys16("z_tok"))]):
            return nc

        def quad_views(prefix, offs):
            vs = [view("%s%d" % (prefix, w), offs[w], [128, 8, 512], BF16) for w in range(4)]
            return vs

        class NameQ:
            pass

        K1 = quad_views("K1q", [O_Y + w * 8 * KB for w in range(4)])
        tabsK1 = [view("tabK1_%d" % b, O_YAT + b * 8 * KB, [128, 4, 8, 128], BF16) for b in range(2)]
        TA = [view("pwA%d" % j, O_TMP + j * 2 * KB, [128, 512], F32) for j in range(12)]
        TAn = ["pwA%d" % j for j in range(12)]
        dft_kpass(0, hs1, "hs1", hd1, "hd1", K1, "K1q", tabsK1, ["tabK1_0", "tabK1_1"], TA, TAn,
                  hook=lambda fb, tok: convert_wff1([fb], after=tok))
        PM1 = quad_views("PMa", [O_A, O_A + 8 * KB, O_B, O_B + 8 * KB])
        PM1n = ["PMa%d" % w for w in range(4)]
        tabsZ1 = [view("tabZ1_%d" % b, O_YAT + b * 8 * KB, [128, 4, 8, 128], BF16) for b in range(2)]
        dft_zpass(0, z_tok, "z_tok", K1, "K1q", PM1, PM1n, tabsZ1, ["tabZ1_0", "tabZ1_1"], TA, TAn,
                  hook=lambda fb, tok: (hy_prefetch("x1", 1024, O_ZOT) if fb == 0 else None, convert_wff1([8 + fb], after=tok)))
        if maybe_stop("fwd1", [(PM1[w].rearrange("p a b -> p (a b)"), [(PM1n[w], f) for f in range(8)]) for w in range(4)]):
            return nc
        x1_tok = view("x1_tok", O_TAB, T4, BF16)
        hy_group("x1", 1024, x1_tok, "x1_tok", O_ZTOK, wb_off=O_ZOT, ob_off=O_ZOT + 8 * KB)
        hs2 = view("hs2", O_ZOT, T4, BF16)
        hd2 = view("hd2", O_Y, T4, BF16)

        def load_filt2(r, dst_, nm_):
            for q4 in range(4):
                P.dma("sync", lambda e, r=r, q4=q4, dst_=dst_: e.dma_start(
                    out=dst_.rearrange("p r j c -> p (r j) c")[:, q4 * 4:(q4 + 1) * 4, :],
                    in_=hsd2_d[r, q4 * 4:(q4 + 1) * 4].rearrange("t p c -> p t c")),
                    reads=[("hsd2", r, t) for t in range(q4 * 4, q4 * 4 + 4)],
                    writes=[(nm_, t) for t in range(q4 * 4, q4 * 4 + 4)])

        u_tok = view("u_tok", O_ZTOK, T4, BF16)
        tabsB = [view("tabB%d" % b, O_YAT + b * 4 * KB, [128, 2, 8, 128], BF16) for b in range(2)]

        def inv1_hook(nb):
            if nb == 3:
                load_filt2(0, hs2, "hs2")
            if nb == 5:
                load_filt2(1, hd2, "hd2")

        lockstep([dft_inverse(PM1, PM1n, x1_tok, "x1_tok", u_tok, "u_tok", tabsB, ["tabB0", "tabB1"], hook=inv1_hook)])
        if maybe_stop("inv1", [(flat4(u_tok), keys16("u_tok"))]):
            return nc
        K2 = quad_views("K2q", [O_A, O_A + 8 * KB, O_B, O_B + 8 * KB])
        tabsK2 = [view("tabK2_%d" % b, O_YAT + b * 8 * KB, [128, 4, 8, 128], BF16) for b in range(2)]
        TC = [view("pwC%d" % j, O_TMP + j * 2 * KB, [128, 512], F32) for j in range(12)]
        TCn = ["pwC%d" % j for j in range(12)]
        dft_kpass(1, hs2, "hs2", hd2, "hd2", K2, "K2q", tabsK2, ["tabK2_0", "tabK2_1"], TC, TCn)
        PM2 = quad_views("PMb", [O_Y + w * 8 * KB for w in range(4)])
        PM2n = ["PMb%d" % w for w in range(4)]
        tabsZ2 = [view("tabZ2_%d" % b, O_YAT + b * 8 * KB, [128, 4, 8, 128], BF16) for b in range(2)]
        dft_zpass(1, u_tok, "u_tok", K2, "K2q", PM2, PM2n, tabsZ2, ["tabZ2_0", "tabZ2_1"], TC, TCn,
                  hook=lambda fb, tok: (hy_prefetch("x2", 1536, O_ZOT) if fb == 0 else None))
        x2_tok = view("x2_tok", O_TAB, T4, BF16)
        hy_group("x2", 1536, x2_tok, "x2_tok", O_ZTOK, wb_off=O_ZOT, ob_off=O_ZOT + 8 * KB)
        zo_tok = view("zo_tok", O_ZTOK, T4, BF16)
        zoT = view("zoT", O_ZOT, [128, 4, L], BF16)
        tabsD = [view("tabD%d" % b, O_TMP + b * 4 * KB, [128, 2, 8, 128], BF16) for b in range(2)]

        def zo_post(r, j):
            t = r * 8 + j
            for cb in range(4):
                tr(psT(6, 4)[:, cb, :], zo_tok[:, r, j, cb * 128:(cb + 1) * 128], reads=[("zo_tok", t)],
                   writes=[("ps", 6)], signal=(cb == 3))
            st_ = 256 * j + r
            cp("scalar", zoT[:, :, st_:st_ + 255:2], psT(6, 4), reads=[("ps", 6)],
               writes=[("zoT", 2 * j), ("zoT", 2 * j + 1)])

        inv2_gen = dft_inverse(PM2, PM2n, x2_tok, "x2_tok", zo_tok, "zo_tok", tabsD, ["tabD0", "tabD1"], post=zo_post,
                               banks=(0,))

        y_aT = view("y_aT", O_YAT, [128, 4, L], BF16)
        wbU = view("wbU", O_B, [128, 8, 512], BF16)
        wbV = view("wbV", O_B + 8 * KB, [128, 8, 512], BF16)
        NB_A = 4
        guv = [view("gu%d" % b, O_A + b * 2 * KB, [128, 512], F32) for b in range(NB_A)]
        gvv = [view("gv%d" % b, O_A + 8 * KB + b * 2 * KB, [128, 512], F32) for b in range(2)]
        vnv = [view("vn%d" % b, O_A + 12 * KB + b * 2 * KB, [128, 512], F32) for b in range(2)]
        wsT = view("wsT", O_END + 4 * KB, [128, 8, 128], BF16)
        vnbv = [view("vnb%d" % b, O_TMP + 16 * KB + b * KB, [128, 512], BF16) for b in range(NB_A)]
        yav = [view("ya%d" % b, O_TMP + 20 * KB + b * KB, [128, 512], BF16) for b in range(3)]
        junkA = view("junkA", O_TMP + 23 * KB, [128, 512], BF16)
        ringA = {"i": 0}

        def next_pair_a():
            s_ = 1 + ringA["i"] % 2
            ringA["i"] += 1
            return s_

        def ma_a(i):
            b = i % NB_A
            b2 = i % 2
            s = next_pair_a()
            for k in range(8):
                mm(PS[:, 2 * s, :], xnT[:, k, i * 128:(i + 1) * 128], wbU[:, k, :], k == 0, k == 7,
                   reads=[("xnT", i), "wbU"], writes=[("ps", 2 * s)], signal=False)
                mm(PS[:, 2 * s + 1, :], xnT[:, k, i * 128:(i + 1) * 128], wbV[:, k, :], k == 0, k == 7,
                   reads=[("xnT", i), "wbV"], writes=[("ps", 2 * s + 1)], signal=(k == 7))
            guk, gvk, vnk, vnbk = "gu%d" % b, "gv%d" % b2, "vn%d" % b2, "vnb%d" % b
            act(guv[b], PS[:, 2 * s, :], AF.Gelu_apprx_tanh, reads=[("ps", 2 * s), ("ps", 2 * s + 1)], writes=[guk])
            act(gvv[b2], PS[:, 2 * s + 1, :], AF.Gelu_apprx_tanh, reads=[("ps", 2 * s), ("ps", 2 * s + 1)],
                writes=[gvk, stk(S_MA_SUM + i)], accum=stc(S_MA_SUM + i))
            ts("vector", stc(S_MA_NM + i), stc(S_MA_SUM + i), -1.0 / 512.0, None, ALU.mult, None,
               reads=[stk(S_MA_SUM + i)], writes=[stk(S_MA_NM + i)])
            act(junkA, gvv[b2], AF.Square, reads=[gvk, stk(S_MA_NM + i)], writes=["junkA", stk(S_MA_VAR + i)],
                bias=stc(S_MA_NM + i), accum=stc(S_MA_VAR + i))
            rsqrt_col(stc(S_MA_VAR + i), 1.0 / 512.0, stk(S_MA_VAR + i))
            ts("vector", vnv[b2], gvv[b2], stc(S_MA_NM + i), stc(S_MA_VAR + i), ALU.add, ALU.mult,
               reads=[gvk, stk(S_MA_NM + i), stk(S_MA_VAR + i)], writes=[vnk])
            tt("vector", vnbv[b], vnv[b2], gainRow, ALU.mult, reads=[vnk, "rowP1"], writes=[vnbk])

        def ma_b(i):
            b = i % NB_A
            by = i % 3
            guk, vnbk, yak = "gu%d" % b, "vnb%d" % b, "ya%d" % by
            for g in range(8):
                mm(PS[:, 7, g * 64:(g + 1) * 64], wsT[:, g, :], vnbv[b][:, g * 64:(g + 1) * 64], True, True,
                   reads=["wsT", vnbk], writes=[("ps", 7)], signal=(g == 7))
            for g in range(8):
                stt("vector", yav[by][:, g * 64:(g + 1) * 64], PS[:, 7, g * 64:(g + 1) * 64],
                    colp[:, C_BST + g:C_BST + g + 1], guv[b][:, g * 64:(g + 1) * 64], ALU.add, ALU.mult,
                    reads=[("ps", 7), guk, "colp"], writes=[yak])

        def ma_c(i):
            b = i % 3
            yak = "ya%d" % b
            for cb in range(4):
                tr(psT(1, 4)[:, cb, :], yav[b][:, cb * 128:(cb + 1) * 128], reads=[yak], writes=[("ps", 1)],
                   signal=(cb == 3))
            cp("scalar", y_aT[:, :, i * 128:(i + 1) * 128], psT(1, 4), reads=[("ps", 1)], writes=[("y_aT", i)])

        def mixa_gen():
            P.dma("gpsimd", lambda e: e.dma_start(out=wbU, in_=w_in_v[:, :, 0:512]), writes=["wbU"])
            P.dma("gpsimd", lambda e: e.dma_start(out=wbV, in_=w_in_v[:, :, 512:1024]), writes=["wbV"])
            P.dma("gpsimd", lambda e: e.dma_start(out=wsT, in_=wsT_d), writes=["wsT"])
            for step in range(NT + 3):
                if step < NT:
                    ma_a(step)
                if 0 <= step - 2 < NT:
                    ma_b(step - 2)
                if 0 <= step - 3 < NT:
                    ma_c(step - 3)
                yield

        lockstep([inv2_gen, mixa_gen()])
        if maybe_stop("zo", [(flat(zoT), [("zoT", t) for t in range(16)])]):
            return nc
        if maybe_stop("mixa", [(flat(y_aT), [("y_aT", t) for t in range(16)])]):
            return nc
        mergedT = view("mergedT", O_A, [128, 8, L], BF16)
        woa = view("woa", O_B, [128, 4, D], BF16)
        wob = view("wob", O_B + 8 * KB, [128, 4, D], BF16)
        wgv = [view("wg%d" % b, O_Y + b * 4 * KB, [128, 8, 2, 128], BF16) for b in range(2)]
        sgav = [view("sga%d" % b, O_Y + 8 * KB + b * 2 * KB, [128, 512], F32) for b in range(2)]
        sgbv = [view("sgb%d" % b, O_Y + 12 * KB + b * 2 * KB, [128, 512], F32) for b in range(2)]
        m1v = [view("m1_%d" % b, O_Y + 16 * KB + b * 2 * KB, [128, 512], F32) for b in range(2)]
        m2v = [view("m2_%d" % b, O_Y + 20 * KB + b * 2 * KB, [128, 512], F32) for b in range(2)]
        wo = view("wo", O_TAB, [128, 8, D], BF16)
        rowP3 = view("rowP3", O_ROW, [128, 3072], F32)
        P.dma("sync", lambda e: e.dma_start(out=rowP3, in_=rowp_d[:, 3072:6144].partition_broadcast(128)),
              writes=["rowP3"])
        GpmRow, GpreRow, GpfRow = rowP3[:, 0:1024], rowP3[:, 1024:2048], rowP3[:, 2048:3072]

        def load_wo():
            P.dma("gpsimd", lambda e: e.dma_start(out=wo, in_=w_o.rearrange("(k p) c -> p k c", p=128)), writes=["wo"])

        it = 0
        for j in range(8):
            wb_ = wgv[j % 2]
            wk = "wg%d" % (j % 2)
            for ab in range(2):
                c0 = 2560 + ab * 1024 + j * 128
                P.dma("gpsimd", lambda e, wb_=wb_, ab=ab, c0=c0: e.dma_start(out=wb_[:, :, ab, :],
                                                                             in_=w_in_v[:, :, c0:c0 + 128]),
                      writes=[(wk, ab)])
            if j == 0:
                P.dma("gpsimd", lambda e: e.dma_start(out=woa, in_=w_out_a.rearrange("(k p) c -> p k c", p=128)),
                      writes=["woa"])
                P.dma("gpsimd", lambda e: e.dma_start(out=wob, in_=w_out_b.rearrange("(k p) c -> p k c", p=128)),
                      writes=["wob"])
                load_wo()
            for tg in range(4):
                b0 = 4 * (it % 2)
                bb = it % 2
                it += 1
                t0 = tg * 512
                kya, kyb, kga, kgb = [("ps", b0 + q) for q in range(4)]
                for k in range(8):
                    mm(psb(b0 + 2), wb_[:, k, 0, :], xnT[:, k, t0:t0 + 512], k == 0, k == 7,
                       reads=[(wk, 0)] + xnT_keys(tg * 4, tg * 4 + 4), writes=[kga])
                for k in range(8):
                    mm(psb(b0 + 3), wb_[:, k, 1, :], xnT[:, k, t0:t0 + 512], k == 0, k == 7,
                       reads=[(wk, 1)] + xnT_keys(tg * 4, tg * 4 + 4), writes=[kgb])
                for cb in range(4):
                    mm(psb(b0), woa[:, cb, j * 128:(j + 1) * 128], y_aT[:, cb, t0:t0 + 512], cb == 0, cb == 3,
                       reads=["woa"] + [("y_aT", t) for t in range(tg * 4, tg * 4 + 4)], writes=[kya])
                for cb in range(4):
                    mm(psb(b0 + 1), wob[:, cb, j * 128:(j + 1) * 128], zoT[:, cb, t0:t0 + 512], cb == 0, cb == 3,
                       reads=["wob"] + [("zoT", t) for t in range(tg * 4, tg * 4 + 4)], writes=[kyb])
                act(sgav[bb], psb(b0 + 2), AF.Sigmoid, reads=[kga], writes=["sga%d" % bb])
                act(sgbv[bb], psb(b0 + 3), AF.Sigmoid, reads=[kgb], writes=["sgb%d" % bb])
                tt("vector", m1v[bb], psb(b0), sgav[bb], ALU.mult, reads=[kya, "sga%d" % bb], writes=["m1_%d" % bb])
                tt("vector", m2v[bb], psb(b0 + 1), sgbv[bb], ALU.mult, reads=[kyb, "sgb%d" % bb], writes=["m2_%d" % bb])
                tt("vector", mergedT[:, j, t0:t0 + 512], m1v[bb], m2v[bb], ALU.add,
                   reads=["m1_%d" % bb, "m2_%d" % bb], writes=[("mergedT", j, tg)])

        if maybe_stop("merge", [(flat(mergedT), [("mergedT", j, tg) for j in range(8) for tg in range(4)])]):
            return nc
        wff2 = view("wff2", 0, [128, 32, D], BF16)
        fT = view("fT", O_Y, [128, 32, 512], BF16)
        w1bv = [view("w1b%d" % b, O_B + b * 4 * KB, [128, 8, 256], BF16) for b in range(2)]
        w1bv.append(view("w1b2", O_END + 4 * KB, [128, 8, 256], BF16))
        hnT = view("hnT", O_B + 8 * KB, [128, 8, 512], BF16)
        hv = [view("h%d" % b, O_TMP + b * 4 * KB, [128, D], F32) for b in range(4)]
        tmpv = [view("tmp%d" % b, O_TMP + 16 * KB + b * 4 * KB, [128, D], F32) for b in range(2)]
        hsbv = [view("hsb0", O_END, [128, D], BF16), view("hsb1", O_END + 2 * KB, [128, D], BF16)]
        junk3 = junk3_t[:]
        w_ff2_v = w_ff2.rearrange("(k p) c -> p k c", p=128)
        w_ff1_v = w_ff1.rearrange("(k p) c -> p k c", p=128)

        def load_wff2():
            for q in range(8):
                P.dma("gpsimd", lambda e, q=q: e.dma_start(out=wff2[:, q * 4:(q + 1) * 4, :],
                                                           in_=w_ff2_v[:, q * 4:(q + 1) * 4, :]),
                      writes=[("wff2", q)])

        out_toks = []
        cnt3 = {"ti": 0}

        def mstage(i, q, s):
            h = hv[q]
            hk = "h%d" % q
            P.dma("sync", lambda e, i=i, h=h: e.dma_start(out=h, in_=x[i * 128:(i + 1) * 128, :]), writes=[hk])
            for half in range(2):
                for k in range(8):
                    mm(PS[:, 2 * s + half, :], mergedT[:, k, i * 128:(i + 1) * 128],
                       wo[:, k, half * 512:(half + 1) * 512], k == 0, k == 7,
                       reads=[("mergedT", k, i // 4), "wo"], writes=[("ps", 2 * s + half)])
            mp = pspair(s)
            act(junk3, mp, AF.Square, reads=pk(s), writes=["junk3", stk(S_M + i)], accum=stc(S_M + i))
            rsqrt_col(stc(S_M + i), 1.0 / D, stk(S_M + i))
            tmp = tmpv[cnt3["ti"] % 2]
            tk_ = "tmp%d" % (cnt3["ti"] % 2)
            cnt3["ti"] += 1
            stt("vector", tmp, mp, stc(S_M + i), GpmRow, ALU.mult, ALU.mult,
                reads=pk(s) + [stk(S_M + i), "rowP3"], writes=[tk_])
            tt("vector", h, h, tmp, ALU.add, reads=[hk, tk_], writes=[hk])
            act(junk3, h, AF.Square, reads=[hk], writes=["junk3", stk(S_H + i)], accum=stc(S_H + i))
            rsqrt_col(stc(S_H + i), 1.0 / D, stk(S_H + i))
            stt("vector", hsbv[q % 2], h, stc(S_H + i), GpreRow, ALU.mult, ALU.mult,
                reads=[hk, stk(S_H + i), "rowP3"], writes=["hsb%d" % (q % 2)])

        def mtrans(q):
            for k in range(8):
                tr(psT(6, 8)[:, k, :], hsbv[q % 2][:, k * 128:(k + 1) * 128], reads=["hsb%d" % (q % 2)],
                   writes=[("ps", 6)], signal=(k == 7))
            cp("scalar", hnT[:, :, q * 128:(q + 1) * 128], psT(6, 8), reads=[("ps", 6)], writes=[("hnT", q)])

        def ff1(tb):
            wb1, w1k = None, None
            for fb in range(32):
                if fb % 2 == 0:
                    fp = fb // 2
                    wb1 = w1bv[fp % 3]
                    w1k = "w1b%d" % (fp % 3)
                    P.dma("sync", lambda e, wb1=wb1, fp=fp: e.dma_start(
                        out=wb1.rearrange("p k c -> p (k c)"), in_=wff1s_d[fp]), reads=[("wff1s", fp)], writes=[w1k])
                bank = 4 + (fb % 2)
                for k in range(8):
                    mm(psb(bank), wb1[:, k, (fb % 2) * 128:(fb % 2) * 128 + 128], hnT[:, k, :], k == 0, k == 7,
                       reads=[w1k] + [("hnT", q) for q in range(4)], writes=[("ps", bank)])
                act(psb(bank), psb(bank), AF.Relu, reads=[("ps", bank)], writes=[("ps", bank)])
                act(fT[:, fb, :], psb(bank), AF.Square, reads=[("ps", bank)], writes=[("fT", fb)])

        def ff2(tb, q):
            i = tb * 4 + q
            h = hv[q]
            hk = "h%d" % q
            s = q % 2
            for half in range(2):
                for fb in range(32):
                    mm(PS[:, 2 * s + half, :], fT[:, fb, q * 128:(q + 1) * 128],
                       wff2[:, fb, half * 512:(half + 1) * 512], fb == 0, fb == 31,
                       reads=[("fT", fb), ("wff2", fb // 4)], writes=[("ps", 2 * s + half)])
            fp_ = pspair(s)
            act(junk3, fp_, AF.Square, reads=pk(s), writes=["junk3", stk(S_F + i)], accum=stc(S_F + i))
            rsqrt_col(stc(S_F + i), 1.0 / D, stk(S_F + i))
            tmp = tmpv[cnt3["ti"] % 2]
            tk_ = "tmp%d" % (cnt3["ti"] % 2)
            cnt3["ti"] += 1
            stt("vector", tmp, fp_, stc(S_F + i), GpfRow, ALU.mult, ALU.mult,
                reads=pk(s) + [stk(S_F + i), "rowP3"], writes=[tk_])
            tt("vector", tmp, tmp, h, ALU.add, reads=[tk_, hk], writes=[tk_])
            out_toks.append(P.dma("sync", lambda e, i=i, tmp=tmp: e.dma_start(out=out[i * 128:(i + 1) * 128, :], in_=tmp),
                                  reads=[tk_]))

        for q in range(4):
            mstage(q, q, q % 3)
            if q >= 1:
                mtrans(q - 1)
        mtrans(3)
        load_wff2()
        for tb in range(4):
            ff1(tb)
            for q in range(4):
                ff2(tb, q)
                if tb < 3:
                    mstage((tb + 1) * 4 + q, q, 2)
                    if q >= 1:
                        mtrans(q - 1)
            if tb < 3:
                mtrans(3)
        for t in out_toks:
            P.wait("sync", t)
        P.emit()
        print("program: ops=%d waits=%d counts=%s" % (P.nops, P.nwaits, P.cnt), flush=True)
    return nc


_CONST = {}


def _constants():
    if _CONST:
        return _CONST
    bf = ml_dtypes.bfloat16
    H = L // 2
    m = np.arange(H, dtype=np.float64)
    f = np.arange(H, dtype=np.float64) + 0.5
    tabs = []
    for r in range(2):
        ang = 2.0 * np.pi * np.outer(2 * m + r, f) / (2 * L)
        tabs.append((np.cos(ang), np.sin(ang)))
    def fwd(T):
        return T.reshape(8, 128, 8, 128).transpose(2, 1, 0, 3)
    TF = np.stack([fwd(tabs[0][0]), fwd(tabs[0][1]), fwd(tabs[1][0]), fwd(tabs[1][1])], axis=2)
    def inv(T):
        return T.reshape(8, 128, 8, 128).transpose(0, 3, 2, 1)
    TI = np.stack([np.stack([inv(tabs[r][0]), inv(tabs[r][1])], axis=2) for r in range(2)], axis=0)
    _CONST["TF"] = np.ascontiguousarray(TF).astype(bf)
    _CONST["TI"] = np.ascontiguousarray(TI).astype(bf)
    _CONST["ident"] = np.eye(128, dtype=np.float32).astype(bf)
    t = np.linspace(0.0, 1.0, L, dtype=np.float32)[:, None]
    bands = 16
    w = (2.0 * math.pi * np.arange(L, dtype=np.float32)[:, None] / L).astype(np.float32)
    fr = np.linspace(1e-4, bands - 1, bands, dtype=np.float32)[None, :]
    feats = np.concatenate([t, np.cos(fr * w), -np.sin(fr * w)], axis=-1).astype(np.float32)
    _CONST["featsT"] = np.ascontiguousarray(feats.T)
    max_decay = math.log(1e-2) / 0.3
    min_decay = math.log(1e-2) / 1.5
    _CONST["delta"] = np.abs(np.linspace(min_decay, max_decay, 512, dtype=np.float32)).astype(np.float32)
    _CONST["tneg"] = np.ascontiguousarray((-t[:, 0]).reshape(8, 128, 2).transpose(1, 2, 0).reshape(128, 16))
    return _CONST


_PROG = {}


def kernel(x, g_pre_mix, w_in, a_v_gain, a_w_s, a_b_s, w_out_a, b_conv_w, b_conv_b,
           b_filt_w1, b_filt_b1, b_filt_f1, b_filt_w2, b_filt_b2, b_filt_f2, b_filt_w3,
           b_skip, w_out_b, w_o, g_post_mix, g_pre_ffn, w_ff1, w_ff2, g_post_ffn):
    f32 = lambda a: np.ascontiguousarray(np.asarray(a, dtype=np.float32))
    c = _constants()
    x = f32(x)
    colp = np.zeros((128, NCOLP), np.float32)
    cw = f32(b_conv_w)[0]
    cbias = f32(b_conv_b)[0]
    colp[:, C_W0:C_W0 + 12] = cw[0].reshape(12, 128).T
    colp[:, C_W1:C_W1 + 12] = cw[1].reshape(12, 128).T
    colp[:, C_W2:C_W2 + 12] = cw[2].reshape(12, 128).T
    colp[:, C_CB:C_CB + 12] = cbias.reshape(12, 128).T
    colp[0:64, C_F1] = f32(b_filt_f1)[0]
    colp[0:64, C_B1] = f32(b_filt_b1)[0]
    colp[0:64, C_F2] = f32(b_filt_f2)[0]
    colp[0:64, C_B2] = f32(b_filt_b2)[0]
    colp[:, C_BST:C_BST + 8] = f32(a_b_s)[0].T
    colp[:, C_TNEG:C_TNEG + 16] = c["tneg"]
    rowp = np.zeros((1, NROWP), np.float32)
    rowp[0, R_GPM:R_GPM + 1024] = f32(g_pre_mix)[0]
    rowp[0, R_GAIN:R_GAIN + 512] = f32(a_v_gain)[0]
    rowp[0, R_SKIP:R_SKIP + 1024] = f32(b_skip)[0].reshape(-1)
    rowp[0, R_DELTA:R_DELTA + 512] = c["delta"]
    rowp[0, R_GPOSTMIX:R_GPOSTMIX + 1024] = f32(g_post_mix)[0]
    rowp[0, R_GPREFFN:R_GPREFFN + 1024] = f32(g_pre_ffn)[0]
    rowp[0, R_GPOSTFFN:R_GPOSTFFN + 1024] = f32(g_post_ffn)[0]
    shared = {
        "w_in": f32(w_in)[0], "w_out_a": f32(w_out_a)[0], "w_out_b": f32(w_out_b)[0], "w_o": f32(w_o)[0],
        "w_ff1": f32(w_ff1)[0], "w_ff2": f32(w_ff2)[0],
        "wsT": np.ascontiguousarray(f32(a_w_s)[0].transpose(2, 0, 1)),
        "colp": colp, "rowp": rowp, "featsT": c["featsT"],
        "fw1": f32(b_filt_w1)[0], "fw2": f32(b_filt_w2)[0], "fw3": f32(b_filt_w3)[0],
        "ident": c["ident"], "TF": c["TF"], "TI": c["TI"],
    }
    if "nc" not in _PROG:
        _PROG["nc"] = build_program(STOP_AFTER)
    nc = _PROG["nc"]
    in_maps = []
    for b in range(8):
        m = dict(shared)
        m["x"] = np.ascontiguousarray(x[b])
        in_maps.append(m)
    res = run_bass_kernel_spmd(nc, in_maps, core_ids=list(range(8)))
    outs = [np.asarray(r["out"], dtype=np.float32) for r in res.results]
    return np.stack(outs, axis=0)
```
